# Optimizing a Trainium2 kernel written in Bass

```python
import math
import jax, jax.numpy as jnp
from jax import lax
import numpy as np

D_MODEL = 1024
BATCH = 8
SEQ = 2048
DEPTH = 2
DEC_BATCH = 128
DEC_SEQ = 8
PAST_LEN = 16384
PAGE_SIZE = 128

N_AB = (DEPTH + 1) // 2
N_CD = DEPTH // 2
D_MIX = D_MODEL
EPS = 1e-6
GDN_HEADS = 4
GDN_DK = D_MODEL // 8
GDN_DV = D_MODEL // 8
GDN_CONV = 4
GDN_CHUNK = 64
D_A = GDN_HEADS * GDN_DV
D_QKV = GDN_HEADS * (2 * GDN_DK + GDN_DV)
SGU_CHUNK = 128
SGU_GROUPS = 4
D_B = D_MIX - D_A
SGU_GW = D_B // SGU_GROUPS
D_IN_AB = D_QKV + D_A + 2 * GDN_HEADS + 2 * D_B
AB_SPLITS = [D_QKV, D_QKV + D_A, D_QKV + D_A + GDN_HEADS, D_QKV + D_A + 2 * GDN_HEADS,
             D_QKV + D_A + 2 * GDN_HEADS + D_B]
POOL_WINDOWS = (2, 4, 8, 16)
POOL_GROUPS = len(POOL_WINDOWS)
D_C = D_MIX // 2
POOL_GW = D_C // POOL_GROUPS
POOL_BUF = max(POOL_WINDOWS) - 1
D_D = D_MIX - D_C
S5_GW = 16
S5_GROUPS = D_D // S5_GW
S5_STATE = 64
D_IN_CD = D_C + D_D
D_FF = 4 * D_MODEL
D_PLE = 256

kernel_name = 'hybrid_gdn_sgu_pool_s5_step'


def rmsnorm(x, g):
    xf = x.astype(jnp.float32)
    y = xf * lax.rsqrt(jnp.mean(xf * xf, axis=-1, keepdims=True) + EPS)
    return (y * g.astype(jnp.float32)).astype(x.dtype)


def layernorm(x, g, b):
    xf = x.astype(jnp.float32)
    xc = xf - jnp.mean(xf, axis=-1, keepdims=True)
    var = jnp.mean(xc * xc, axis=-1, keepdims=True)
    return (xc * lax.rsqrt(var + EPS) * g.astype(jnp.float32) + b.astype(jnp.float32)).astype(x.dtype)


def l2norm(x):
    return x * lax.rsqrt(jnp.sum(x * x, axis=-1, keepdims=True) + EPS)


def short_conv(x, buf, w):
    t = x.shape[1]
    xe = jnp.concatenate([buf.astype(x.dtype), x], axis=1)
    y = xe[:, 0:t] * w[0]
    for i in range(1, GDN_CONV):
        y = y + xe[:, i:i + t] * w[i]
    return y, xe[:, t:]


def gated_delta_rule(q, k, v, beta, g, s0):
    bsz, t, nh, dk = q.shape
    dv = v.shape[-1]
    c = min(GDN_CHUNK, t)
    n = -(-t // c)
    pad = n * c - t

    def prep(a):
        a = jnp.pad(a, [(0, 0), (0, pad)] + [(0, 0)] * (a.ndim - 2))
        a = a.reshape((bsz, n, c) + a.shape[2:])
        return jnp.moveaxis(a, 3, 1)

    q, k, v, beta, g = (prep(a) for a in (q, k, v, beta, g))
    gc = jnp.cumsum(g, axis=-1)
    tril = jnp.tril(jnp.ones((c, c), bool))
    strict = jnp.tril(jnp.ones((c, c), bool), -1)
    decay = jnp.exp(jnp.where(tril, gc[..., :, None] - gc[..., None, :], -jnp.inf))
    kk = jnp.einsum('bhnid,bhnjd->bhnij', k, k)
    lmat = jnp.where(strict, beta[..., :, None] * kk * decay, 0.0)
    eye = jnp.eye(c, dtype=jnp.float32)
    tinv = lax.linalg.triangular_solve(eye + lmat, jnp.broadcast_to(eye, lmat.shape),
                                       left_side=True, lower=True, unit_diagonal=True)
    u_base = jnp.einsum('bhnij,bhnje->bhnie', tinv, v * beta[..., None])
    w_dec = jnp.einsum('bhnij,bhnjd->bhnid', tinv, k * (beta * jnp.exp(gc))[..., None])
    qk = jnp.einsum('bhnid,bhnjd->bhnij', q, k) * decay
    g_last = gc[..., -1]
    k_tail = k * jnp.exp(g_last[..., None] - gc)[..., None]
    q_head = q * jnp.exp(gc)[..., None]

    def step(s, xs):
        q_h, qk_i, u_b, w_d, k_t, g_l = xs
        u = u_b - jnp.einsum('bhid,bhde->bhie', w_d, s)
        o = jnp.einsum('bhid,bhde->bhie', q_h, s) + jnp.einsum('bhij,bhje->bhie', qk_i, u)
        s = s * jnp.exp(g_l)[..., None, None] + jnp.einsum('bhid,bhie->bhde', k_t, u)
        return s, o

    xs = tuple(jnp.moveaxis(a, 2, 0) for a in (q_head, qk, u_base, w_dec, k_tail, g_last))
    s_fin, o = lax.scan(step, s0, xs)
    o = jnp.moveaxis(o, 0, 2).reshape(bsz, nh, n * c, dv)[:, :, :t]
    return jnp.transpose(o, (0, 2, 1, 3)), s_fin


def chunk_sgu(zu, zv, ln_g, ln_b, w_sp, b_sp):
    bsz, t, _ = zu.shape
    c = SGU_CHUNK
    u = jax.nn.gelu(zu)
    v = layernorm(jax.nn.gelu(zv), ln_g, ln_b)
    n = -(-t // c)
    pad = n * c - t
    vc = jnp.pad(v, ((0, 0), (0, pad), (0, 0))).reshape(bsz, n, c, SGU_GROUPS, SGU_GW)
    w = jnp.where(jnp.tril(jnp.ones((c, c), bool)), w_sp, 0.0)
    mixed = jnp.einsum('gts,bnsgc->bntgc', w, vc) + b_sp.T[None, None, :, :, None]
    mixed = mixed.reshape(bsz, n * c, D_B)[:, :t]
    start = ((t - 1) // c) * c
    return u * mixed.astype(u.dtype), v[:, start:]


def pool_mixer(x, buf, pos0, w_pool, scale):
    bsz, t, _ = x.shape
    xe = jnp.concatenate([buf.astype(x.dtype), x], axis=1)
    cs = jnp.pad(jnp.cumsum(xe.astype(jnp.float32), axis=1), ((0, 0), (1, 0), (0, 0)))
    pos = pos0 + jnp.arange(t)
    xf = x.astype(jnp.float32)
    e0 = POOL_BUF + 1
    outs = []
    for gi, win in enumerate(POOL_WINDOWS):
        lo, hi = gi * POOL_GW, (gi + 1) * POOL_GW
        s = cs[:, e0:e0 + t, lo:hi] - cs[:, e0 - win:e0 - win + t, lo:hi]
        cnt = jnp.minimum(win, pos + 1).astype(jnp.float32)[None, :, None]
        m = s / cnt - xf[:, :, lo:hi]
        outs.append(m @ w_pool[gi].astype(jnp.float32))
    y = jnp.concatenate(outs, axis=-1) * scale.astype(jnp.float32)
    return y.astype(x.dtype), xe[:, t:]


def s5_mixer(x, st_re, st_im, lam_re, lam_im, log_dt, b_re, b_im, c_re, c_im, d_skip, w_glu, b_glu):
    bsz, t, _ = x.shape
    f32 = jnp.float32
    u = x.astype(f32).reshape(bsz, t, S5_GROUPS, S5_GW)
    lam = lax.complex(lam_re.astype(f32), lam_im.astype(f32))
    dt = jnp.exp(log_dt.astype(f32))[:, None]
    lam_bar = jnp.exp(lam * dt)
    b_bar = ((lam_bar - 1.0) / lam)[..., None] * lax.complex(b_re.astype(f32), b_im.astype(f32))
    bu = jnp.einsum('gnc,btgc->btgn', b_bar, u.astype(jnp.complex64))
    s0 = lax.complex(st_re.astype(f32), st_im.astype(f32))
    bu = bu.at[:, 0].add(lam_bar * s0)
    a = jnp.broadcast_to(lam_bar, bu.shape)

    def comb(e1, e2):
        a1, b1 = e1
        a2, b2 = e2
        return a1 * a2, a2 * b1 + b2

    _, s = lax.associative_scan(comb, (a, bu), axis=1)
    cm = lax.complex(c_re.astype(f32), c_im.astype(f32))
    y = jnp.real(jnp.einsum('gcn,btgn->btgc', cm, s)) + d_skip.astype(f32).reshape(S5_GROUPS, S5_GW) * u
    y = jax.nn.gelu(y.reshape(bsz, t, D_D))
    y = y * jax.nn.sigmoid(y @ w_glu.astype(f32) + b_glu.astype(f32))
    s_last = s[:, -1]
    return y.astype(x.dtype), jnp.real(s_last), jnp.imag(s_last)


def layer_ab(h, conv_buf, s0, norm_mix, w_in, conv_w, a_log, dt_bias, norm_o,
             ln_g, ln_b, w_sp, b_sp, w_out):
    f32 = jnp.float32
    bsz, t, _ = h.shape
    z = rmsnorm(h, norm_mix) @ w_in
    qkv, zg, zb, za, zu, zv = jnp.split(z, AB_SPLITS, axis=-1)
    qkv, new_buf = short_conv(qkv, conv_buf, conv_w)
    qkv = jax.nn.silu(qkv).astype(f32)
    q, k, v = jnp.split(qkv, [GDN_HEADS * GDN_DK, 2 * GDN_HEADS * GDN_DK], axis=-1)
    q = l2norm(q.reshape(bsz, t, GDN_HEADS, GDN_DK)) * (GDN_DK ** -0.5)
    k = l2norm(k.reshape(bsz, t, GDN_HEADS, GDN_DK))
    v = v.reshape(bsz, t, GDN_HEADS, GDN_DV)
    beta = jax.nn.sigmoid(zb.astype(f32))
    g = -jnp.exp(a_log.astype(f32)) * jax.nn.softplus(za.astype(f32) + dt_bias.astype(f32))
    o, s_new = gated_delta_rule(q, k, v, beta, g, s0.astype(f32))
    o = rmsnorm(o, norm_o) * jax.nn.silu(zg.astype(f32).reshape(bsz, t, GDN_HEADS, GDN_DV))
    o_a = o.reshape(bsz, t, D_A).astype(h.dtype)
    o_b, v_rows = chunk_sgu(zu, zv, ln_g, ln_b, w_sp, b_sp)
    y = jnp.concatenate([o_a, o_b.astype(h.dtype)], axis=-1) @ w_out
    return h + y, new_buf, s_new, v_rows


def layer_cd(h, pool_buf, st_re, st_im, pos0, norm_mix, w_in, w_pool, pool_scale,
             lam_re, lam_im, log_dt, b_re, b_im, c_re, c_im, d_skip, w_glu, b_glu, w_out):
    z = rmsnorm(h, norm_mix) @ w_in
    xc, xd = jnp.split(z, [D_C], axis=-1)
    o_c, new_pool = pool_mixer(xc, pool_buf, pos0, w_pool, pool_scale)
    o_d, s_re, s_im = s5_mixer(xd, st_re, st_im, lam_re, lam_im, log_dt, b_re, b_im,
                               c_re, c_im, d_skip, w_glu, b_glu)
    y = jnp.concatenate([o_c, o_d], axis=-1) @ w_out
    return h + y, new_pool, s_re, s_im


def channel_mixer(h, norm_g, w_up, w_down):
    a = jax.nn.relu(rmsnorm(h, norm_g) @ w_up)
    return h + (a * a) @ w_down


def per_layer_embed(h, p, norm_g, w_proj, w_gate):
    gate = jax.nn.sigmoid(rmsnorm(h, norm_g) @ w_gate)
    return h + (p.astype(h.dtype) @ w_proj) * gate


def trunk(h, p, conv0, delta0, pool0, s5re0, s5im0, pos0, wt):
    convs, deltas, vrows, pools, s5res, s5ims = [], [], [], [], [], []
    for i in range(DEPTH):
        j = i // 2
        if i % 2 == 0:
            h, cb, sd, vr = layer_ab(h, conv0[j], delta0[j], wt['norm_mix'][i], wt['w_in_ab'][j],
                                     wt['conv_qkv'][j], wt['a_log'][j], wt['dt_bias'][j], wt['norm_o'][j],
                                     wt['ln_v_gain'][j], wt['ln_v_bias'][j], wt['w_spatial'][j],
                                     wt['b_spatial'][j], wt['w_out_ab'][j])
            convs.append(cb)
            deltas.append(sd)
            vrows.append(vr)
        else:
            h, pb, sre, sim = layer_cd(h, pool0[j], s5re0[j], s5im0[j], pos0, wt['norm_mix'][i],
                                       wt['w_in_cd'][j], wt['w_pool'][j], wt['pool_scale'][j],
                                       wt['lam_re'][j], wt['lam_im'][j], wt['log_dt'][j],
                                       wt['b_re'][j], wt['b_im'][j], wt['c_re'][j], wt['c_im'][j],
                                       wt['d_skip'][j], wt['w_glu'][j], wt['b_glu'][j], wt['w_out_cd'][j])
            pools.append(pb)
            s5res.append(sre)
            s5ims.append(sim)
        h = channel_mixer(h, wt['norm_ffn'][i], wt['w_ffn_up'][i], wt['w_ffn_down'][i])
        h = per_layer_embed(h, p[i], wt['norm_pe'][i], wt['w_pe_proj'][i], wt['w_pe_gate'][i])
    y = rmsnorm(h, wt['norm_final'])
    return (y, jnp.stack(convs), jnp.stack(deltas), jnp.stack(vrows),
            jnp.stack(pools), jnp.stack(s5res), jnp.stack(s5ims))


def setup_inputs(seed: int = 0) -> dict:
    key = jax.random.key(seed)
    ks = iter(jax.random.split(key, 64))
    f32 = jnp.float32

    def nrm(shape, s=1.0):
        return jax.random.normal(next(ks), shape, f32) * s

    def gain(shape):
        return 1.0 + nrm(shape, 0.02)

    def unif(shape, lo, hi):
        return jax.random.uniform(next(ks), shape, f32, minval=lo, maxval=hi)

    dt_a = unif((N_AB, GDN_HEADS), 1e-3, 1e-1)
    inp = {
        'x_prompt': nrm((BATCH, SEQ, D_MODEL)),
        'x_sample': nrm((DEC_BATCH, DEC_SEQ, D_MODEL)),
        'state_conv': nrm((N_AB, DEC_BATCH, GDN_CONV - 1, D_QKV)),
        'state_delta': nrm((N_AB, DEC_BATCH, GDN_HEADS, GDN_DK, GDN_DV), GDN_DK ** -0.5),
        'state_pool': nrm((N_CD, DEC_BATCH, POOL_BUF, D_C)),
        'state_s5_re': nrm((N_CD, DEC_BATCH, S5_GROUPS, S5_STATE), 0.1),
        'state_s5_im': nrm((N_CD, DEC_BATCH, S5_GROUPS, S5_STATE), 0.1),
        'p_prompt': nrm((DEPTH, BATCH, SEQ, D_PLE)),
        'p_sample': nrm((DEPTH, DEC_BATCH, DEC_SEQ, D_PLE)),
        'norm_mix': gain((DEPTH, D_MODEL)),
        'norm_ffn': gain((DEPTH, D_MODEL)),
        'norm_pe': gain((DEPTH, D_MODEL)),
        'norm_final': gain((D_MODEL,)),
        'w_in_ab': nrm((N_AB, D_MODEL, D_IN_AB), D_MODEL ** -0.5),
        'conv_qkv': nrm((N_AB, GDN_CONV, D_QKV), GDN_CONV ** -0.5),
        'a_log': jnp.log(unif((N_AB, GDN_HEADS), 1.0, 16.0)),
        'dt_bias': jnp.log(jnp.expm1(dt_a)),
        'norm_o': gain((N_AB, GDN_DV)),
        'ln_v_gain': gain((N_AB, D_B)),
        'ln_v_bias': nrm((N_AB, D_B), 0.02),
        'w_spatial': nrm((N_AB, SGU_GROUPS, SGU_CHUNK, SGU_CHUNK), SGU_CHUNK ** -0.5),
        'b_spatial': 1.0 + nrm((N_AB, SGU_GROUPS, SGU_CHUNK), 0.1),
        'w_out_ab': nrm((N_AB, D_A + D_B, D_MODEL), (D_A + D_B) ** -0.5),
        'w_in_cd': nrm((N_CD, D_MODEL, D_IN_CD), D_MODEL ** -0.5),
        'w_pool': nrm((N_CD, POOL_GROUPS, POOL_GW, POOL_GW), POOL_GW ** -0.5),
        'pool_scale': gain((N_CD, D_C)),
        'lam_re': -0.5 + nrm((N_CD, S5_GROUPS, S5_STATE), 0.01),
        'lam_im': math.pi * jnp.arange(S5_STATE, dtype=f32) + nrm((N_CD, S5_GROUPS, S5_STATE), 0.01),
        'log_dt': unif((N_CD, S5_GROUPS), math.log(1e-3), math.log(1e-1)),
        'b_re': nrm((N_CD, S5_GROUPS, S5_STATE, S5_GW), S5_GW ** -0.5),
        'b_im': nrm((N_CD, S5_GROUPS, S5_STATE, S5_GW), S5_GW ** -0.5),
        'c_re': nrm((N_CD, S5_GROUPS, S5_GW, S5_STATE), S5_STATE ** -0.5),
        'c_im': nrm((N_CD, S5_GROUPS, S5_GW, S5_STATE), S5_STATE ** -0.5),
        'd_skip': nrm((N_CD, D_D)),
        'w_glu': nrm((N_CD, D_D, D_D), D_D ** -0.5),
        'b_glu': nrm((N_CD, D_D), 0.01),
        'w_out_cd': nrm((N_CD, D_C + D_D, D_MODEL), (D_C + D_D) ** -0.5),
        'w_ffn_up': nrm((DEPTH, D_MODEL, D_FF), D_MODEL ** -0.5),
        'w_ffn_down': nrm((DEPTH, D_FF, D_MODEL), D_FF ** -0.5),
        'w_pe_proj': nrm((DEPTH, D_PLE, D_MODEL), D_PLE ** -0.5),
        'w_pe_gate': nrm((DEPTH, D_MODEL, D_MODEL), D_MODEL ** -0.5),
    }
    return inp


def reference(x_prompt, x_sample, state_conv, state_delta, state_pool, state_s5_re, state_s5_im,
              p_prompt, p_sample, norm_mix, norm_ffn, norm_pe, norm_final, w_in_ab, conv_qkv,
              a_log, dt_bias, norm_o, ln_v_gain, ln_v_bias, w_spatial, b_spatial, w_out_ab,
              w_in_cd, w_pool, pool_scale, lam_re, lam_im, log_dt, b_re, b_im, c_re, c_im,
              d_skip, w_glu, b_glu, w_out_cd, w_ffn_up, w_ffn_down, w_pe_proj, w_pe_gate):
    wt = {'norm_mix': norm_mix, 'norm_ffn': norm_ffn, 'norm_pe': norm_pe, 'norm_final': norm_final,
          'w_in_ab': w_in_ab, 'conv_qkv': conv_qkv, 'a_log': a_log, 'dt_bias': dt_bias,
          'norm_o': norm_o, 'ln_v_gain': ln_v_gain, 'ln_v_bias': ln_v_bias, 'w_spatial': w_spatial,
          'b_spatial': b_spatial, 'w_out_ab': w_out_ab, 'w_in_cd': w_in_cd, 'w_pool': w_pool,
          'pool_scale': pool_scale, 'lam_re': lam_re, 'lam_im': lam_im, 'log_dt': log_dt,
          'b_re': b_re, 'b_im': b_im, 'c_re': c_re, 'c_im': c_im, 'd_skip': d_skip,
          'w_glu': w_glu, 'b_glu': b_glu, 'w_out_cd': w_out_cd, 'w_ffn_up': w_ffn_up,
          'w_ffn_down': w_ffn_down, 'w_pe_proj': w_pe_proj, 'w_pe_gate': w_pe_gate}
    bp = x_prompt.shape[0]
    z_conv = jnp.zeros((N_AB, bp) + state_conv.shape[2:], x_prompt.dtype)
    z_delta = jnp.zeros((N_AB, bp) + state_delta.shape[2:], jnp.float32)
    z_pool = jnp.zeros((N_CD, bp) + state_pool.shape[2:], x_prompt.dtype)
    z_s5 = jnp.zeros((N_CD, bp) + state_s5_re.shape[2:], jnp.float32)
    y_prompt, conv_p, delta_p, sgu_v_p, pool_p, s5re_p, s5im_p = trunk(
        x_prompt, p_prompt, z_conv, z_delta, z_pool, z_s5, z_s5, 0, wt)
    y_sample, conv_s, delta_s, sgu_v_s, pool_s, s5re_s, s5im_s = trunk(
        x_sample, p_sample, state_conv, state_delta, state_pool, state_s5_re, state_s5_im, PAST_LEN, wt)
    return (y_prompt, y_sample, conv_p, delta_p, sgu_v_p, pool_p, s5re_p, s5im_p,
            conv_s, delta_s, sgu_v_s, pool_s, s5re_s, s5im_s)
```

```python
import contextlib
import numpy as np
import concourse.bass as bass
import concourse.mybir as mybir
from concourse.bass_utils import run_bass_kernel_spmd

F32 = mybir.dt.float32
BF16 = mybir.dt.bfloat16
I32 = mybir.dt.int32
ALU = mybir.AluOpType
AF = mybir.ActivationFunctionType

NCORES = 8
D = 1024
NT = 17
TT = 2176
EPS = 1e-6
DEBUG = {}


class Buf:
    def __init__(self, t, name="", excl=False):
        self.t = t
        self.name = name
        self.w = None
        self.r = {}
        self.excl = excl

    def __getitem__(self, k):
        return self.t[k]


class Ctx:
    NDMA = 40

    def __init__(self, nc, es):
        self.nc = nc
        self.es = es
        self.eng = {"pe": nc.tensor, "act": nc.scalar, "dve": nc.vector, "pool": nc.gpsimd, "sp": nc.sync}
        self.sems = {}
        self.cnt = {}
        for e in self.eng:
            self.sems[e] = es.enter_context(nc.semaphore("s_" + e))
            self.cnt[e] = 0
        self.dsem = [es.enter_context(nc.semaphore("d%d" % i)) for i in range(self.NDMA)]
        for i, s in enumerate(self.dsem):
            self.sems[("d", i)] = s
        self.dval = [0] * self.NDMA
        self.drr = 0
        self.drr_sw = 0
        self.seen = {e: {} for e in self.eng}
        self.pend = []
        self.out_events = []
        self.uid = 0
        self.dead = False

    def sb(self, name, shape, dtype=F32, es=None):
        self.uid += 1
        t = (es or self.es).enter_context(self.nc.sbuf_tensor("%s_%d" % (name, self.uid), list(shape), dtype))
        return Buf(t, name)

    def _wait(self, e, key, val):
        if val <= 0 or self.seen[e].get(key, 0) >= val:
            return
        self.eng[e].wait_ge(self.sems[key], val)
        self.seen[e][key] = val

    def _deps(self, e, reads, writes):
        for b in reads:
            if b.excl:
                continue
            if b.w is not None:
                self._wait(e, *b.w)
        for b in list(writes) + [b for b in reads if b.excl]:
            if b.w is not None and not (b.excl and e == "pe" and b.w[0] == "pe"):
                self._wait(e, *b.w)
            for k, v in b.r.items():
                self._wait(e, k, v)

    def _commit(self, ev, reads, writes):
        k, v = ev
        for b in reads:
            if b.excl:
                b.w = ev
                b.r = {}
            elif b.r.get(k, 0) < v:
                b.r[k] = v
        for b in writes:
            b.w = ev
            b.r = {}

    def op(self, e, fn, reads=(), writes=()):
        if self.dead:
            return
        self._deps(e, reads, writes)
        ins = fn(self.eng[e])
        self.cnt[e] += 1
        ins.then_inc(self.sems[e], 1)
        self._commit((e, self.cnt[e]), reads, writes)

    def mm(self, fn, reads=(), writes=(), inc=True):
        if self.dead:
            return
        e = "pe"
        self._deps(e, reads, writes)
        ins = fn(self.eng[e])
        self.pend.append((tuple(reads), tuple(writes)))
        if inc:
            self.cnt[e] += 1
            ins.then_inc(self.sems[e], 1)
            ev = (e, self.cnt[e])
            for r, w in self.pend:
                self._commit(ev, r, w)
            self.pend = []

    def dma(self, q, out_ap, in_ap, reads=(), writes=(), is_output=False, **kw):
        if self.dead:
            return
        half = self.NDMA // 2
        if q == "pool":
            i = half + self.drr_sw
            self.drr_sw = (self.drr_sw + 1) % (self.NDMA - half)
        else:
            i = self.drr
            self.drr = (self.drr + 1) % half
        key = ("d", i)
        self._wait(q, key, self.dval[i])
        self._deps(q, reads, writes)
        ins = self.eng[q].dma_start(out=out_ap, in_=in_ap, **kw)
        self.dval[i] += 16
        ins.then_inc(self.dsem[i], 16)
        ev = (key, self.dval[i])
        self._commit(ev, reads, writes)
        if is_output:
            self.out_events.append(ev)

    def barrier(self):
        if self.dead:
            return
        assert not self.pend
        for e in self.eng:
            for e2 in ("pe", "act", "dve", "pool"):
                self._wait(e, e2, self.cnt[e2])
            for i in range(self.NDMA):
                self._wait(e, ("d", i), self.dval[i])

    def finish(self):
        for k, v in self.out_events:
            self._wait("sp", k, v)
        for e in ("pe", "act", "dve", "pool"):
            self._wait("sp", e, self.cnt[e])


class _Stop(Exception):
    pass


def build_nc(dbg=None, stop=None):
    dbg = dbg or {}

    cref = []

    def stage(n):
        if stop == n:
            cref[0].dead = True
    nc = bass.Bass("TRN2", target_bir_lowering=False)

    def din(name, shape):
        return nc.dram_tensor(name, list(shape), F32, kind="ExternalInput").ap()

    def dout(name, shape):
        return nc.dram_tensor(name, list(shape), F32, kind="ExternalOutput").ap()

    xp = din("xp", [2048, D]); xs = din("xs", [128, D])
    pp = din("pp", [2, 2048, 256]); psm = din("psm", [2, 128, 256])
    st_conv = din("st_conv", [48, 1536]); st_delta = din("st_delta", [16, 4, 128, 128])
    st_pool = din("st_pool", [16, 15, 512]); st_s5re = din("st_s5re", [16, 2048]); st_s5im = din("st_s5im", [16, 2048])
    norm_mix = din("norm_mix", [2, D]); norm_ffn = din("norm_ffn", [2, D]); norm_pe = din("norm_pe", [2, D])
    norm_final = din("norm_final", [D])
    w_in_ab = din("w_in_ab", [D, 3080]); conv_qkv = din("conv_qkv", [4, 1536])
    a_log = din("a_log", [4]); dt_bias = din("dt_bias", [4]); norm_o = din("norm_o", [128])
    ln_g = din("ln_g", [512]); ln_b = din("ln_b", [512]); w_sp = din("w_sp", [4, 128, 128]); b_sp = din("b_sp", [4, 128])
    w_out_ab = din("w_out_ab", [D, D]); w_in_cd = din("w_in_cd", [D, D]); w_pool = din("w_pool", [4, 128, 128])
    pool_scale = din("pool_scale", [512])
    lam_re = din("lam_re", [2048]); lam_im = din("lam_im", [2048]); log_dt = din("log_dt", [32])
    b_re = din("b_re", [32, 64, 16]); b_im = din("b_im", [32, 64, 16]); c_re = din("c_re", [32, 16, 64]); c_im = din("c_im", [32, 16, 64])
    d_skip = din("d_skip", [512]); w_glu = din("w_glu", [512, 512]); b_glu = din("b_glu", [512]); w_out_cd = din("w_out_cd", [D, D])
    w_up = din("w_up", [2, D, 4096]); w_down = din("w_down", [2, 4096, D])
    w_pe_proj = din("w_pe_proj", [2, 256, D]); w_pe_gate = din("w_pe_gate", [2, D, D])
    c_ident = din("c_ident", [128, 128]); c_masks = din("c_masks", [10, 128, 128]); c_selcol = din("c_selcol", [128, 16]); c_misc = din("c_misc", [128, 32])

    y_p = dout("y_p", [2048, D]); y_s = dout("y_s", [128, D])
    o_conv_p = dout("o_conv_p", [3, 1536]); o_delta_p = dout("o_delta_p", [4, 128, 128]); o_sguv_p = dout("o_sguv_p", [128, 512])
    o_pool_p = dout("o_pool_p", [15, 512]); o_s5re_p = dout("o_s5re_p", [2048]); o_s5im_p = dout("o_s5im_p", [2048])
    o_conv_s = dout("o_conv_s", [48, 1536]); o_delta_s = dout("o_delta_s", [16, 4, 128, 128]); o_sguv_s = dout("o_sguv_s", [128, 512])
    o_pool_s = dout("o_pool_s", [16, 15, 512]); o_s5re_s = dout("o_s5re_s", [16, 2048]); o_s5im_s = dout("o_s5im_s", [16, 2048])
    dbg_out = {k: nc.dram_tensor("dbg_" + k, list(shp[0]), BF16 if shp[1] == "bf16" else F32, kind="ExternalOutput").ap() for k, shp in dbg.items()}

    with contextlib.ExitStack() as es:
      c = Ctx(nc, es)
      cref.append(c)
      try:
            V = lambda fn, r, w: c.op("dve", fn, r, w)
            A = lambda fn, r, w: c.op("act", fn, r, w)
            G = lambda fn, r, w: c.op("pool", fn, r, w)

            h = c.sb("h", [128, NT, D])
            xnT = c.sb("xnT", [128, 8, TT], BF16)
            ident = c.sb("ident", [128, 128]); identb = c.sb("identb", [128, 128], BF16)
            masks = c.sb("masks", [128, 10, 128])
            selcol = c.sb("selcol", [128, 16]); misc = c.sb("misc", [128, 32])
            onesb = c.sb("onesb", [128, 128], BF16); onesf = c.sb("onesf", [128, 128]); zerof = c.sb("zerof", [128, 128])
            gains = c.sb("gains", [128, 7, 8])
            junk = c.sb("junk", [128, D], BF16)
            ss = c.sb("ss", [128, NT]); rstd = c.sb("rstd", [128, NT])
            xsb = [c.sb("xsb%d" % i, [128, D], BF16) for i in range(2)]
            psum_t = es.enter_context(nc.psum_tensor("psum", [128, 8, 512], F32))
            PB = [Buf(psum_t[:, i, :], "pb%d" % i, excl=True) for i in range(8)]
            PBb = [psum_t.bitcast(BF16)[:, i, :] for i in range(8)]
            pbi = [0]

            def nextpb():
                i = pbi[0]
                pbi[0] = (i + 1) % 8
                return i

            M_TRIU, M_TRIU_S, M_STRICTL, M_STRICTL_S, M_INCLU, M_INCLU_S, M_TAIL, M_TAIL_S, M_SPS = range(9)

            c.dma("sp", h[:, 0:16, :], xp.rearrange("(n p) d -> p n d", p=128), writes=[h])
            c.dma("sp", h[:, 16, :], xs, writes=[h])
            c.dma("sp", ident[:], c_ident, writes=[ident])
            c.dma("sp", masks[:], c_masks.rearrange("m p f -> p m f"), writes=[masks])
            c.dma("sp", selcol[:], c_selcol, writes=[selcol])
            c.dma("sp", misc[:], c_misc, writes=[misc])
            gsrc = [norm_mix[0], norm_ffn[0], norm_pe[0], norm_mix[1], norm_ffn[1], norm_pe[1], norm_final]
            for i, g in enumerate(gsrc):
                c.dma("sp", gains[:, i, :], g.rearrange("(k p) -> p k", p=128), writes=[gains], allow_slow_non_contiguous=True)
            V(lambda e: e.tensor_copy(identb[:], ident[:]), [ident], [identb])
            G(lambda e: e.memset(onesb[:], 1.0), [], [onesb])
            G(lambda e: e.memset(onesf[:], 1.0), [], [onesf])
            G(lambda e: e.memset(zerof[:], 0.0), [], [zerof])
            stage(-1)

            def tap(name, buf, ap):
                if name in dbg_out:
                    c.dma("sp", dbg_out[name], ap, reads=[buf], is_output=True, allow_slow_non_contiguous=True)

            def load_w(dst_buf, dst_ap, src_ap):
                c.dma("pool", dst_ap, src_ap, writes=[dst_buf])

            def wview(w2d, c0, ncols, k0=0, nk=8):
                return w2d[k0 * 128:(k0 + nk) * 128, c0:c0 + ncols].rearrange("(k p) n -> p k n", p=128)

            def rmsnorm_T(gi):
                for t in range(NT):
                    A(lambda e, t=t: e.activation(out=junk[:], in_=h[:, t, :], func=AF.Square, accum_out=ss[:, t:t + 1]), [h], [junk, ss])
                V(lambda e: e.tensor_scalar(rstd[:], ss[:], 1.0 / D, EPS, ALU.mult, ALU.add), [ss], [rstd])
                A(lambda e: e.activation(out=rstd[:], in_=rstd[:], func=AF.Sqrt), [rstd], [rstd])
                V(lambda e: e.reciprocal(rstd[:], rstd[:]), [rstd], [rstd])
                for t in range(NT):
                    xb = xsb[t % 2]
                    A(lambda e, t=t, xb=xb: e.activation(out=xb[:], in_=h[:, t, :], func=AF.Copy, scale=rstd[:, t:t + 1]), [h, rstd], [xb])
                    pi = nextpb()
                    for k in range(8):
                        c.mm(lambda e, k=k, xb=xb, pi=pi: e.transpose(PBb[pi][:, k * 128:(k + 1) * 128], xb[:, k * 128:(k + 1) * 128], identb[:]),
                             [xb, identb], [PB[pi]], inc=(k == 7))
                    V(lambda e, t=t, pi=pi: e.tensor_tensor(
                        xnT[:, :, t * 128:(t + 1) * 128], PBb[pi].rearrange("p (k f) -> p k f", k=8),
                        gains[:, gi, :].unsqueeze(2).to_broadcast([128, 8, 128]), ALU.mult), [PB[pi], gains], [xnT])

            TB = [(0, 512), (512, 512), (1024, 512), (1536, 512), (2048, 128)]

            def proj_fm(pi, W, wcol, c0, n):
                for k in range(8):
                    c.mm(lambda e, k=k: e.matmul(PB[pi][:, 0:n], lhsT=W[:, k, wcol:wcol + 128], rhs=xnT[:, k, c0:c0 + n],
                                                 start=(k == 0), stop=(k == 7)), [W, xnT], [PB[pi]], inc=(k == 7))

            def proj_tm(pi, W, wcol, ncols, t, src=None, nk=8):
                src = src or xnT
                for k in range(nk):
                    c.mm(lambda e, k=k: e.matmul(PB[pi][:, 0:ncols], lhsT=src[:, k, t * 128:(t + 1) * 128], rhs=W[:, k, wcol:wcol + ncols],
                                                 start=(k == 0), stop=(k == nk - 1)), [W, src], [PB[pi]], inc=(k == nk - 1))

            rmsnorm_T(0)
            tap("xnT", xnT, xnT[:, 0, :])
            stage(1)

            with contextlib.ExitStack() as L0:
                omT = c.sb("omT", [128, 4, TT], BF16, L0)
                with contextlib.ExitStack() as GD:
                    sb = lambda name, shape, dt=F32: c.sb(name, shape, dt, GD)
                    Wh = [sb("Wh0", [128, 8, 512], BF16)] * 2
                    Wba = sb("Wba", [128, 8, 8], BF16)
                    cw = sb("cw", [128, 12, 4])
                    cst = sb("cst", [128, 12, 48])
                    cnew_s = sb("cnew_s", [128, 12, 48]); cnew_p = sb("cnew_p", [128, 12, 3])
                    alb = sb("alb", [128, 4]); dtb = sb("dtb", [128, 4]); nob = sb("nob", [128, 1])
                    bg = sb("bg", [128, NT, 8])
                    beta = sb("beta", [128, NT, 4]); nbeta = sb("nbeta", [128, NT, 4]); gg = sb("gg", [128, NT, 4])
                    gc = sb("gc", [128, NT, 4]); egc = sb("egc", [128, NT, 4]); bexp = sb("bexp", [128, NT, 4]); etail = sb("etail", [128, NT, 4])
                    HD = GD.enter_context(contextlib.ExitStack())
                    sb = lambda name, shape, dt=F32: c.sb(name, shape, dt, HD)
                    qnT = sb("qnT", [128, TT], BF16); knT = sb("knT", [128, TT], BF16); gT = sb("gT", [128, TT], BF16)
                    ktok = sb("ktok", [128, NT, 128], BF16); vtok = sb("vtok", [128, NT, 128], BF16)
                    S = sb("S", [128, 128]); Sb = sb("Sb", [128, 128], BF16)
                    S0s = sb("S0s", [128, 16, 128]); Sbs = sb("Sbs", [128, 16, 128], BF16); Sout = S0s

                    c.dma("sp", alb[:], a_log.partition_broadcast(128), writes=[alb])
                    c.dma("sp", dtb[:], dt_bias.partition_broadcast(128), writes=[dtb])
                    c.dma("sp", nob[:], norm_o.rearrange("(p o) -> p o", o=1), writes=[nob])
                    load_w(Wba, Wba[:], wview(w_in_ab, 2048, 8))
                    with contextlib.ExitStack() as TMP:
                        cw4 = c.sb("cw4", [4, 1536], F32, TMP)
                        stc = c.sb("stc", [48, 1536], F32, TMP)
                        c.dma("sp", cw4[:], conv_qkv, writes=[cw4])
                        c.dma("sp", stc[:], st_conv, writes=[stc])
                        for ch in range(12):
                            pi = nextpb()
                            c.mm(lambda e, ch=ch, pi=pi: e.transpose(PB[pi][:, 0:4], cw4[0:4, ch * 128:(ch + 1) * 128], ident[0:4, 0:4]), [cw4, ident], [PB[pi]])
                            c.mm(lambda e, ch=ch, pi=pi: e.transpose(PB[pi][:, 64:112], stc[0:48, ch * 128:(ch + 1) * 128], ident[0:48, 0:48]), [stc, ident], [PB[pi]])
                            V(lambda e, ch=ch, pi=pi: e.tensor_copy(cw[:, ch, :], PB[pi][:, 0:4]), [PB[pi]], [cw])
                            V(lambda e, ch=ch, pi=pi: e.tensor_copy(cst[:, ch, :], PB[pi][:, 64:112]), [PB[pi]], [cst])
                        c.barrier()

                    pi = nextpb()
                    for t in range(NT):
                        for k in range(8):
                            c.mm(lambda e, k=k, t=t: e.matmul(PB[pi][:, t * 8:(t + 1) * 8], lhsT=xnT[:, k, t * 128:(t + 1) * 128], rhs=Wba[:, k, :],
                                                              start=(k == 0), stop=(k == 7)), [Wba, xnT], [PB[pi]], inc=(k == 7))
                    V(lambda e: e.tensor_copy(bg[:], PB[pi][:, 0:NT * 8].rearrange("p (t j) -> p t j", j=8)), [PB[pi]], [bg])
                    A(lambda e: e.activation(out=beta[:], in_=bg[:, :, 0:4], func=AF.Sigmoid), [bg], [beta])
                    V(lambda e: e.tensor_scalar(nbeta[:], beta[:], -1.0, None, ALU.mult), [beta], [nbeta])
                    V(lambda e: e.tensor_tensor(gg[:], bg[:, :, 4:8], dtb[:].unsqueeze(1).to_broadcast([128, NT, 4]), ALU.add), [bg, dtb], [gg])
                    A(lambda e: e.activation(out=gg[:], in_=gg[:], func=AF.Exp), [gg], [gg])
                    A(lambda e: e.activation(out=gg[:], in_=gg[:], func=AF.Ln, bias=1.0), [gg], [gg])
                    A(lambda e: e.activation(out=alb[:], in_=alb[:], func=AF.Exp), [alb], [alb])
                    V(lambda e: e.scalar_tensor_tensor(gg[:], gg[:], -1.0, alb[:].unsqueeze(1).to_broadcast([128, NT, 4]), ALU.mult, ALU.mult), [gg, alb], [gg])
                    pi = nextpb(); pj = nextpb()
                    for t in range(NT):
                        mtri = M_TRIU if t < 16 else M_TRIU_S
                        mtail = M_TAIL if t < 16 else M_TAIL_S
                        c.mm(lambda e, t=t, mtri=mtri: e.matmul(PB[pi][:, t * 4:(t + 1) * 4], lhsT=masks[:, mtri, :], rhs=gg[:, t, :], start=True, stop=True), [masks, gg], [PB[pi]])
                        c.mm(lambda e, t=t, mtail=mtail: e.matmul(PB[pj][:, t * 4:(t + 1) * 4], lhsT=masks[:, mtail, :], rhs=gg[:, t, :], start=True, stop=True), [masks, gg], [PB[pj]])
                    V(lambda e: e.tensor_copy(gc[:], PB[pi][:, 0:NT * 4].rearrange("p (t j) -> p t j", j=4)), [PB[pi]], [gc])
                    A(lambda e: e.activation(out=egc[:], in_=gc[:], func=AF.Exp), [gc], [egc])
                    A(lambda e: e.activation(out=etail[:], in_=PB[pj][:, 0:NT * 4].rearrange("p (t j) -> p t j", j=4), func=AF.Exp), [PB[pj]], [etail])
                    V(lambda e: e.tensor_tensor(bexp[:], beta[:], egc[:], ALU.mult), [beta, egc], [bexp])
                    tap("gc", gc, gc[:, :, 0])
                    tap("beta", beta, beta[:, :, 0])
                    stage(2)


                    def gdn_tile(hd, t, T_, sid):
                        b0, b1 = 2 * sid, 2 * sid + 1
                        B0, B1 = PB[b0], PB[b1]
                        is_s = (t == 16)
                        ts = slice(t * 128, (t + 1) * 128)
                        mtri = M_TRIU_S if is_s else M_TRIU
                        mstr = M_STRICTL_S if is_s else M_STRICTL
                        minc = M_INCLU_S if is_s else M_INCLU
                        gcol = gc[:, t, hd:hd + 1]
                        XX, PTb = T_["XX"], T_["PT"]
                        gTri, nd, nd2, eA2 = T_["gTri"], T_["nd"], T_["nd2"], T_["eA2"]
                        decm, decTm = nd, nd2
                        TTb, Vb, Kb, ktl, qh, wdT, qkTm, ub, u = T_["TTb"], T_["Vb"], T_["Kb"], T_["ktl"], T_["qh"], T_["wdT"], T_["qkTm"], T_["ub"], T_["u"]
                        osq, rr, on = T_["osq"], gTri, ub

                        def out_norm(src_ap, src_buf):
                            A(lambda e: e.activation(out=osq[:], in_=src_ap, func=AF.Square), [src_buf], [osq])
                            c.mm(lambda e: e.matmul(B0[:, 256:384], lhsT=onesb[:], rhs=osq[:], start=True, stop=True), [onesb, osq], [B0])
                            A(lambda e: e.activation(out=rr[:], in_=B0[:, 256:384], func=AF.Ln, scale=1.0 / 128, bias=EPS), [B0], [rr])
                            A(lambda e: e.activation(out=rr[:], in_=rr[:], func=AF.Exp, scale=-0.5), [rr], [rr])
                            V(lambda e: e.tensor_tensor(on[:], src_ap, rr[:], ALU.mult), [src_buf, rr], [on])
                            V(lambda e: e.scalar_tensor_tensor(omT[:, hd, ts], on[:], nob[:, 0:1], gT[:, ts], ALU.mult, ALU.mult), [on, nob, gT], [omT])

                        V(lambda e: e.tensor_scalar(gTri[:], masks[:, mtri, :], gg[:, t, hd:hd + 1], None, ALU.mult), [masks, gg], [gTri])
                        c.mm(lambda e: e.matmul(B0[:, 0:128], lhsT=onesf[:], rhs=gTri[:], start=True, stop=True), [onesf, gTri], [B0])
                        c.mm(lambda e: e.matmul(B0[:, 128:256], lhsT=knT[:, ts], rhs=knT[:, ts], start=True, stop=True), [knT], [B0])
                        c.mm(lambda e: e.matmul(B0[:, 256:384], lhsT=knT[:, ts], rhs=qnT[:, ts], start=True, stop=True), [knT, qnT], [B0])
                        yield
                        V(lambda e: e.scalar_tensor_tensor(nd[:], B0[:, 0:128], gcol, zerof[:], ALU.subtract, ALU.max), [B0, gc, zerof], [nd])
                        V(lambda e: e.scalar_tensor_tensor(nd2[:], B0[:, 0:128], gcol, zerof[:], ALU.subtract, ALU.min), [B0, gc, zerof], [nd2])
                        A(lambda e: e.activation(out=eA2[:], in_=B0[:, 0:128], func=AF.Exp), [B0], [eA2])
                        A(lambda e: e.activation(out=nd[:], in_=nd[:], func=AF.Exp, scale=-1.0), [nd], [nd])
                        A(lambda e: e.activation(out=nd2[:], in_=nd2[:], func=AF.Exp), [nd2], [nd2])
                        yield
                        V(lambda e: e.tensor_tensor(decm[:], nd[:], masks[:, mstr, :], ALU.mult), [nd, masks], [decm])
                        V(lambda e: e.scalar_tensor_tensor(XX[0][:, 0:128], B0[:, 128:256], nbeta[:, t, hd:hd + 1], decm[:], ALU.mult, ALU.mult), [B0, nbeta, decm], [XX[0]])
                        V(lambda e: e.tensor_tensor(decTm[:], nd2[:], masks[:, minc, :], ALU.mult), [nd2, masks], [decTm])
                        V(lambda e: e.tensor_tensor(qkTm[:], B0[:, 256:384], decTm[:], ALU.mult), [B0, decTm], [qkTm])
                        c.mm(lambda e: e.transpose(B1[:, 0:128], XX[0][:, 0:128], ident[:]), [XX[0], ident], [B1])
                        yield
                        A(lambda e: e.copy(XX[0][:, 128:256], B1[:, 0:128]), [B1], [XX[0]])
                        V(lambda e: e.tensor_tensor(PTb[0][:], B1[:, 0:128], ident[:], ALU.add), [B1, ident], [PTb[0]])
                        V(lambda e: e.tensor_scalar(Vb[:], vtok[:, t, :], beta[:, t, hd:hd + 1], None, ALU.mult), [vtok, beta], [Vb])
                        V(lambda e: e.tensor_scalar(Kb[:], ktok[:, t, :], bexp[:, t, hd:hd + 1], None, ALU.mult), [ktok, bexp], [Kb])
                        V(lambda e: e.tensor_scalar(ktl[:], ktok[:, t, :], etail[:, t, hd:hd + 1], None, ALU.mult), [ktok, etail], [ktl])
                        V(lambda e: e.tensor_tensor(qh[:], qnT[:, ts], eA2[:], ALU.mult), [qnT, eA2], [qh])
                        nlev = 2 if is_s else 6
                        for k in range(nlev):
                            a, b = k % 2, (k + 1) % 2
                            c.mm(lambda e: e.matmul(B0[:, 0:128], lhsT=XX[a][:, 128:256], rhs=XX[a][:, 0:128], start=True, stop=True), [XX[a]], [B0])
                            if k < nlev - 1:
                                c.mm(lambda e: e.matmul(B0[:, 128:256], lhsT=XX[a][:, 0:128], rhs=XX[a][:, 128:256], start=True, stop=True), [XX[a]], [B0])
                            yield
                            ncp = 256 if k < nlev - 1 else 128
                            A(lambda e: e.copy(XX[b][:, 0:ncp], B0[:, 0:ncp]), [B0], [XX[b]])
                            c.mm(lambda e: e.matmul(B1[:, 0:128], lhsT=XX[b][:, 0:128], rhs=PTb[a][:], start=True, stop=True), [XX[b], PTb[a]], [B1])
                            yield
                            dstP = PTb[b] if k < nlev - 1 else TTb
                            V(lambda e: e.tensor_tensor(dstP[:], PTb[a][:], B1[:, 0:128], ALU.add), [PTb[a], B1], [dstP])
                        c.mm(lambda e: e.matmul(B0[:, 0:128], lhsT=Kb[:], rhs=TTb[:], start=True, stop=True), [Kb, TTb], [B0])
                        if not is_s:
                            c.mm(lambda e: e.matmul(B0[:, 128:256], lhsT=TTb[:], rhs=Vb[:], start=True, stop=True), [TTb, Vb], [B0])
                        else:
                            c.mm(lambda e: e.matmul(B0[:, 128:256], lhsT=Vb[:], rhs=TTb[:], start=True, stop=True), [TTb, Vb], [B0])
                        yield
                        A(lambda e: e.copy(wdT[:], B0[:, 0:128]), [B0], [wdT])
                        A(lambda e: e.copy(ub[:], B0[:, 128:256]), [B0], [ub])
                        yield
                        if not is_s:
                            c.mm(lambda e: e.matmul(B1[:, 0:128], lhsT=wdT[:], rhs=Sb[:], start=True, stop=True), [wdT, Sb], [B1])
                            V(lambda e: e.tensor_tensor(u[:], ub[:], B1[:, 0:128], ALU.subtract), [ub, B1], [u])
                            c.mm(lambda e: e.matmul(B1[:, 128:256], lhsT=Sb[:], rhs=qh[:], start=True, stop=False), [Sb, qh], [B1], inc=False)
                            c.mm(lambda e: e.matmul(B1[:, 128:256], lhsT=u[:], rhs=qkTm[:], start=False, stop=True), [u, qkTm], [B1])
                            c.mm(lambda e: e.matmul(B0[:, 0:128], lhsT=ktl[:], rhs=u[:], start=True, stop=True), [ktl, u], [B0])
                            V(lambda e: e.scalar_tensor_tensor(S[:], S[:], eA2[:, 127:128], B0[:, 0:128], ALU.mult, ALU.add), [S, eA2, B0], [S])
                            A(lambda e: e.copy(Sb[:], S[:]), [S], [Sb])
                            yield
                            out_norm(B1[:, 128:256], B1)
                        else:
                            uT, osum, ktm = T_["uT"], T_["osum"], T_["ktm"]
                            for s_ in range(16):
                                c.mm(lambda e, s_=s_: e.matmul(B1[:, s_ * 8:s_ * 8 + 8], lhsT=Sbs[:, s_, :], rhs=wdT[:, s_ * 8:s_ * 8 + 8], start=True, stop=True),
                                     [Sbs, wdT], [B1], inc=(s_ == 15))
                            for s_ in range(16):
                                c.mm(lambda e, s_=s_: e.matmul(B1[:, 128 + s_ * 8:128 + s_ * 8 + 8], lhsT=Sbs[:, s_, :], rhs=qh[:, s_ * 8:s_ * 8 + 8], start=True, stop=True),
                                     [Sbs, qh], [B1], inc=(s_ == 15))
                            yield
                            V(lambda e: e.tensor_tensor(uT[:], ub[:], B1[:, 0:128], ALU.subtract), [ub, B1], [uT])
                            c.mm(lambda e: e.transpose(B0[:, 0:128], uT[:], ident[:]), [uT, ident], [B0])
                            yield
                            A(lambda e: e.copy(u[:], B0[:, 0:128]), [B0], [u])
                            c.mm(lambda e: e.matmul(B0[:, 128:256], lhsT=u[:], rhs=qkTm[:], start=True, stop=True), [u, qkTm], [B0])
                            yield
                            A(lambda e: e.copy(osum[:], B0[:, 128:256]), [B0], [osum])
                            V(lambda e: e.tensor_tensor(osum[:], osum[:], B1[:, 128:256], ALU.add), [osum, B1], [osum])
                            out_norm(osum[:], osum)
                            for s_ in range(16):
                                kt = ktm[s_ % 2]
                                V(lambda e, s_=s_, kt=kt: e.tensor_scalar(kt[:], ktl[:], selcol[:, s_:s_ + 1], None, ALU.mult), [ktl, selcol], [kt])
                                BS = B0 if s_ % 2 == 0 else B1
                                c.mm(lambda e, kt=kt, BS=BS: e.matmul(BS[:, 384:512], lhsT=kt[:], rhs=u[:], start=True, stop=True), [kt, u], [BS])
                                V(lambda e, s_=s_, BS=BS: e.scalar_tensor_tensor(Sout[:, s_, :], S0s[:, s_, :], eA2[:, s_ * 8 + 7:s_ * 8 + 8], BS[:, 384:512], ALU.mult, ALU.add),
                                  [S0s, eA2, BS], [Sout])
                                if s_ % 4 == 3:
                                    yield

                    def run_interleaved(gens):
                        active = list(gens)
                        while active:
                            for g_ in list(active):
                                try:
                                    next(g_)
                                except StopIteration:
                                    active.remove(g_)

                    for hd in range(4):
                        W = Wh[hd % 2]
                        for j, base in enumerate((0, 512, 1024, 1536)):
                            load_w(W, W[:, :, j * 128:(j + 1) * 128], wview(w_in_ab, base + hd * 128, 128))
                        c.dma("sp", S0s[:], st_delta[:, hd].rearrange("s p d -> p s d"), writes=[S0s])
                        with contextlib.ExitStack() as PJ:
                            sbp = lambda name, shape, dt=F32: c.sb(name, shape, dt, PJ)
                            zc = [sbp("zc%d" % i, [128, 515]) for i in range(2)]
                            zcs = sbp("zcs", [128, 16, 11])
                            accs = [sbp("acc%d" % i, [128, 512]) for i in range(2)]
                            sqs = [sbp("sq%d" % i, [128, 512], BF16) for i in range(2)]
                            rs1s = [sbp("rs1%d" % i, [128, 512]) for i in range(2)]
                            vTbs = [sbp("vTb%d" % i, [128, 512], BF16) for i in range(2)]
                            sstok = sbp("sstok", [128, NT]); rstok = sbp("rstok", [128, NT]); qtok = sbp("qtok", [128, 4, 128], BF16)
                            pend_tail = []
                            cntb = 0
                            for which in range(4):
                                chq = which * 4 + hd
                                for bi, (c0, n) in enumerate(TB):
                                    pi = nextpb()
                                    proj_fm(pi, W, which * 128, c0, n)
                                    while pend_tail:
                                        pend_tail.pop(0)()
                                    if which == 3:
                                        A(lambda e, pi=pi, c0=c0, n=n: e.activation(out=gT[:, c0:c0 + n], in_=PB[pi][:, 0:n], func=AF.Silu), [PB[pi]], [gT])
                                        continue
                                    cntb += 1
                                    acc = accs[cntb % 2]; sq = sqs[cntb % 2]; rs1 = rs1s[cntb % 2]; vTb = vTbs[cntb % 2]
                                    if bi < 4:
                                        z = zc[bi % 2]
                                        if bi == 0:
                                            G(lambda e, z=z: e.memset(z[:, 0:3], 0.0), [], [z])
                                        A(lambda e, z=z, pi=pi: e.copy(z[:, 3:515], PB[pi][:, 0:512]), [PB[pi]], [z])
                                        if bi < 3:
                                            zn = zc[(bi + 1) % 2]
                                            V(lambda e, z=z, zn=zn: e.tensor_copy(zn[:, 0:3], z[:, 512:515]), [z], [zn])
                                        else:
                                            G(lambda e, z=z, chq=chq: e.tensor_copy(cnew_p[:, chq, :], z[:, 512:515]), [z], [cnew_p])
                                        srcs = [z[:, i:i + 512] for i in range(4)]
                                        accv = acc[:, 0:512]
                                        zb_ = z
                                    else:
                                        V(lambda e, chq=chq: e.tensor_copy(zcs[:, :, 0:3], cst[:, chq, :].rearrange("p (s j) -> p s j", j=3)), [cst], [zcs])
                                        A(lambda e, pi=pi: e.copy(zcs[:, :, 3:11], PB[pi][:, 0:128].rearrange("p (s j) -> p s j", j=8)), [PB[pi]], [zcs])
                                        G(lambda e, chq=chq: e.tensor_copy(cnew_s[:, chq, :].rearrange("p (s j) -> p s j", j=3), zcs[:, :, 8:11]), [zcs], [cnew_s])
                                        srcs = [zcs[:, :, i:i + 8] for i in range(4)]
                                        accv = acc[:, 0:128].rearrange("p (s j) -> p s j", j=8)
                                        zb_ = zcs
                                    V(lambda e, accv=accv, srcs=srcs, chq=chq: e.tensor_scalar(accv, srcs[0], cw[:, chq, 0:1], None, ALU.mult), [zb_, cw], [acc])
                                    for i in range(1, 4):
                                        V(lambda e, i=i, accv=accv, srcs=srcs, chq=chq: e.scalar_tensor_tensor(accv, srcs[i], cw[:, chq, i:i + 1], accv, ALU.mult, ALU.add), [zb_, cw, acc], [acc])
                                    if which == 2:
                                        A(lambda e, n=n, acc=acc, vTb=vTb: e.activation(out=vTb[:, 0:n], in_=acc[:, 0:n], func=AF.Silu), [acc], [vTb])

                                        def tail_v(c0=c0, n=n, vTb=vTb):
                                            pj = nextpb()
                                            nt_ = n // 128
                                            for tt in range(nt_):
                                                c.mm(lambda e, tt=tt: e.transpose(PBb[pj][:, tt * 128:(tt + 1) * 128], vTb[:, tt * 128:(tt + 1) * 128], identb[:]), [vTb, identb], [PB[pj]], inc=(tt == nt_ - 1))
                                            V(lambda e: e.tensor_copy(vtok[:, c0 // 128:c0 // 128 + nt_, :], PBb[pj][:, 0:n].rearrange("p (t f) -> p t f", f=128)), [PB[pj]], [vtok])
                                        pend_tail.append(tail_v)
                                        continue
                                    dst = qnT if which == 0 else knT
                                    A(lambda e, n=n, acc=acc, dst=dst, c0=c0: e.activation(out=dst[:, c0:c0 + n], in_=acc[:, 0:n], func=AF.Silu), [acc], [dst])
                                    A(lambda e, n=n, sq=sq, dst=dst, c0=c0: e.activation(out=sq[:, 0:n], in_=dst[:, c0:c0 + n], func=AF.Square), [dst], [sq])
                                    pj = nextpb()
                                    nt_ = n // 128
                                    for tt in range(nt_):
                                        c.mm(lambda e, tt=tt, sq=sq: e.matmul(PB[pj][:, tt:tt + 1], lhsT=sq[:, tt * 128:(tt + 1) * 128], rhs=onesb[:, 0:1], start=True, stop=True),
                                             [sq, onesb], [PB[pj]], inc=(tt == nt_ - 1))
                                    V(lambda e, pj=pj, c0=c0, nt_=nt_: e.tensor_copy(sstok[:, c0 // 128:c0 // 128 + nt_], PB[pj][:, 0:nt_]), [PB[pj]], [sstok])
                                if which in (0, 1):
                                    sc = 128.0 if which == 0 else 1.0
                                    dst = qnT if which == 0 else knT
                                    while pend_tail:
                                        pend_tail.pop(0)()
                                    V(lambda e, sc=sc: e.tensor_scalar(rstok[:], sstok[:], sc, EPS * sc, ALU.mult, ALU.add), [sstok], [rstok])
                                    A(lambda e: e.activation(out=rstok[:], in_=rstok[:], func=AF.Sqrt), [rstok], [rstok])
                                    V(lambda e: e.reciprocal(rstok[:], rstok[:]), [rstok], [rstok])
                                    for (c0, n) in TB:
                                        nt_ = n // 128
                                        t0_ = c0 // 128
                                        pk = nextpb()
                                        for tt in range(nt_):
                                            c.mm(lambda e, tt=tt, c0=c0, dst=dst: e.transpose(PBb[pk][:, tt * 128:(tt + 1) * 128], dst[:, c0 + tt * 128:c0 + (tt + 1) * 128], identb[:]),
                                                 [dst, identb], [PB[pk]], inc=(tt == nt_ - 1))
                                        tok = ktok[:, t0_:t0_ + nt_, :] if which == 1 else qtok[:, 0:nt_, :]
                                        tokb = ktok if which == 1 else qtok
                                        V(lambda e, pk=pk, n=n, nt_=nt_, t0_=t0_, tok=tok: e.tensor_tensor(
                                            tok, PBb[pk][:, 0:n].rearrange("p (t f) -> p t f", f=128),
                                            rstok[:, t0_:t0_ + nt_].unsqueeze(2).to_broadcast([128, nt_, 128]), ALU.mult), [PB[pk], rstok], [tokb])
                                        pk2 = nextpb()
                                        for tt in range(nt_):
                                            src_t = ktok[:, t0_ + tt, :] if which == 1 else qtok[:, tt, :]
                                            c.mm(lambda e, tt=tt, src_t=src_t: e.transpose(PBb[pk2][:, tt * 128:(tt + 1) * 128], src_t, identb[:]),
                                                 [tokb, identb], [PB[pk2]], inc=(tt == nt_ - 1))
                                        A(lambda e, pk2=pk2, c0=c0, n=n, dst=dst: e.copy(dst[:, c0:c0 + n], PBb[pk2][:, 0:n]), [PB[pk2]], [dst])
                            while pend_tail:
                                pend_tail.pop(0)()
                            c.barrier()
                        if hd == 0:
                            stage(3)
                            tap("qnT", qnT, qnT[:, :]); tap("knT", knT, knT[:, :]); tap("vtok", vtok, vtok[:, :, :]); tap("gT", gT, gT[:, :])
                        with contextlib.ExitStack() as TL:
                            sbt = lambda name, shape, dt=F32: c.sb(name, shape, dt, TL)
                            TS = []
                            NSET = 3
                            for p_ in range(NSET):
                                d_ = {}
                                for nm in ("gTri", "nd", "nd2", "eA2", "ub"):
                                    d_[nm] = sbt(nm + str(p_), [128, 128])
                                for nm in ("TTb", "Vb", "Kb", "ktl", "qh", "wdT", "qkTm", "u", "osq"):
                                    d_[nm] = sbt(nm + str(p_), [128, 128], BF16)
                                d_["XX"] = [sbt("XX%d%d" % (p_, i), [128, 256], F32) for i in range(2)]
                                d_["PT"] = [sbt("PT%d%d" % (p_, i), [128, 128], F32) for i in range(2)]
                                TS.append(d_)
                            G(lambda e: e.memset(S[:], 0.0), [], [S])
                            G(lambda e: e.memset(Sb[:], 0.0), [], [Sb])
                            V(lambda e: e.tensor_copy(Sbs[:], S0s[:]), [S0s], [Sbs])
                            pending = list(range(NT)); free_sets = list(range(NSET)); active = []
                            while pending or active:
                                while pending and free_sets:
                                    t_ = pending.pop(0); s_ = free_sets.pop(0)
                                    T_ = TS[s_]
                                    if t_ == 16:
                                        T_ = dict(T_, uT=T_["nd"], osum=T_["nd2"], ktm=[T_["Vb"], T_["Kb"]])
                                    active.append((gdn_tile(hd, t_, T_, s_), s_))
                                for ent in list(active):
                                    try:
                                        next(ent[0])
                                    except StopIteration:
                                        active.remove(ent); free_sets.append(ent[1])
                            c.dma("sp", o_delta_p[hd], S[:], reads=[S], is_output=True)
                            c.dma("sp", o_delta_s[:, hd].rearrange("s p d -> p s d"), Sout[:], reads=[Sout], is_output=True)
                            c.barrier()
                        stage(4 + hd)
                    c.barrier()
                    HD.close()
                    cno_p = c.sb("cno_p", [3, 1536], F32, GD); cno_s = c.sb("cno_s", [48, 1536], F32, GD)
                    for ch in range(12):
                        pi = nextpb()
                        c.mm(lambda e, ch=ch, pi=pi: e.transpose(PB[pi][0:3, 0:128], cnew_p[:, ch, :], ident[:]), [cnew_p, ident], [PB[pi]])
                        c.mm(lambda e, ch=ch, pi=pi: e.transpose(PB[pi][0:48, 128:256], cnew_s[:, ch, :], ident[:]), [cnew_s, ident], [PB[pi]])
                        V(lambda e, ch=ch, pi=pi: e.tensor_copy(cno_p[:, ch * 128:(ch + 1) * 128], PB[pi][0:3, 0:128]), [PB[pi]], [cno_p])
                        V(lambda e, ch=ch, pi=pi: e.tensor_copy(cno_s[:, ch * 128:(ch + 1) * 128], PB[pi][0:48, 128:256]), [PB[pi]], [cno_s])
                    c.dma("sp", o_conv_p, cno_p[:], reads=[cno_p], is_output=True)
                    c.dma("sp", o_conv_s, cno_s[:], reads=[cno_s], is_output=True)
                    c.barrier()
                tap("omT", omT, omT[:, 0, :])

                omTb = c.sb("omTb", [128, 4, TT], BF16, L0)
                with contextlib.ExitStack() as SG:
                    sb = lambda name, shape, dt=F32: c.sb(name, shape, dt, SG)
                    uT = sb("uT", [128, 4, TT], BF16)
                    Wzu = sb("Wzu", [128, 8, 512], BF16); Wzv = sb("Wzv", [128, 8, 512], BF16)
                    wspn = sb("wspn", [128, 4, 128]); WspT = sb("WspT", [128, 4, 128], BF16); WspTf = sb("WspTf", [128, 4, 128])
                    WspS = sb("WspS", [128, 4, 128], BF16); x8 = sb("x8", [8, 128])
                    bspb = sb("bspb", [128, 4, 128]); bsps = sb("bsps", [128, 4, 128])
                    lngb = sb("lngb", [128, 512]); lnbb = sb("lnbb", [128, 512])
                    vg = sb("vg", [128, 512]); vsq = sb("vsq", [128, 512], BF16); vf = sb("vf", [128, 512]); vb = sb("vb", [128, 512], BF16)
                    st1 = sb("st1", [128, 8]); mx = sb("mx", [128, 4, 128])
                    load_w(Wzu, Wzu[:], wview(w_in_ab, 2056, 512))
                    load_w(Wzv, Wzv[:], wview(w_in_ab, 2568, 512))
                    c.dma("sp", wspn[:], w_sp.rearrange("g t s -> t g s"), writes=[wspn])
                    c.dma("sp", bspb[:].rearrange("p g t -> p (g t)"), b_sp.rearrange("g t -> (g t)").partition_broadcast(128), writes=[bspb])
                    c.dma("sp", lngb[:], ln_g.partition_broadcast(128), writes=[lngb])
                    c.dma("sp", lnbb[:], ln_b.partition_broadcast(128), writes=[lnbb])
                    for g in range(4):
                        pi = nextpb()
                        c.mm(lambda e, g=g: e.transpose(PB[pi][:, 0:128], wspn[:, g, :], ident[:]), [wspn, ident], [PB[pi]])
                        V(lambda e, g=g: e.tensor_tensor(WspTf[:, g, :], PB[pi][:, 0:128], masks[:, M_INCLU, :], ALU.mult), [PB[pi], masks], [WspTf])
                        V(lambda e, g=g: e.tensor_copy(WspT[:, g, :], WspTf[:, g, :]), [WspTf], [WspT])
                        V(lambda e, g=g: e.tensor_copy(x8[0:8, :].rearrange("p (s j) -> p s j", j=8), WspTf[0:8, g, 0:8].unsqueeze(1).to_broadcast([8, 16, 8])), [WspTf], [x8])
                        pj = nextpb()
                        c.mm(lambda e: e.matmul(PB[pj][:, 0:128], lhsT=masks[0:8, M_SPS, :], rhs=x8[0:8, :], start=True, stop=True), [masks, x8], [PB[pj]])
                        V(lambda e, g=g: e.tensor_tensor(WspS[:, g, :], PB[pj][:, 0:128], masks[:, M_INCLU_S, :], ALU.mult), [PB[pj], masks], [WspS])
                    V(lambda e: e.tensor_copy(bsps[:].rearrange("p g (s j) -> p g s j", j=8), bspb[:, :, 0:8].unsqueeze(2).to_broadcast([128, 4, 16, 8])), [bspb], [bsps])
                    for g in range(4):
                        for (c0, n) in TB:
                            pi = nextpb()
                            proj_fm(pi, Wzu, g * 128, c0, n)
                            A(lambda e, g=g, c0=c0, n=n, pi=pi: e.activation(out=uT[:, g, c0:c0 + n], in_=PB[pi][:, 0:n], func=AF.Gelu_apprx_tanh), [PB[pi]], [uT])
                    vgs = [vg, sb("vg2", [128, 512])]; vbs = [vb, vb]; st1s = [st1, sb("st1b", [128, 8])]
                    pzv = {}

                    def zv_proj(t):
                        pzv[t] = nextpb()
                        proj_tm(pzv[t], Wzv, 0, 512, t)
                    zv_proj(0)
                    for t in range(NT):
                        ts = slice(t * 128, (t + 1) * 128)
                        vg_, vb_, s1 = vgs[t % 2], vbs[t % 2], st1s[t % 2]
                        pi = pzv.pop(t)
                        if t + 1 < NT:
                            zv_proj(t + 1)
                        A(lambda e, pi=pi: e.activation(out=vg_[:], in_=PB[pi][:, 0:512], func=AF.Gelu_apprx_tanh, accum_out=s1[:, 0:1]), [PB[pi]], [vg_, s1])
                        A(lambda e: e.activation(out=vsq[:], in_=vg_[:], func=AF.Square, accum_out=s1[:, 1:2]), [vg_, s1], [vsq, s1])
                        V(lambda e: e.tensor_scalar(s1[:, 2:3], s1[:, 0:1], 1.0 / 512, None, ALU.mult), [s1], [s1])
                        V(lambda e: e.tensor_tensor(s1[:, 3:4], s1[:, 2:3], s1[:, 2:3], ALU.mult), [s1], [s1])
                        V(lambda e: e.scalar_tensor_tensor(s1[:, 4:5], s1[:, 1:2], 1.0 / 512, s1[:, 3:4], ALU.mult, ALU.subtract), [s1], [s1])
                        A(lambda e: e.activation(out=s1[:, 5:6], in_=s1[:, 4:5], func=AF.Sqrt, bias=EPS), [s1], [s1])
                        V(lambda e: e.reciprocal(s1[:, 5:6], s1[:, 5:6]), [s1], [s1])
                        V(lambda e: e.scalar_tensor_tensor(s1[:, 6:7], s1[:, 2:3], -1.0, s1[:, 5:6], ALU.mult, ALU.mult), [s1], [s1])
                        A(lambda e: e.activation(out=vg_[:], in_=vg_[:], func=AF.Identity, scale=s1[:, 5:6], bias=s1[:, 6:7]), [vg_, s1], [vg_])
                        V(lambda e: e.tensor_tensor(vg_[:], vg_[:], lngb[:], ALU.mult), [vg_, lngb], [vg_])
                        if t >= 15:
                            V(lambda e: e.tensor_tensor(vf[:], vg_[:], lnbb[:], ALU.add), [vg_, lnbb], [vf])
                            c.dma("sp", o_sguv_p if t == 15 else o_sguv_s, vf[:], reads=[vf], is_output=True)
                        V(lambda e: e.tensor_tensor(vb_[:], vg_[:], lnbb[:], ALU.add), [vg_, lnbb], [vb_])
                        Wm = WspS if t == 16 else WspT
                        bb = bsps if t == 16 else bspb
                        pj = nextpb()
                        for g in range(4):
                            c.mm(lambda e, g=g: e.matmul(PB[pj][:, g * 128:(g + 1) * 128], lhsT=vb_[:, g * 128:(g + 1) * 128], rhs=Wm[:, g, :], start=True, stop=True),
                                 [vb_, Wm], [PB[pj]], inc=(g == 3))
                        V(lambda e, pj=pj: e.tensor_tensor(mx[:], PB[pj][:, 0:512].rearrange("p (g t) -> p g t", g=4), bb[:], ALU.add), [PB[pj], bb], [mx])
                        G(lambda e, ts=ts: e.tensor_tensor(omTb[:, :, ts], mx[:], uT[:, :, ts], ALU.mult), [mx, uT], [omTb])
                    tap("omB", omTb, omTb[:, :, :].rearrange("p g t -> p (g t)"))
                    c.barrier()
                with contextlib.ExitStack() as WO:
                    Wo = c.sb("Wo", [128, 8, 1024], BF16, WO)
                    load_w(Wo, Wo[:], wview(w_out_ab, 0, 1024))
                    for t in range(NT):
                        for cb in range(2):
                            pi = nextpb()
                            for k in range(8):
                                src = omT if k < 4 else omTb
                                c.mm(lambda e, k=k, src=src: e.matmul(PB[pi][:, 0:512], lhsT=src[:, k % 4, t * 128:(t + 1) * 128], rhs=Wo[:, k, cb * 512:(cb + 1) * 512],
                                                                   start=(k == 0), stop=(k == 7)), [src, Wo], [PB[pi]], inc=(k == 7))
                            V(lambda e, t=t, cb=cb, pi=pi: e.tensor_tensor(h[:, t, cb * 512:(cb + 1) * 512], h[:, t, cb * 512:(cb + 1) * 512], PB[pi][:, 0:512], ALU.add), [h, PB[pi]], [h])
                    c.barrier()
            tap("hA", h, h[:, :, :])

            def ffn(li, gi):
                with contextlib.ExitStack() as FF:
                    Wu = [c.sb("Wu%d" % i, [128, 8, 512], BF16, FF) for i in range(2)]
                    Wd = [c.sb("Wd%d" % i, [128, 4, 1024], BF16, FF) for i in range(2)]
                    aT = [c.sb("aT%d" % i, [128, 4, 512], BF16, FF) for i in range(3)]
                    rl = [c.sb("rl%d" % i, [128, 512], BF16, FF) for i in range(2)]

                    def load_pass(p):
                        load_w(Wu[p % 2], Wu[p % 2][:], wview(w_up[li], p * 512, 512))
                        load_w(Wd[p % 2], Wd[p % 2][:], wview(w_down[li], 0, 1024, k0=p * 4, nk=4))

                    seq = [(p, c0, n) for p in range(8) for (c0, n) in TB]

                    def up(i):
                        p, c0, n = seq[i]
                        a = aT[i % 3]
                        for j in range(4):
                            pi = nextpb()
                            proj_fm(pi, Wu[p % 2], j * 128, c0, n)
                            r_ = rl[j % 2]
                            A(lambda e, pi=pi, n=n, r_=r_: e.activation(out=r_[:, 0:n], in_=PB[pi][:, 0:n], func=AF.Relu), [PB[pi]], [r_])
                            G(lambda e, j=j, n=n, r_=r_, a=a: e.tensor_tensor(a[:, j, 0:n], r_[:, 0:n], r_[:, 0:n], ALU.mult), [r_], [a])

                    def down(i):
                        p, c0, n = seq[i]
                        a = aT[i % 3]; wd = Wd[p % 2]
                        for tt in range(n // 128):
                            t = c0 // 128 + tt
                            for cb in range(2):
                                pi = nextpb()
                                for j in range(4):
                                    c.mm(lambda e, j=j, tt=tt, cb=cb: e.matmul(PB[pi][:, 0:512], lhsT=a[:, j, tt * 128:(tt + 1) * 128], rhs=wd[:, j, cb * 512:(cb + 1) * 512],
                                                                           start=(j == 0), stop=(j == 3)), [a, wd], [PB[pi]], inc=(j == 3))
                                V(lambda e, t=t, cb=cb, pi=pi: e.tensor_tensor(h[:, t, cb * 512:(cb + 1) * 512], h[:, t, cb * 512:(cb + 1) * 512], PB[pi][:, 0:512], ALU.add), [h, PB[pi]], [h])

                    load_pass(0)
                    load_pass(1)
                    rmsnorm_T(gi)
                    up(0)
                    for i in range(len(seq)):
                        if i + 1 < len(seq):
                            up(i + 1)
                        down(i)
                        if seq[i][1] == 2048 and seq[i][0] + 2 < 8:
                            load_pass(seq[i][0] + 2)
                    c.barrier()

            def ple(li, gi):
                with contextlib.ExitStack() as PE_:
                    Wg = c.sb("Wg", [128, 8, 1024], BF16, PE_); Wp = c.sb("Wp", [128, 2, 1024], BF16, PE_)
                    ptok = c.sb("ptok", [128, NT, 256], BF16, PE_); pT = c.sb("pT", [128, 2, TT], BF16, PE_)
                    gt = [c.sb("gt%d" % i, [128, 512], F32, PE_) for i in range(2)]
                    load_w(Wg, Wg[:], wview(w_pe_gate[li], 0, 1024))
                    load_w(Wp, Wp[:], wview(w_pe_proj[li], 0, 1024, nk=2))
                    c.dma("pool", ptok[:, 0:16, :], pp[li].rearrange("(n p) d -> p n d", p=128), writes=[ptok])
                    c.dma("pool", ptok[:, 16, :], psm[li], writes=[ptok])
                    for t in range(NT):
                        pi = nextpb()
                        for k in range(2):
                            c.mm(lambda e, k=k, t=t: e.transpose(PBb[pi][:, k * 128:(k + 1) * 128], ptok[:, t, k * 128:(k + 1) * 128], identb[:]), [ptok, identb], [PB[pi]], inc=(k == 1))
                        V(lambda e, t=t, pi=pi: e.tensor_copy(pT[:, :, t * 128:(t + 1) * 128], PBb[pi][:, 0:256].rearrange("p (k f) -> p k f", k=2)), [PB[pi]], [pT])
                    rmsnorm_T(gi)
                    for t in range(NT):
                        for cb in range(2):
                            g_ = gt[(2 * t + cb) % 2]
                            pi = nextpb()
                            proj_tm(pi, Wg, cb * 512, 512, t)
                            A(lambda e, pi=pi, g_=g_: e.activation(out=g_[:], in_=PB[pi][:, 0:512], func=AF.Sigmoid), [PB[pi]], [g_])
                            pj = nextpb()
                            proj_tm(pj, Wp, cb * 512, 512, t, src=pT, nk=2)
                            V(lambda e, pj=pj, g_=g_: e.tensor_tensor(g_[:], g_[:], PB[pj][:, 0:512], ALU.mult), [g_, PB[pj]], [g_])
                            G(lambda e, t=t, cb=cb, g_=g_: e.tensor_tensor(h[:, t, cb * 512:(cb + 1) * 512], h[:, t, cb * 512:(cb + 1) * 512], g_[:], ALU.add), [h, g_], [h])
                    c.barrier()

            ffn(0, 1)
            tap("hF0", h, h[:, :, :])
            ple(0, 2)
            tap("hP0", h, h[:, :, :])
            stage(30)

            rmsnorm_T(3)
            with contextlib.ExitStack() as L1:
                omC = c.sb("omC", [128, 8, TT], BF16, L1)
                with contextlib.ExitStack() as PL:
                    sb = lambda name, shape, dt=F32: c.sb(name, shape, dt, PL)
                    Wc = sb("Wc", [128, 8, 512], BF16)
                    load_w(Wc, Wc[:], wview(w_in_cd, 0, 512))
                    wpl = sb("wpl", [128, 4, 128], BF16)
                    c.dma("pool", wpl[:], w_pool.rearrange("g i o -> i g o"), writes=[wpl])
                    psc = sb("psc", [128, 4])
                    c.dma("sp", psc[:], pool_scale.rearrange("(g p) -> p g", p=128), writes=[psc], allow_slow_non_contiguous=True)
                    pst = sb("pst", [128, 4, 240])
                    with contextlib.ExitStack() as TMP:
                        stp = [c.sb("stp%d" % i, [120, 512], F32, TMP) for i in range(2)]
                        for i in range(2):
                            c.dma("sp", stp[i][:], st_pool.rearrange("s j d -> (s j) d")[i * 120:(i + 1) * 120, :], writes=[stp[i]])
                        for gi in range(4):
                            pi = nextpb()
                            for i in range(2):
                                c.mm(lambda e, i=i, gi=gi: e.transpose(PB[pi][:, i * 120:(i + 1) * 120], stp[i][0:120, gi * 128:(gi + 1) * 128], ident[0:120, 0:120]), [stp[i], ident], [PB[pi]])
                            V(lambda e, gi=gi, pi=pi: e.tensor_copy(pst[:, gi, :], PB[pi][:, 0:240]), [PB[pi]], [pst])
                        c.barrier()
                    stage(31)
                    xe = sb("xe", [128, 2063]); xa = [sb("xa%d" % i, [128, 2063]) for i in range(2)]
                    xes = sb("xes", [128, 16, 23]); xas = [sb("xas%d" % i, [128, 16, 23]) for i in range(2)]
                    mT = sb("mT", [128, TT], BF16); tmpf = sb("tmpf", [128, 16]); t240 = sb("t240", [128, 240])
                    pno_p = sb("pno_p", [15, 512]); pno_s = sb("pno_s", [120, 2, 512])
                    G(lambda e: e.memset(xe[:, 0:15], 0.0), [], [xe])
                    for b_ in xa + xas:
                        G(lambda e, b_=b_: e.memset(b_[:], 0.0), [], [b_])
                    for gi in range(4):
                        win = 2 ** (gi + 1)
                        for (c0, n) in TB:
                            pi = nextpb()
                            proj_fm(pi, Wc, gi * 128, c0, n)
                            if c0 < 2048:
                                A(lambda e, pi=pi, c0=c0: e.copy(xe[:, 15 + c0:15 + c0 + 512], PB[pi][:, 0:512]), [PB[pi]], [xe])
                            else:
                                A(lambda e, pi=pi: e.copy(xes[:, :, 15:23], PB[pi][:, 0:128].rearrange("p (s j) -> p s j", j=8)), [PB[pi]], [xes])
                        G(lambda e, gi=gi: e.tensor_copy(xes[:, :, 0:15], pst[:, gi, :].rearrange("p (s j) -> p s j", j=15)), [pst], [xes])
                        src, srcs = xe, xes
                        for k in range(gi + 1):
                            sh = 2 ** k
                            dst, dsts = xa[k % 2], xas[k % 2]
                            V(lambda e, src=src, dst=dst, sh=sh: e.tensor_tensor(dst[:, sh:2063], src[:, sh:2063], src[:, 0:2063 - sh], ALU.add), [src], [dst])
                            V(lambda e, srcs=srcs, dsts=dsts, sh=sh: e.tensor_tensor(dsts[:, :, sh:23], srcs[:, :, sh:23], srcs[:, :, 0:23 - sh], ALU.add), [srcs], [dsts])
                            src, srcs = dst, dsts
                        V(lambda e, src=src, win=win: e.scalar_tensor_tensor(mT[:, 0:2048], src[:, 15:2063], 1.0 / win, xe[:, 15:2063], ALU.mult, ALU.subtract), [src, xe], [mT])
                        V(lambda e, src=src, win=win: e.tensor_tensor(tmpf[:, 0:win - 1], src[:, 15:15 + win - 1], misc[:, 0:win - 1], ALU.mult), [src, misc], [tmpf])
                        V(lambda e, win=win: e.tensor_tensor(mT[:, 0:win - 1], tmpf[:, 0:win - 1], xe[:, 15:15 + win - 1], ALU.subtract), [tmpf, xe], [mT])
                        V(lambda e, srcs=srcs, win=win: e.scalar_tensor_tensor(mT[:, 2048:2176].rearrange("p (s j) -> p s j", j=8), srcs[:, :, 15:23], 1.0 / win, xes[:, :, 15:23], ALU.mult, ALU.subtract), [srcs, xes], [mT])
                        pi = nextpb()
                        c.mm(lambda e: e.transpose(PB[pi][0:15, 0:128], xe[:, 2048:2063], ident[:]), [xe, ident], [PB[pi]])
                        V(lambda e, gi=gi, pi=pi: e.tensor_copy(pno_p[0:15, gi * 128:(gi + 1) * 128], PB[pi][0:15, 0:128]), [PB[pi]], [pno_p])
                        G(lambda e: e.tensor_copy(t240[:].rearrange("p (s j) -> p s j", j=15), xes[:, :, 8:23]), [xes], [t240])
                        for i in range(2):
                            pj = nextpb()
                            c.mm(lambda e, i=i, pj=pj: e.transpose(PB[pj][0:120, 0:128], t240[:, i * 120:(i + 1) * 120], ident[:]), [t240, ident], [PB[pj]])
                            V(lambda e, i=i, gi=gi, pj=pj: e.tensor_copy(pno_s[0:120, i, gi * 128:(gi + 1) * 128], PB[pj][0:120, 0:128]), [PB[pj]], [pno_s])
                        for (c0, n) in TB:
                            pi = nextpb()
                            c.mm(lambda e, gi=gi, c0=c0, n=n, pi=pi: e.matmul(PB[pi][:, 0:n], lhsT=wpl[:, gi, :], rhs=mT[:, c0:c0 + n], start=True, stop=True), [wpl, mT], [PB[pi]])
                            V(lambda e, gi=gi, c0=c0, n=n, pi=pi: e.tensor_scalar(omC[:, gi, c0:c0 + n], PB[pi][:, 0:n], psc[:, gi:gi + 1], None, ALU.mult), [PB[pi], psc], [omC])
                        if gi == 0: stage(32)
                    c.dma("sp", o_pool_p, pno_p[:], reads=[pno_p], is_output=True)
                    for i in range(2):
                        c.dma("sp", o_pool_s.rearrange("s j d -> (s j) d")[i * 120:(i + 1) * 120, :], pno_s[:, i, :], reads=[pno_s], is_output=True)
                    c.barrier()

                with contextlib.ExitStack() as S5:
                    sb = lambda name, shape, dt=F32: c.sb(name, shape, dt, S5)
                    PI2 = 6.283185307179586
                    xdT = sb("xdT", [128, 4, TT], BF16)
                    with contextlib.ExitStack() as TMP:
                        Wd5 = c.sb("Wd5", [128, 8, 512], BF16, TMP)
                        load_w(Wd5, Wd5[:], wview(w_in_cd, 512, 512))
                        for cc in range(4):
                            for (c0, n) in TB:
                                pi = nextpb()
                                proj_fm(pi, Wd5, cc * 128, c0, n)
                                A(lambda e, cc=cc, c0=c0, n=n, pi=pi: e.copy(xdT[:, cc, c0:c0 + n], PB[pi][:, 0:n]), [PB[pi]], [xdT])
                        c.barrier()
                    flat = xnT.t.bitcast(F32).rearrange("p k f -> p (k f)")
                    EXr = Buf(flat[:, 0:2048].rearrange("p (m f) -> p m f", m=16), "EXr"); EXi = Buf(flat[:, 2048:4096].rearrange("p (m f) -> p m f", m=16), "EXi")
                    Er = sb("Er", [128, 16, 128]); Ei = sb("Ei", [128, 16, 128])
                    cv = lambda k: xnT.t[:, k, 0:2048].rearrange("p (m f) -> p m f", m=16)
                    CexpR = Buf(cv(4), "CexpR"); CexpIn = Buf(cv(5), "CexpIn"); BexpR = Buf(cv(6), "BexpR"); BexpI = Buf(cv(7), "BexpI")
                    SET = S5.enter_context(contextlib.ExitStack())
                    sbs = lambda name, shape, dt=F32: c.sb(name, shape, dt, SET)
                    diagD = sb("diagD", [128, 4, 128], BF16); Wgl = sb("Wgl", [128, 4, 512], BF16)
                    prm = sb("prm", [128, 24, 16])
                    prmi = sb("prmi", [128, 16], I32)
                    dsk = sb("dsk", [128, 4]); bgl = sb("bgl", [128, 4])
                    s0 = sb("s0", [128, 2, 16, 16])
                    sst = sb("sst", [128, 2, 16]); tiny = sb("tiny", [128, 4]); ee2 = sb("ee2", [128, 16, 2])
                    nidf = sb("nidf", [128, 128])
                    V(lambda e: e.tensor_scalar(nidf[:], ident[:], -1.0, None, ALU.mult), [ident], [nidf])
                    ldtb = sbs("ldtb", [128, 32]); l16 = sbs("l16", [16, 2, 128]); d4 = sbs("d4", [4, 2, 128])
                    Ball = sbs("Ball", [128, 2, 16, 16]); Bs = sbs("Bs", [128, 2, 16, 16]); Btmp = sbs("Btmp", [128, 2, 16, 16])
                    Cn = sbs("Cn", [128, 2, 64]); CX = sbs("CX", [128, 2, 128])
                    s0n = Buf(flat[0:16, 0:2048], "s0n")
                    P_ = lambda i: prm[:, i, :]
                    LRE, LIM, LDT, DTV, RHO, TH, Q, QF, R_, SN, CS, AB, LBR, LBI, AA, DEN, CR, CI, T1, T2 = range(20)
                    c.dma("sp", l16[:, 0, :], lam_re.rearrange("(m p) -> m p", p=128), writes=[l16])
                    c.dma("sp", l16[:, 1, :], lam_im.rearrange("(m p) -> m p", p=128), writes=[l16])
                    c.dma("sp", ldtb[:], log_dt.partition_broadcast(128), writes=[ldtb])
                    c.dma("sp", d4[:, 0, :], d_skip.rearrange("(c p) -> c p", p=128), writes=[d4])
                    c.dma("sp", d4[:, 1, :], b_glu.rearrange("(c p) -> c p", p=128), writes=[d4])
                    c.dma("sp", Ball[:, 0, :, :], b_re.rearrange("(m two) n c -> (two n) m c", two=2), writes=[Ball])
                    c.dma("sp", Ball[:, 1, :, :], b_im.rearrange("(m two) n c -> (two n) m c", two=2), writes=[Ball])
                    load_w(Wgl, Wgl[:], wview(w_glu, 0, 512, nk=4))
                    pi = nextpb()
                    c.mm(lambda e: e.transpose(PB[pi][:, 0:16], l16[0:16, 0, :], ident[0:16, 0:16]), [l16, ident], [PB[pi]])
                    c.mm(lambda e: e.transpose(PB[pi][:, 16:32], l16[0:16, 1, :], ident[0:16, 0:16]), [l16, ident], [PB[pi]])
                    c.mm(lambda e: e.transpose(PB[pi][:, 32:36], d4[0:4, 0, :], ident[0:4, 0:4]), [d4, ident], [PB[pi]])
                    c.mm(lambda e: e.transpose(PB[pi][:, 36:40], d4[0:4, 1, :], ident[0:4, 0:4]), [d4, ident], [PB[pi]])
                    V(lambda e: e.tensor_copy(prm[:, 0:2, :], PB[pi][:, 0:32].rearrange("p (a m) -> p a m", a=2)), [PB[pi]], [prm])
                    V(lambda e: e.tensor_copy(dsk[:], PB[pi][:, 32:36]), [PB[pi]], [dsk])
                    V(lambda e: e.tensor_copy(bgl[:], PB[pi][:, 36:40]), [PB[pi]], [bgl])
                    V(lambda e: e.tensor_copy(prm[0:64, LDT, :], ldtb[0:64, 0:32:2]), [ldtb], [prm])
                    V(lambda e: e.tensor_copy(prm[64:128, LDT, :], ldtb[64:128, 1:32:2]), [ldtb], [prm])
                    pp_ = [prm]
                    A(lambda e: e.activation(out=P_(DTV), in_=P_(LDT), func=AF.Exp), pp_, pp_)
                    V(lambda e: e.tensor_tensor(P_(RHO), P_(LRE), P_(DTV), ALU.mult), pp_, pp_)
                    A(lambda e: e.activation(out=P_(RHO), in_=P_(RHO), func=AF.Exp), pp_, pp_)
                    V(lambda e: e.tensor_tensor(P_(TH), P_(LIM), P_(DTV), ALU.mult), pp_, pp_)
                    V(lambda e: e.tensor_scalar(P_(R_), P_(TH), 0.125, None, ALU.mult), pp_, pp_)
                    V(lambda e: e.tensor_scalar(P_(R_), P_(R_), -3.14159, 3.14159, ALU.max, ALU.min), pp_, pp_)
                    A(lambda e: e.activation(out=P_(SN), in_=P_(R_), func=AF.Sin), pp_, pp_)
                    V(lambda e: e.tensor_scalar(P_(T1), P_(R_), -1.0, None, ALU.mult), pp_, pp_)
                    V(lambda e: e.tensor_tensor(P_(AB), P_(R_), P_(T1), ALU.max), pp_, pp_)
                    V(lambda e: e.tensor_scalar(P_(AB), P_(AB), -1.0, 1.5707963, ALU.mult, ALU.add), pp_, pp_)
                    A(lambda e: e.activation(out=P_(CS), in_=P_(AB), func=AF.Sin), pp_, pp_)
                    for _ in range(3):
                        V(lambda e: e.tensor_tensor(P_(T1), P_(CS), P_(CS), ALU.mult), pp_, pp_)
                        V(lambda e: e.tensor_tensor(P_(T2), P_(SN), P_(SN), ALU.mult), pp_, pp_)
                        V(lambda e: e.scalar_tensor_tensor(P_(SN), P_(CS), 2.0, P_(SN), ALU.mult, ALU.mult), pp_, pp_)
                        V(lambda e: e.tensor_tensor(P_(CS), P_(T1), P_(T2), ALU.subtract), pp_, pp_)
                    V(lambda e: e.tensor_tensor(P_(LBR), P_(RHO), P_(CS), ALU.mult), pp_, pp_)
                    V(lambda e: e.tensor_tensor(P_(LBI), P_(RHO), P_(SN), ALU.mult), pp_, pp_)
                    V(lambda e: e.tensor_scalar(P_(AA), P_(LBR), -1.0, None, ALU.add), pp_, pp_)
                    V(lambda e: e.tensor_tensor(P_(T1), P_(LRE), P_(LRE), ALU.mult), pp_, pp_)
                    V(lambda e: e.tensor_tensor(P_(T2), P_(LIM), P_(LIM), ALU.mult), pp_, pp_)
                    V(lambda e: e.tensor_tensor(P_(DEN), P_(T1), P_(T2), ALU.add), pp_, pp_)
                    V(lambda e: e.reciprocal(P_(DEN), P_(DEN)), pp_, pp_)
                    V(lambda e: e.tensor_tensor(P_(T1), P_(AA), P_(LRE), ALU.mult), pp_, pp_)
                    V(lambda e: e.tensor_tensor(P_(T2), P_(LBI), P_(LIM), ALU.mult), pp_, pp_)
                    V(lambda e: e.tensor_tensor(P_(CR), P_(T1), P_(T2), ALU.add), pp_, pp_)
                    V(lambda e: e.tensor_tensor(P_(CR), P_(CR), P_(DEN), ALU.mult), pp_, pp_)
                    V(lambda e: e.tensor_tensor(P_(T1), P_(LBI), P_(LRE), ALU.mult), pp_, pp_)
                    V(lambda e: e.tensor_tensor(P_(T2), P_(AA), P_(LIM), ALU.mult), pp_, pp_)
                    V(lambda e: e.tensor_tensor(P_(CI), P_(T1), P_(T2), ALU.subtract), pp_, pp_)
                    V(lambda e: e.tensor_tensor(P_(CI), P_(CI), P_(DEN), ALU.mult), pp_, pp_)
                    V(lambda e: e.tensor_copy(Er[:, :, 0], P_(CS)), pp_, [Er])
                    V(lambda e: e.tensor_copy(Ei[:, :, 0], P_(SN)), pp_, [Ei])
                    ta = Buf(flat[:, 0:1024].rearrange("p (m f) -> p m f", m=16), "ta")
                    tb = Buf(flat[:, 1024:2048].rearrange("p (m f) -> p m f", m=16), "tb")
                    L = 1
                    while L < 128:
                        bc = lambda T_: T_[:, :, L - 1:L].to_broadcast([128, 16, L])
                        V(lambda e, L=L: e.tensor_tensor(ta[:, :, 0:L], Er[:, :, 0:L], Er[:, :, L - 1:L].to_broadcast([128, 16, L]), ALU.mult), [Er], [ta])
                        V(lambda e, L=L: e.tensor_tensor(tb[:, :, 0:L], Ei[:, :, 0:L], Ei[:, :, L - 1:L].to_broadcast([128, 16, L]), ALU.mult), [Ei], [tb])
                        V(lambda e, L=L: e.tensor_tensor(Er[:, :, L:2 * L], ta[:, :, 0:L], tb[:, :, 0:L], ALU.subtract), [ta, tb], [Er])
                        V(lambda e, L=L: e.tensor_tensor(ta[:, :, 0:L], Er[:, :, 0:L], Ei[:, :, L - 1:L].to_broadcast([128, 16, L]), ALU.mult), [Er, Ei], [ta])
                        V(lambda e, L=L: e.tensor_tensor(tb[:, :, 0:L], Ei[:, :, 0:L], Er[:, :, L - 1:L].to_broadcast([128, 16, L]), ALU.mult), [Er, Ei], [tb])
                        V(lambda e, L=L: e.tensor_tensor(Ei[:, :, L:2 * L], ta[:, :, 0:L], tb[:, :, 0:L], ALU.add), [ta, tb], [Ei])
                        L *= 2
                    c.barrier()
                    crb = prm[:, CR, :].unsqueeze(2).to_broadcast([128, 16, 16]); cib = prm[:, CI, :].unsqueeze(2).to_broadcast([128, 16, 16])
                    V(lambda e: e.tensor_tensor(Bs[:, 0], Ball[:, 0], crb, ALU.mult), [Ball, prm], [Bs])
                    V(lambda e: e.tensor_tensor(Btmp[:, 0], Ball[:, 1], cib, ALU.mult), [Ball, prm], [Btmp])
                    V(lambda e: e.tensor_tensor(Bs[:, 0], Bs[:, 0], Btmp[:, 0], ALU.subtract), [Bs, Btmp], [Bs])
                    V(lambda e: e.tensor_tensor(Bs[:, 1], Ball[:, 1], crb, ALU.mult), [Ball, prm], [Bs])
                    V(lambda e: e.tensor_tensor(Btmp[:, 1], Ball[:, 0], cib, ALU.mult), [Ball, prm], [Btmp])
                    V(lambda e: e.tensor_tensor(Bs[:, 1], Bs[:, 1], Btmp[:, 1], ALU.add), [Bs, Btmp], [Bs])
                    V(lambda e: e.memset(EXr[:], 0.0), [], [EXr])
                    V(lambda e: e.memset(EXi[:], 0.0), [], [EXi])
                    for a_, EX in ((0, EXr), (1, EXi)):
                        for j in range(4):
                            V(lambda e, a_=a_, EX=EX, j=j: e.tensor_copy(EX[0:64, j:16:4, 32 * j:32 * j + 16], Bs[0:64, a_, j:16:4, :]), [Bs], [EX])
                            V(lambda e, a_=a_, EX=EX, j=j: e.tensor_copy(EX[64:128, j:16:4, 32 * j + 16:32 * j + 32], Bs[64:128, a_, j:16:4, :]), [Bs], [EX])
                    for a_, EX, BX in ((0, EXr, BexpR), (1, EXi, BexpI)):
                        for m4 in range(4):
                            pi = nextpb()
                            for j in range(4):
                                c.mm(lambda e, j=j, m4=m4, EX=EX: e.transpose(PB[pi][:, j * 128:(j + 1) * 128], EX[:, m4 * 4 + j, :], ident[:]), [EX, ident], [PB[pi]], inc=(j == 3))
                            V(lambda e, m4=m4, BX=BX, pi=pi: e.tensor_copy(BX[:, m4 * 4:m4 * 4 + 4, :], PB[pi][:, 0:512].rearrange("p (j f) -> p j f", j=4)), [PB[pi]], [BX])
                    c.barrier()
                    V(lambda e: e.memset(CexpR[:], 0.0), [], [CexpR])
                    V(lambda e: e.memset(CexpIn[:], 0.0), [], [CexpIn])
                    for cc in range(4):
                        c.dma("sp", Cn[:, 0, :], c_re.rearrange("g c n -> (g c) n")[cc * 128:(cc + 1) * 128, :], writes=[Cn])
                        c.dma("sp", Cn[:, 1, :], c_im.rearrange("g c n -> (g c) n")[cc * 128:(cc + 1) * 128, :], writes=[Cn])
                        for a_ in range(2):
                            V(lambda e, a_=a_: e.tensor_scalar(CX[:, a_, 0:64], Cn[:, a_, :], misc[:, 16:17], None, ALU.mult), [Cn, misc], [CX])
                            V(lambda e, a_=a_: e.tensor_scalar(CX[:, a_, 64:128], Cn[:, a_, :], misc[:, 17:18], None, ALU.mult), [Cn, misc], [CX])
                        pi = nextpb()
                        c.mm(lambda e: e.transpose(PB[pi][:, 0:128], CX[:, 0, :], ident[:]), [CX, ident], [PB[pi]], inc=False)
                        c.mm(lambda e: e.transpose(PB[pi][:, 128:256], CX[:, 1, :], ident[:]), [CX, ident], [PB[pi]])
                        for j in range(4):
                            V(lambda e, cc=cc, j=j, pi=pi: e.tensor_copy(CexpR[:, cc * 4 + j, 32 * j:32 * j + 32], PB[pi][:, 32 * j:32 * j + 32]), [PB[pi]], [CexpR])
                            V(lambda e, cc=cc, j=j, pi=pi: e.tensor_scalar(CexpIn[:, cc * 4 + j, 32 * j:32 * j + 32], PB[pi][:, 128 + 32 * j:128 + 32 * j + 32], -1.0, None, ALU.mult), [PB[pi]], [CexpIn])
                        V(lambda e, cc=cc: e.tensor_scalar(diagD[:, cc, :], ident[:], dsk[:, cc:cc + 1], None, ALU.mult), [ident, dsk], [diagD])
                    for a_ in range(2):
                        c.dma("sp", s0n[:, :], st_s5re if a_ == 0 else st_s5im, writes=[s0n])
                        for m4 in range(4):
                            pi = nextpb()
                            for j in range(4):
                                m = m4 * 4 + j
                                c.mm(lambda e, a_=a_, m=m, j=j: e.transpose(PB[pi][:, j * 16:(j + 1) * 16], s0n[0:16, m * 128:(m + 1) * 128], ident[0:16, 0:16]), [s0n, ident], [PB[pi]], inc=(j == 3))
                            V(lambda e, a_=a_, m4=m4, pi=pi: e.tensor_copy(s0[:, a_, m4 * 4:m4 * 4 + 4, :], PB[pi][:, 0:64].rearrange("p (j s) -> p j s", j=4)), [PB[pi]], [s0])
                    V(lambda e: e.memset(sst[:], 0.0), [], [sst])
                    V(lambda e: e.tensor_scalar(ee2[:, :, 0], Ei[:, :, 127], -1.0, None, ALU.mult), [Ei], [ee2])
                    V(lambda e: e.tensor_copy(ee2[:, :, 1], Ei[:, :, 127]), [Ei], [ee2])
                    c.barrier()
                    SET.close()
                    sfin = sb("sfin", [128, 2, 16, 16])
                    sbr = [sb("sbr%d" % i, [128, 512], BF16) for i in range(2)]; sbi = [sb("sbi%d" % i, [128, 512], BF16) for i in range(2)]
                    ygT = sb("ygT", [128, 4, 512], BF16); gate = sb("gate", [128, 512], BF16)
                    sop = sb("sop", [16, 2, 128])
                    WT = [Buf(flat[:, i * 512:(i + 1) * 512], "wt%d" % i) for i in range(8)]
                    WX = [Buf(xsb[0].t.bitcast(F32)[:, 0:512], "wx0"), Buf(xsb[1].t.bitcast(F32)[:, 0:512], "wx1")]
                    bank = [0]

                    def nb():
                        bank[0] = (bank[0] + 1) % 6
                        return bank[0]
                    it = 0
                    for bi, (c0, n) in enumerate(TB):
                        is_s = (c0 == 2048)
                        nt_ = n // 128
                        if is_s:
                            v3 = lambda ap: ap[:, 0:128].rearrange("p (s j) -> p s j", j=8)
                            eb = lambda T_, m: T_[:, m, 0:8].unsqueeze(1).to_broadcast([128, 16, 8])
                        else:
                            v3 = lambda ap: ap[:, 0:512].rearrange("p (t f) -> p t f", f=128)
                            eb = lambda T_, m: T_[:, m, :].unsqueeze(1).to_broadcast([128, 4, 128])
                        def emit_b(m_):
                            cc_ = m_ // 4
                            pr_ = nb()
                            c.mm(lambda e: e.matmul(PB[pr_][:, 0:n], lhsT=BexpR[:, m_, :], rhs=xdT[:, cc_, c0:c0 + n], start=True, stop=True), [BexpR, xdT], [PB[pr_]])
                            pq_ = nb()
                            c.mm(lambda e: e.matmul(PB[pq_][:, 0:n], lhsT=BexpI[:, m_, :], rhs=xdT[:, cc_, c0:c0 + n], start=True, stop=True), [BexpI, xdT], [PB[pq_]])
                            return pr_, pq_
                        bq = {0: emit_b(0)}
                        for cc in range(4):
                            pY = 6 + (cc % 2)
                            for j in range(4):
                                m = cc * 4 + j
                                it += 1
                                if m + 1 < 16:
                                    bq[m + 1] = emit_b(m + 1)
                                t1, t2, t3, q1, q2, q3 = WT[0], WT[1], WT[2], WX[0], WX[1], WT[3]
                                rbase = 3072 if it % 2 == 0 else 2048
                                rre, rim = (WT[6], WT[7]) if it % 2 == 0 else (WT[4], WT[5])
                                pr, pq = bq.pop(m)
                                t4 = q3
                                V(lambda e, m=m: e.tensor_tensor(v3(t1), v3(PB[pr]), eb(Er, m), ALU.mult), [PB[pr], Er], [t1])
                                V(lambda e, m=m: e.tensor_tensor(v3(t2), v3(PB[pq]), eb(Ei, m), ALU.mult), [PB[pq], Ei], [t2])
                                V(lambda e, m=m: e.tensor_tensor(v3(t3), v3(PB[pq]), eb(Er, m), ALU.mult), [PB[pq], Er], [t3])
                                V(lambda e, m=m: e.tensor_tensor(v3(t4), v3(PB[pr]), eb(Ei, m), ALU.mult), [PB[pr], Ei], [t4])
                                pc = nb()
                                c.mm(lambda e: e.matmul(PB[pc][:, 0:n], lhsT=ident[:], rhs=t1[:, 0:n], start=True, stop=False), [ident, t1], [PB[pc]], inc=False)
                                c.mm(lambda e: e.matmul(PB[pc][:, 0:n], lhsT=ident[:], rhs=t2[:, 0:n], start=False, stop=True), [ident, t2], [PB[pc]])
                                pd = nb()
                                c.mm(lambda e: e.matmul(PB[pd][:, 0:n], lhsT=ident[:], rhs=t3[:, 0:n], start=True, stop=False), [ident, t3], [PB[pd]], inc=False)
                                c.mm(lambda e: e.matmul(PB[pd][:, 0:n], lhsT=nidf[:], rhs=t4[:, 0:n], start=False, stop=True), [nidf, t4], [PB[pd]])
                                rho_b = prm[:, RHO, m:m + 1]
                                if not is_s:
                                    for tt in range(nt_):
                                        sl = slice(tt * 128, (tt + 1) * 128)
                                        V(lambda e, sl=sl, m=m: e.tensor_tensor_scan(rre[:, sl], rho_b.to_broadcast([128, 128]), PB[pc][:, sl], sst[:, 0, m:m + 1], ALU.mult, ALU.add), [prm, PB[pc], sst], [rre])
                                        V(lambda e, sl=sl, m=m: e.tensor_tensor_scan(rim[:, sl], rho_b.to_broadcast([128, 128]), PB[pd][:, sl], sst[:, 1, m:m + 1], ALU.mult, ALU.add), [prm, PB[pd], sst], [rim])
                                        last = rbase + tt * 128 + 127
                                        a_fwd = flat[:, last:last + 513:512]
                                        a_rev = flat[:, last + 512:last - 1:-512]
                                        V(lambda e, a_rev=a_rev, m=m: e.tensor_tensor(tiny[:, 0:2], a_rev, ee2[:, m, :], ALU.mult), [rre, rim, ee2], [tiny])
                                        V(lambda e, a_fwd=a_fwd, m=m: e.scalar_tensor_tensor(sst[:, :, m], a_fwd, Er[:, m, 127:128], tiny[:, 0:2], ALU.mult, ALU.add), [rre, rim, Er, tiny], [sst])
                                else:
                                    for s_ in range(16):
                                        sl = slice(s_ * 8, s_ * 8 + 8)
                                        V(lambda e, sl=sl, m=m, s_=s_: e.tensor_tensor_scan(rre[:, sl], rho_b.to_broadcast([128, 8]), PB[pc][:, sl], s0[:, 0, m, s_:s_ + 1], ALU.mult, ALU.add), [prm, PB[pc], s0], [rre])
                                        V(lambda e, sl=sl, m=m, s_=s_: e.tensor_tensor_scan(rim[:, sl], rho_b.to_broadcast([128, 8]), PB[pd][:, sl], s0[:, 1, m, s_:s_ + 1], ALU.mult, ALU.add), [prm, PB[pd], s0], [rim])
                                G(lambda e, m=m: e.tensor_tensor(v3(q1), v3(rre), eb(Er, m), ALU.mult), [rre, Er], [q1])
                                G(lambda e, m=m: e.tensor_tensor(v3(q2), v3(rim), eb(Ei, m), ALU.mult), [rim, Ei], [q2])
                                br_, bi_ = sbr[it % 2], sbi[it % 2]
                                if is_s:
                                    G(lambda e: e.tensor_tensor(q3[:, 0:n], q1[:, 0:n], q2[:, 0:n], ALU.subtract), [q1, q2], [q3])
                                    G(lambda e, m=m: e.tensor_copy(sfin[:, 0, m, :], q3[:, 7:128:8]), [q3], [sfin])
                                G(lambda e, br_=br_: e.tensor_tensor(br_[:, 0:n], q1[:, 0:n], q2[:, 0:n], ALU.subtract), [q1, q2], [br_])
                                G(lambda e, m=m: e.tensor_tensor(v3(q2), v3(rim), eb(Er, m), ALU.mult), [rim, Er], [q2])
                                G(lambda e, m=m: e.tensor_tensor(v3(q3), v3(rre), eb(Ei, m), ALU.mult), [rre, Ei], [q3])
                                if is_s:
                                    G(lambda e: e.tensor_tensor(q1[:, 0:n], q2[:, 0:n], q3[:, 0:n], ALU.add), [q2, q3], [q1])
                                    G(lambda e, m=m: e.tensor_copy(sfin[:, 1, m, :], q1[:, 7:128:8]), [q1], [sfin])
                                G(lambda e, bi_=bi_: e.tensor_tensor(bi_[:, 0:n], q2[:, 0:n], q3[:, 0:n], ALU.add), [q2, q3], [bi_])
                                c.mm(lambda e, m=m, j=j, br_=br_: e.matmul(PB[pY][:, 0:n], lhsT=CexpR[:, m, :], rhs=br_[:, 0:n], start=(j == 0), stop=False), [CexpR, br_], [PB[pY]], inc=False)
                                c.mm(lambda e, m=m, bi_=bi_: e.matmul(PB[pY][:, 0:n], lhsT=CexpIn[:, m, :], rhs=bi_[:, 0:n], start=False, stop=False), [CexpIn, bi_], [PB[pY]], inc=True)
                            c.mm(lambda e, cc=cc: e.matmul(PB[pY][:, 0:n], lhsT=diagD[:, cc, :], rhs=xdT[:, cc, c0:c0 + n], start=False, stop=True), [diagD, xdT], [PB[pY]])
                            A(lambda e, cc=cc, pY=pY: e.activation(out=ygT[:, cc, 0:n], in_=PB[pY][:, 0:n], func=AF.Gelu_apprx_tanh), [PB[pY]], [ygT])
                        for oc in range(4):
                            pg = nb()
                            for cc in range(4):
                                c.mm(lambda e, cc=cc, oc=oc: e.matmul(PB[pg][:, 0:n], lhsT=Wgl[:, cc, oc * 128:(oc + 1) * 128], rhs=ygT[:, cc, 0:n], start=(cc == 0), stop=(cc == 3)),
                                     [Wgl, ygT], [PB[pg]], inc=(cc == 3))
                            A(lambda e, oc=oc, pg=pg: e.activation(out=gate[:, 0:n], in_=PB[pg][:, 0:n], func=AF.Sigmoid, bias=bgl[:, oc:oc + 1]), [PB[pg], bgl], [gate])
                            V(lambda e, oc=oc: e.tensor_tensor(omC[:, 4 + oc, c0:c0 + n], ygT[:, oc, 0:n], gate[:, 0:n], ALU.mult), [ygT, gate], [omC])
                    c.barrier()
                    so = Buf(flat[0:16, 0:4096].rearrange("p (a f) -> p a f", a=2), "so")
                    pi = nextpb()
                    c.mm(lambda e: e.transpose(PB[pi][0:16, 0:128], sst[:, 0, :], ident[:]), [sst, ident], [PB[pi]], inc=False)
                    c.mm(lambda e: e.transpose(PB[pi][0:16, 128:256], sst[:, 1, :], ident[:]), [sst, ident], [PB[pi]])
                    V(lambda e: e.tensor_copy(sop[:], PB[pi][0:16, 0:256].rearrange("p (a f) -> p a f", a=2)), [PB[pi]], [sop])
                    c.dma("sp", o_s5re_p.rearrange("(m p) -> m p", p=128), sop[:, 0, :], reads=[sop], is_output=True)
                    c.dma("sp", o_s5im_p.rearrange("(m p) -> m p", p=128), sop[:, 1, :], reads=[sop], is_output=True)
                    for a_ in range(2):
                        for m4 in range(4):
                            pi = nextpb()
                            for j in range(4):
                                c.mm(lambda e, a_=a_, m4=m4, j=j: e.transpose(PB[pi][0:16, j * 128:(j + 1) * 128], sfin[:, a_, m4 * 4 + j, :], ident[:]), [sfin, ident], [PB[pi]], inc=(j == 3))
                            V(lambda e, a_=a_, m4=m4, pi=pi: e.tensor_copy(so[:, a_, m4 * 512:(m4 + 1) * 512], PB[pi][0:16, 0:512]), [PB[pi]], [so])
                    c.dma("sp", o_s5re_s, so[:, 0, :], reads=[so], is_output=True)
                    c.dma("sp", o_s5im_s, so[:, 1, :], reads=[so], is_output=True)
                    c.barrier()
                stage(33)
                tap("omC", omC, omC[:, :, :].rearrange("p g t -> p (g t)"))
                with contextlib.ExitStack() as WO:
                    Wo = c.sb("Wo1", [128, 8, 1024], BF16, WO)
                    load_w(Wo, Wo[:], wview(w_out_cd, 0, 1024))
                    for t in range(NT):
                        for cb in range(2):
                            pi = nextpb()
                            for k in range(8):
                                c.mm(lambda e, k=k: e.matmul(PB[pi][:, 0:512], lhsT=omC[:, k, t * 128:(t + 1) * 128], rhs=Wo[:, k, cb * 512:(cb + 1) * 512],
                                                             start=(k == 0), stop=(k == 7)), [omC, Wo], [PB[pi]], inc=(k == 7))
                            V(lambda e, t=t, cb=cb, pi=pi: e.tensor_tensor(h[:, t, cb * 512:(cb + 1) * 512], h[:, t, cb * 512:(cb + 1) * 512], PB[pi][:, 0:512], ALU.add), [h, PB[pi]], [h])
                    c.barrier()
            tap("hC", h, h[:, :, :])
            stage(34)
            ffn(1, 4)
            stage(35)
            ple(1, 5)
            stage(36)
            with contextlib.ExitStack() as FN:
                gfb = c.sb("gfb", [128, D], F32, FN)
                yo = [c.sb("yo%d" % i, [128, D], F32, FN) for i in range(2)]
                c.dma("sp", gfb[:], norm_final.partition_broadcast(128), writes=[gfb])
                for t in range(NT):
                    A(lambda e, t=t: e.activation(out=junk[:], in_=h[:, t, :], func=AF.Square, accum_out=ss[:, t:t + 1]), [h], [junk, ss])
                V(lambda e: e.tensor_scalar(rstd[:], ss[:], 1.0 / D, EPS, ALU.mult, ALU.add), [ss], [rstd])
                A(lambda e: e.activation(out=rstd[:], in_=rstd[:], func=AF.Sqrt), [rstd], [rstd])
                V(lambda e: e.reciprocal(rstd[:], rstd[:]), [rstd], [rstd])
                for t in range(NT):
                    y_ = yo[t % 2]
                    V(lambda e, t=t, y_=y_: e.scalar_tensor_tensor(y_[:], h[:, t, :], rstd[:, t:t + 1], gfb[:], ALU.mult, ALU.mult), [h, rstd, gfb], [y_])
                    dst = y_p[t * 128:(t + 1) * 128, :] if t < 16 else y_s
                    c.dma("sp", dst, y_[:], reads=[y_], is_output=True)

      except _Stop:
        pass
      c.dead = False
      c.finish()
    return nc


def _consts():
    i = np.arange(128)
    blk = (i[:, None] // 8) == (i[None, :] // 8)
    m = np.zeros((10, 128, 128), np.float32)
    le = i[:, None] <= i[None, :]
    gt = i[:, None] > i[None, :]
    m[0] = le; m[1] = le & blk
    m[2] = gt; m[3] = gt & blk
    m[4] = le; m[5] = le & blk
    m[6] = gt; m[7] = gt & blk
    m[8, :8, :] = (i[None, :] % 8 == np.arange(8)[:, None])
    sel = (i[:, None] // 8 == np.arange(16)[None, :]).astype(np.float32)
    misc = np.zeros((128, 32), np.float32)
    misc[:, 0:15] = 1.0 / (np.arange(15)[None, :] + 1.0)
    misc[:, 16] = ((i // 16) % 2 == 0)
    misc[:, 17] = ((i // 16) % 2 == 1)
    return np.eye(128, dtype=np.float32), m, sel, misc


def make_in_maps(inp):
    f = lambda a: np.ascontiguousarray(np.asarray(a, dtype=np.float32))
    ident, masks, sel, misc = _consts()
    shared = {
        "norm_mix": f(inp["norm_mix"]), "norm_ffn": f(inp["norm_ffn"]), "norm_pe": f(inp["norm_pe"]), "norm_final": f(inp["norm_final"]),
        "w_in_ab": f(inp["w_in_ab"][0]), "conv_qkv": f(inp["conv_qkv"][0]), "a_log": f(inp["a_log"][0]), "dt_bias": f(inp["dt_bias"][0]),
        "norm_o": f(inp["norm_o"][0]), "ln_g": f(inp["ln_v_gain"][0]), "ln_b": f(inp["ln_v_bias"][0]), "w_sp": f(inp["w_spatial"][0]),
        "b_sp": f(inp["b_spatial"][0]), "w_out_ab": f(inp["w_out_ab"][0]), "w_in_cd": f(inp["w_in_cd"][0]), "w_pool": f(inp["w_pool"][0]),
        "pool_scale": f(inp["pool_scale"][0]), "lam_re": f(inp["lam_re"][0]).reshape(2048), "lam_im": f(inp["lam_im"][0]).reshape(2048),
        "log_dt": f(inp["log_dt"][0]), "b_re": f(inp["b_re"][0]), "b_im": f(inp["b_im"][0]), "c_re": f(inp["c_re"][0]), "c_im": f(inp["c_im"][0]),
        "d_skip": f(inp["d_skip"][0]), "w_glu": f(inp["w_glu"][0]), "b_glu": f(inp["b_glu"][0]), "w_out_cd": f(inp["w_out_cd"][0]),
        "w_up": f(inp["w_ffn_up"]), "w_down": f(inp["w_ffn_down"]), "w_pe_proj": f(inp["w_pe_proj"]), "w_pe_gate": f(inp["w_pe_gate"]),
        "c_ident": ident, "c_masks": masks, "c_selcol": sel, "c_misc": misc,
    }
    maps = []
    for ci in range(NCORES):
        sl = slice(16 * ci, 16 * ci + 16)
        m = dict(shared)
        m["xp"] = f(inp["x_prompt"][ci]); m["xs"] = f(inp["x_sample"][sl]).reshape(128, D)
        m["pp"] = f(inp["p_prompt"][:, ci]); m["psm"] = f(inp["p_sample"][:, sl]).reshape(2, 128, 256)
        m["st_conv"] = f(inp["state_conv"][0, sl]).reshape(48, 1536); m["st_delta"] = f(inp["state_delta"][0, sl])
        m["st_pool"] = f(inp["state_pool"][0, sl]); m["st_s5re"] = f(inp["state_s5_re"][0, sl]).reshape(16, 2048)
        m["st_s5im"] = f(inp["state_s5_im"][0, sl]).reshape(16, 2048)
        maps.append(m)
    return maps


_NC_CACHE = {}


def kernel(**inputs):
    if "nc" not in _NC_CACHE:
        _NC_CACHE["nc"] = build_nc()
    nc = _NC_CACHE["nc"]
    maps = make_in_maps(inputs)
    res = run_bass_kernel_spmd(nc, maps, core_ids=list(range(NCORES)))
    R = res.results
    cat = lambda k, shp: np.concatenate([np.asarray(r[k], np.float32).reshape(shp) for r in R], axis=0)
    y_prompt = cat("y_p", (1, 2048, D)); y_sample = cat("y_s", (16, 8, D))
    conv_p = cat("o_conv_p", (1, 3, 1536))[None]; delta_p = cat("o_delta_p", (1, 4, 128, 128))[None]
    sguv_p = cat("o_sguv_p", (1, 128, 512))[None]; pool_p = cat("o_pool_p", (1, 15, 512))[None]
    s5re_p = cat("o_s5re_p", (1, 32, 64))[None]; s5im_p = cat("o_s5im_p", (1, 32, 64))[None]
    conv_s = cat("o_conv_s", (16, 3, 1536))[None]; delta_s = cat("o_delta_s", (16, 4, 128, 128))[None]
    sguv_s = cat("o_sguv_s", (16, 8, 512))[None]; pool_s = cat("o_pool_s", (16, 15, 512))[None]
    s5re_s = cat("o_s5re_s", (16, 32, 64))[None]; s5im_s = cat("o_s5im_s", (16, 32, 64))[None]
    return (y_prompt, y_sample, conv_p, delta_p, sguv_p, pool_p, s5re_p, s5im_p,
            conv_s, delta_s, sguv_s, pool_s, s5re_s, s5im_s)
```

```python
import contextlib
import numpy as np
import concourse.bass as bass
import concourse.mybir as mybir
from concourse.bass_utils import run_bass_kernel_spmd

F32 = mybir.dt.float32
BF16 = mybir.dt.bfloat16
I32 = mybir.dt.int32
ALU = mybir.AluOpType
AF = mybir.ActivationFunctionType

NCORES = 8
D = 1024
NT = 17
TT = 2176
EPS = 1e-6
DEBUG = {}


class Buf:
    def __init__(self, t, name="", excl=False):
        self.t = t
        self.name = name
        self.w = None
        self.r = {}
        self.excl = excl

    def __getitem__(self, k):
        return self.t[k]


class Ctx:
    NDMA = 40

    def __init__(self, nc, es):
        self.nc = nc
        self.es = es
        self.eng = {"pe": nc.tensor, "act": nc.scalar, "dve": nc.vector, "pool": nc.gpsimd, "sp": nc.sync}
        self.sems = {}
        self.cnt = {}
        for e in self.eng:
            self.sems[e] = es.enter_context(nc.semaphore("s_" + e))
            self.cnt[e] = 0
        self.dsem = [es.enter_context(nc.semaphore("d%d" % i)) for i in range(self.NDMA)]
        for i, s in enumerate(self.dsem):
            self.sems[("d", i)] = s
        self.dval = [0] * self.NDMA
        self.drr = 0
        self.drr_sw = 0
        self.seen = {e: {} for e in self.eng}
        self.pend = []
        self.out_events = []
        self.uid = 0
        self.dead = False

    def sb(self, name, shape, dtype=F32, es=None):
        self.uid += 1
        t = (es or self.es).enter_context(self.nc.sbuf_tensor("%s_%d" % (name, self.uid), list(shape), dtype))
        return Buf(t, name)

    def _wait(self, e, key, val):
        if val <= 0 or self.seen[e].get(key, 0) >= val:
            return
        self.eng[e].wait_ge(self.sems[key], val)
        self.seen[e][key] = val

    def _deps(self, e, reads, writes):
        for b in reads:
            if b.excl:
                continue
            if b.w is not None:
                self._wait(e, *b.w)
        for b in list(writes) + [b for b in reads if b.excl]:
            if b.w is not None and not (b.excl and e == "pe" and b.w[0] == "pe"):
                self._wait(e, *b.w)
            for k, v in b.r.items():
                self._wait(e, k, v)

    def _commit(self, ev, reads, writes):
        k, v = ev
        for b in reads:
            if b.excl:
                b.w = ev
                b.r = {}
            elif b.r.get(k, 0) < v:
                b.r[k] = v
        for b in writes:
            b.w = ev
            b.r = {}

    def op(self, e, fn, reads=(), writes=()):
        if self.dead:
            return
        self._deps(e, reads, writes)
        ins = fn(self.eng[e])
        self.cnt[e] += 1
        ins.then_inc(self.sems[e], 1)
        self._commit((e, self.cnt[e]), reads, writes)

    def mm(self, fn, reads=(), writes=(), inc=True):
        if self.dead:
            return
        e = "pe"
        self._deps(e, reads, writes)
        ins = fn(self.eng[e])
        self.pend.append((tuple(reads), tuple(writes)))
        if inc:
            self.cnt[e] += 1
            ins.then_inc(self.sems[e], 1)
            ev = (e, self.cnt[e])
            for r, w in self.pend:
                self._commit(ev, r, w)
            self.pend = []

    def dma(self, q, out_ap, in_ap, reads=(), writes=(), is_output=False, **kw):
        if self.dead:
            return
        half = self.NDMA // 2
        if q == "pool":
            i = half + self.drr_sw
            self.drr_sw = (self.drr_sw + 1) % (self.NDMA - half)
        else:
            i = self.drr
            self.drr = (self.drr + 1) % half
        key = ("d", i)
        self._wait(q, key, self.dval[i])
        self._deps(q, reads, writes)
        ins = self.eng[q].dma_start(out=out_ap, in_=in_ap, **kw)
        self.dval[i] += 16
        ins.then_inc(self.dsem[i], 16)
        ev = (key, self.dval[i])
        self._commit(ev, reads, writes)
        if is_output:
            self.out_events.append(ev)

    def barrier(self):
        if self.dead:
            return
        assert not self.pend
        for e in self.eng:
            for e2 in ("pe", "act", "dve", "pool"):
                self._wait(e, e2, self.cnt[e2])
            for i in range(self.NDMA):
                self._wait(e, ("d", i), self.dval[i])

    def finish(self):
        for k, v in self.out_events:
            self._wait("sp", k, v)
        for e in ("pe", "act", "dve", "pool"):
            self._wait("sp", e, self.cnt[e])


class _Stop(Exception):
    pass


def build_nc(dbg=None, stop=None):
    dbg = dbg or {}

    cref = []

    def stage(n):
        if stop == n:
            cref[0].dead = True
    nc = bass.Bass("TRN2", target_bir_lowering=False)

    def din(name, shape):
        return nc.dram_tensor(name, list(shape), F32, kind="ExternalInput").ap()

    def dout(name, shape):
        return nc.dram_tensor(name, list(shape), F32, kind="ExternalOutput").ap()

    xp = din("xp", [2048, D]); xs = din("xs", [128, D])
    pp = din("pp", [2, 2048, 256]); psm = din("psm", [2, 128, 256])
    st_conv = din("st_conv", [48, 1536]); st_delta = din("st_delta", [16, 4, 128, 128])
    st_pool = din("st_pool", [16, 15, 512]); st_s5re = din("st_s5re", [16, 2048]); st_s5im = din("st_s5im", [16, 2048])
    norm_mix = din("norm_mix", [2, D]); norm_ffn = din("norm_ffn", [2, D]); norm_pe = din("norm_pe", [2, D])
    norm_final = din("norm_final", [D])
    w_in_ab = din("w_in_ab", [D, 3080]); conv_qkv = din("conv_qkv", [4, 1536])
    a_log = din("a_log", [4]); dt_bias = din("dt_bias", [4]); norm_o = din("norm_o", [128])
    ln_g = din("ln_g", [512]); ln_b = din("ln_b", [512]); w_sp = din("w_sp", [4, 128, 128]); b_sp = din("b_sp", [4, 128])
    w_out_ab = din("w_out_ab", [D, D]); w_in_cd = din("w_in_cd", [D, D]); w_pool = din("w_pool", [4, 128, 128])
    pool_scale = din("pool_scale", [512])
    lam_re = din("lam_re", [2048]); lam_im = din("lam_im", [2048]); log_dt = din("log_dt", [32])
    b_re = din("b_re", [32, 64, 16]); b_im = din("b_im", [32, 64, 16]); c_re = din("c_re", [32, 16, 64]); c_im = din("c_im", [32, 16, 64])
    d_skip = din("d_skip", [512]); w_glu = din("w_glu", [512, 512]); b_glu = din("b_glu", [512]); w_out_cd = din("w_out_cd", [D, D])
    w_up = din("w_up", [2, D, 4096]); w_down = din("w_down", [2, 4096, D])
    w_pe_proj = din("w_pe_proj", [2, 256, D]); w_pe_gate = din("w_pe_gate", [2, D, D])
    c_ident = din("c_ident", [128, 128]); c_masks = din("c_masks", [10, 128, 128]); c_selcol = din("c_selcol", [128, 16]); c_misc = din("c_misc", [128, 32])

    y_p = dout("y_p", [2048, D]); y_s = dout("y_s", [128, D])
    o_conv_p = dout("o_conv_p", [3, 1536]); o_delta_p = dout("o_delta_p", [4, 128, 128]); o_sguv_p = dout("o_sguv_p", [128, 512])
    o_pool_p = dout("o_pool_p", [15, 512]); o_s5re_p = dout("o_s5re_p", [2048]); o_s5im_p = dout("o_s5im_p", [2048])
    o_conv_s = dout("o_conv_s", [48, 1536]); o_delta_s = dout("o_delta_s", [16, 4, 128, 128]); o_sguv_s = dout("o_sguv_s", [128, 512])
    o_pool_s = dout("o_pool_s", [16, 15, 512]); o_s5re_s = dout("o_s5re_s", [16, 2048]); o_s5im_s = dout("o_s5im_s", [16, 2048])
    dbg_out = {k: nc.dram_tensor("dbg_" + k, list(shp[0]), BF16 if shp[1] == "bf16" else F32, kind="ExternalOutput").ap() for k, shp in dbg.items()}

    with contextlib.ExitStack() as es:
      c = Ctx(nc, es)
      cref.append(c)
      try:
            V = lambda fn, r, w: c.op("dve", fn, r, w)
            A = lambda fn, r, w: c.op("act", fn, r, w)
            G = lambda fn, r, w: c.op("pool", fn, r, w)

            h = c.sb("h", [128, NT, D])
            xnT = c.sb("xnT", [128, 8, TT], BF16)
            ident = c.sb("ident", [128, 128]); identb = c.sb("identb", [128, 128], BF16)
            masks = c.sb("masks", [128, 10, 128])
            selcol = c.sb("selcol", [128, 16]); misc = c.sb("misc", [128, 32])
            onesb = c.sb("onesb", [128, 128], BF16); onesf = c.sb("onesf", [128, 128]); zerof = c.sb("zerof", [128, 128])
            gains = c.sb("gains", [128, 7, 8])
            junk = c.sb("junk", [128, D], BF16)
            ss = c.sb("ss", [128, NT]); rstd = c.sb("rstd", [128, NT])
            xsb = [c.sb("xsb%d" % i, [128, D], BF16) for i in range(2)]
            psum_t = es.enter_context(nc.psum_tensor("psum", [128, 8, 512], F32))
            PB = [Buf(psum_t[:, i, :], "pb%d" % i, excl=True) for i in range(8)]
            PBb = [psum_t.bitcast(BF16)[:, i, :] for i in range(8)]
            pbi = [0]

            def nextpb():
                i = pbi[0]
                pbi[0] = (i + 1) % 8
                return i

            M_TRIU, M_TRIU_S, M_STRICTL, M_STRICTL_S, M_INCLU, M_INCLU_S, M_TAIL, M_TAIL_S, M_SPS = range(9)

            c.dma("sp", h[:, 0:16, :], xp.rearrange("(n p) d -> p n d", p=128), writes=[h])
            c.dma("sp", h[:, 16, :], xs, writes=[h])
            c.dma("sp", ident[:], c_ident, writes=[ident])
            c.dma("sp", masks[:], c_masks.rearrange("m p f -> p m f"), writes=[masks])
            c.dma("sp", selcol[:], c_selcol, writes=[selcol])
            c.dma("sp", misc[:], c_misc, writes=[misc])
            gsrc = [norm_mix[0], norm_ffn[0], norm_pe[0], norm_mix[1], norm_ffn[1], norm_pe[1], norm_final]
            for i, g in enumerate(gsrc):
                c.dma("sp", gains[:, i, :], g.rearrange("(k p) -> p k", p=128), writes=[gains], allow_slow_non_contiguous=True)
            V(lambda e: e.tensor_copy(identb[:], ident[:]), [ident], [identb])
            G(lambda e: e.memset(onesb[:], 1.0), [], [onesb])
            G(lambda e: e.memset(onesf[:], 1.0), [], [onesf])
            G(lambda e: e.memset(zerof[:], 0.0), [], [zerof])
            stage(-1)

            def tap(name, buf, ap):
                if name in dbg_out:
                    c.dma("sp", dbg_out[name], ap, reads=[buf], is_output=True, allow_slow_non_contiguous=True)

            def load_w(dst_buf, dst_ap, src_ap):
                c.dma("pool", dst_ap, src_ap, writes=[dst_buf])

            def wview(w2d, c0, ncols, k0=0, nk=8):
                return w2d[k0 * 128:(k0 + nk) * 128, c0:c0 + ncols].rearrange("(k p) n -> p k n", p=128)

            def rmsnorm_T(gi):
                for t in range(NT):
                    A(lambda e, t=t: e.activation(out=junk[:], in_=h[:, t, :], func=AF.Square, accum_out=ss[:, t:t + 1]), [h], [junk, ss])
                V(lambda e: e.tensor_scalar(rstd[:], ss[:], 1.0 / D, EPS, ALU.mult, ALU.add), [ss], [rstd])
                A(lambda e: e.activation(out=rstd[:], in_=rstd[:], func=AF.Sqrt), [rstd], [rstd])
                V(lambda e: e.reciprocal(rstd[:], rstd[:]), [rstd], [rstd])
                for t in range(NT):
                    xb = xsb[t % 2]
                    A(lambda e, t=t, xb=xb: e.activation(out=xb[:], in_=h[:, t, :], func=AF.Copy, scale=rstd[:, t:t + 1]), [h, rstd], [xb])
                    pi = nextpb()
                    for k in range(8):
                        c.mm(lambda e, k=k, xb=xb, pi=pi: e.transpose(PBb[pi][:, k * 128:(k + 1) * 128], xb[:, k * 128:(k + 1) * 128], identb[:]),
                             [xb, identb], [PB[pi]], inc=(k == 7))
                    V(lambda e, t=t, pi=pi: e.tensor_tensor(
                        xnT[:, :, t * 128:(t + 1) * 128], PBb[pi].rearrange("p (k f) -> p k f", k=8),
                        gains[:, gi, :].unsqueeze(2).to_broadcast([128, 8, 128]), ALU.mult), [PB[pi], gains], [xnT])

            TB = [(0, 512), (512, 512), (1024, 512), (1536, 512), (2048, 128)]

            def proj_fm(pi, W, wcol, c0, n):
                for k in range(8):
                    c.mm(lambda e, k=k: e.matmul(PB[pi][:, 0:n], lhsT=W[:, k, wcol:wcol + 128], rhs=xnT[:, k, c0:c0 + n],
                                                 start=(k == 0), stop=(k == 7)), [W, xnT], [PB[pi]], inc=(k == 7))

            def proj_tm(pi, W, wcol, ncols, t, src=None, nk=8):
                src = src or xnT
                for k in range(nk):
                    c.mm(lambda e, k=k: e.matmul(PB[pi][:, 0:ncols], lhsT=src[:, k, t * 128:(t + 1) * 128], rhs=W[:, k, wcol:wcol + ncols],
                                                 start=(k == 0), stop=(k == nk - 1)), [W, src], [PB[pi]], inc=(k == nk - 1))

            rmsnorm_T(0)
            tap("xnT", xnT, xnT[:, 0, :])
            stage(1)

            with contextlib.ExitStack() as L0:
                omT = c.sb("omT", [128, 4, TT], BF16, L0)
                with contextlib.ExitStack() as GD:
                    sb = lambda name, shape, dt=F32: c.sb(name, shape, dt, GD)
                    Wh = [sb("Wh0", [128, 8, 512], BF16)] * 2
                    Wba = sb("Wba", [128, 8, 8], BF16)
                    cw = sb("cw", [128, 12, 4])
                    cst = sb("cst", [128, 12, 48])
                    cnew_s = sb("cnew_s", [128, 12, 48]); cnew_p = sb("cnew_p", [128, 12, 3])
                    alb = sb("alb", [128, 4]); dtb = sb("dtb", [128, 4]); nob = sb("nob", [128, 1])
                    bg = sb("bg", [128, NT, 8])
                    beta = sb("beta", [128, NT, 4]); nbeta = sb("nbeta", [128, NT, 4]); gg = sb("gg", [128, NT, 4])
                    gc = sb("gc", [128, NT, 4]); egc = sb("egc", [128, NT, 4]); bexp = sb("bexp", [128, NT, 4]); etail = sb("etail", [128, NT, 4])
                    ngc = sb("ngc", [128, NT, 4])
                    HD = GD.enter_context(contextlib.ExitStack())
                    sb = lambda name, shape, dt=F32: c.sb(name, shape, dt, HD)
                    qnT = sb("qnT", [128, TT], BF16); knT = sb("knT", [128, TT], BF16); gT = sb("gT", [128, TT], BF16)
                    ktok = sb("ktok", [128, NT, 128], BF16); vtok = sb("vtok", [128, NT, 128], BF16)
                    S = sb("S", [128, 128]); Sb = sb("Sb", [128, 128], BF16)
                    S0s = sb("S0s", [128, 16, 128]); Sbs = sb("Sbs", [128, 16, 128], BF16); Sout = S0s

                    c.dma("sp", alb[:], a_log.partition_broadcast(128), writes=[alb])
                    c.dma("sp", dtb[:], dt_bias.partition_broadcast(128), writes=[dtb])
                    c.dma("sp", nob[:], norm_o.rearrange("(p o) -> p o", o=1), writes=[nob])
                    load_w(Wba, Wba[:], wview(w_in_ab, 2048, 8))
                    with contextlib.ExitStack() as TMP:
                        cw4 = c.sb("cw4", [4, 1536], F32, TMP)
                        stc = c.sb("stc", [48, 1536], F32, TMP)
                        c.dma("sp", cw4[:], conv_qkv, writes=[cw4])
                        c.dma("sp", stc[:], st_conv, writes=[stc])
                        for ch in range(12):
                            pi = nextpb()
                            c.mm(lambda e, ch=ch, pi=pi: e.transpose(PB[pi][:, 0:4], cw4[0:4, ch * 128:(ch + 1) * 128], ident[0:4, 0:4]), [cw4, ident], [PB[pi]])
                            c.mm(lambda e, ch=ch, pi=pi: e.transpose(PB[pi][:, 64:112], stc[0:48, ch * 128:(ch + 1) * 128], ident[0:48, 0:48]), [stc, ident], [PB[pi]])
                            V(lambda e, ch=ch, pi=pi: e.tensor_copy(cw[:, ch, :], PB[pi][:, 0:4]), [PB[pi]], [cw])
                            V(lambda e, ch=ch, pi=pi: e.tensor_copy(cst[:, ch, :], PB[pi][:, 64:112]), [PB[pi]], [cst])
                        c.barrier()

                    pi = nextpb()
                    for t in range(NT):
                        for k in range(8):
                            c.mm(lambda e, k=k, t=t: e.matmul(PB[pi][:, t * 8:(t + 1) * 8], lhsT=xnT[:, k, t * 128:(t + 1) * 128], rhs=Wba[:, k, :],
                                                              start=(k == 0), stop=(k == 7)), [Wba, xnT], [PB[pi]], inc=(k == 7))
                    V(lambda e: e.tensor_copy(bg[:], PB[pi][:, 0:NT * 8].rearrange("p (t j) -> p t j", j=8)), [PB[pi]], [bg])
                    A(lambda e: e.activation(out=beta[:], in_=bg[:, :, 0:4], func=AF.Sigmoid), [bg], [beta])
                    V(lambda e: e.tensor_scalar(nbeta[:], beta[:], -1.0, None, ALU.mult), [beta], [nbeta])
                    V(lambda e: e.tensor_tensor(gg[:], bg[:, :, 4:8], dtb[:].unsqueeze(1).to_broadcast([128, NT, 4]), ALU.add), [bg, dtb], [gg])
                    A(lambda e: e.activation(out=gg[:], in_=gg[:], func=AF.Exp), [gg], [gg])
                    A(lambda e: e.activation(out=gg[:], in_=gg[:], func=AF.Ln, bias=1.0), [gg], [gg])
                    A(lambda e: e.activation(out=alb[:], in_=alb[:], func=AF.Exp), [alb], [alb])
                    V(lambda e: e.scalar_tensor_tensor(gg[:], gg[:], -1.0, alb[:].unsqueeze(1).to_broadcast([128, NT, 4]), ALU.mult, ALU.mult), [gg, alb], [gg])
                    pi = nextpb(); pj = nextpb()
                    for t in range(NT):
                        mtri = M_TRIU if t < 16 else M_TRIU_S
                        mtail = M_TAIL if t < 16 else M_TAIL_S
                        c.mm(lambda e, t=t, mtri=mtri: e.matmul(PB[pi][:, t * 4:(t + 1) * 4], lhsT=masks[:, mtri, :], rhs=gg[:, t, :], start=True, stop=True), [masks, gg], [PB[pi]])
                        c.mm(lambda e, t=t, mtail=mtail: e.matmul(PB[pj][:, t * 4:(t + 1) * 4], lhsT=masks[:, mtail, :], rhs=gg[:, t, :], start=True, stop=True), [masks, gg], [PB[pj]])
                    V(lambda e: e.tensor_copy(gc[:], PB[pi][:, 0:NT * 4].rearrange("p (t j) -> p t j", j=4)), [PB[pi]], [gc])
                    A(lambda e: e.activation(out=egc[:], in_=gc[:], func=AF.Exp), [gc], [egc])
                    V(lambda e: e.tensor_scalar(ngc[:], gc[:], -1.0, None, ALU.mult), [gc], [ngc])
                    A(lambda e: e.activation(out=etail[:], in_=PB[pj][:, 0:NT * 4].rearrange("p (t j) -> p t j", j=4), func=AF.Exp), [PB[pj]], [etail])
                    V(lambda e: e.tensor_tensor(bexp[:], beta[:], egc[:], ALU.mult), [beta, egc], [bexp])
                    tap("gc", gc, gc[:, :, 0])
                    tap("beta", beta, beta[:, :, 0])
                    stage(2)


                    def gdn_tile(hd, t, T_, sid):
                        b0, b1 = 2 * sid, 2 * sid + 1
                        B0, B1 = PB[b0], PB[b1]
                        is_s = (t == 16)
                        ts = slice(t * 128, (t + 1) * 128)
                        mtri = M_TRIU_S if is_s else M_TRIU
                        mstr = M_STRICTL_S if is_s else M_STRICTL
                        minc = M_INCLU_S if is_s else M_INCLU
                        gcol = gc[:, t, hd:hd + 1]
                        XX, PTb = T_["XX"], T_["PT"]
                        gTri, nd, nd2, eA2 = T_["gTri"], T_["nd"], T_["nd2"], T_["eA2"]
                        decm, decTm = nd, nd2
                        TTb, Vb, Kb, ktl, qh, wdT, qkTm, ub, u = T_["TTb"], T_["Vb"], T_["Kb"], T_["ktl"], T_["qh"], T_["wdT"], T_["qkTm"], T_["ub"], T_["u"]
                        osq, rr, on = T_["osq"], gTri, ub

                        def out_norm(src_ap, src_buf):
                            A(lambda e: e.activation(out=osq[:], in_=src_ap, func=AF.Square), [src_buf], [osq])
                            c.mm(lambda e: e.matmul(B0[:, 256:384], lhsT=onesb[:], rhs=osq[:], start=True, stop=True), [onesb, osq], [B0])
                            A(lambda e: e.activation(out=rr[:], in_=B0[:, 256:384], func=AF.Ln, scale=1.0 / 128, bias=EPS), [B0], [rr])
                            A(lambda e: e.activation(out=rr[:], in_=rr[:], func=AF.Exp, scale=-0.5), [rr], [rr])
                            V(lambda e: e.tensor_tensor(on[:], src_ap, rr[:], ALU.mult), [src_buf, rr], [on])
                            V(lambda e: e.scalar_tensor_tensor(omT[:, hd, ts], on[:], nob[:, 0:1], gT[:, ts], ALU.mult, ALU.mult), [on, nob, gT], [omT])

                        V(lambda e: e.tensor_scalar(gTri[:], masks[:, mtri, :], gg[:, t, hd:hd + 1], None, ALU.mult), [masks, gg], [gTri])
                        c.mm(lambda e: e.matmul(B0[:, 0:128], lhsT=onesf[:], rhs=gTri[:], start=True, stop=True), [onesf, gTri], [B0])
                        c.mm(lambda e: e.matmul(B0[:, 128:256], lhsT=knT[:, ts], rhs=knT[:, ts], start=True, stop=True), [knT], [B0])
                        c.mm(lambda e: e.matmul(B0[:, 256:384], lhsT=knT[:, ts], rhs=qnT[:, ts], start=True, stop=True), [knT, qnT], [B0])
                        yield
                        A(lambda e: e.activation(out=nd[:], in_=B0[:, 0:128], func=AF.Abs, bias=ngc[:, t, hd:hd + 1]), [B0, ngc], [nd])
                        A(lambda e: e.activation(out=eA2[:], in_=B0[:, 0:128], func=AF.Exp), [B0], [eA2])
                        A(lambda e: e.activation(out=nd[:], in_=nd[:], func=AF.Exp, scale=-1.0), [nd], [nd])
                        yield
                        V(lambda e: e.tensor_tensor(nd2[:], nd[:], masks[:, minc, :], ALU.mult), [nd, masks], [nd2])
                        V(lambda e: e.tensor_tensor(decm[:], nd[:], masks[:, mstr, :], ALU.mult), [nd, masks], [decm])
                        V(lambda e: e.scalar_tensor_tensor(XX[0][:, 0:128], B0[:, 128:256], nbeta[:, t, hd:hd + 1], decm[:], ALU.mult, ALU.mult), [B0, nbeta, decm], [XX[0]])
                        V(lambda e: e.tensor_tensor(qkTm[:], B0[:, 256:384], decTm[:], ALU.mult), [B0, decTm], [qkTm])
                        c.mm(lambda e: e.transpose(B1[:, 0:128], XX[0][:, 0:128], ident[:]), [XX[0], ident], [B1])
                        yield
                        A(lambda e: e.copy(XX[0][:, 128:256], B1[:, 0:128]), [B1], [XX[0]])
                        V(lambda e: e.tensor_tensor(PTb[0][:], B1[:, 0:128], ident[:], ALU.add), [B1, ident], [PTb[0]])
                        V(lambda e: e.tensor_scalar(Vb[:], vtok[:, t, :], beta[:, t, hd:hd + 1], None, ALU.mult), [vtok, beta], [Vb])
                        V(lambda e: e.tensor_scalar(Kb[:], ktok[:, t, :], bexp[:, t, hd:hd + 1], None, ALU.mult), [ktok, bexp], [Kb])
                        V(lambda e: e.tensor_scalar(ktl[:], ktok[:, t, :], etail[:, t, hd:hd + 1], None, ALU.mult), [ktok, etail], [ktl])
                        V(lambda e: e.tensor_tensor(qh[:], qnT[:, ts], eA2[:], ALU.mult), [qnT, eA2], [qh])
                        nlev = 2 if is_s else 6
                        for k in range(nlev):
                            a, b = k % 2, (k + 1) % 2
                            c.mm(lambda e: e.matmul(B0[:, 0:128], lhsT=XX[a][:, 128:256], rhs=XX[a][:, 0:128], start=True, stop=True), [XX[a]], [B0])
                            if k < nlev - 1:
                                c.mm(lambda e: e.matmul(B0[:, 128:256], lhsT=XX[a][:, 0:128], rhs=XX[a][:, 128:256], start=True, stop=True), [XX[a]], [B0])
                            yield
                            ncp = 256 if k < nlev - 1 else 128
                            A(lambda e: e.copy(XX[b][:, 0:ncp], B0[:, 0:ncp]), [B0], [XX[b]])
                            c.mm(lambda e: e.matmul(B1[:, 0:128], lhsT=XX[b][:, 0:128], rhs=PTb[a][:], start=True, stop=True), [XX[b], PTb[a]], [B1])
                            yield
                            dstP = PTb[b] if k < nlev - 1 else TTb
                            V(lambda e: e.tensor_tensor(dstP[:], PTb[a][:], B1[:, 0:128], ALU.add), [PTb[a], B1], [dstP])
                        c.mm(lambda e: e.matmul(B0[:, 0:128], lhsT=Kb[:], rhs=TTb[:], start=True, stop=True), [Kb, TTb], [B0])
                        if not is_s:
                            c.mm(lambda e: e.matmul(B0[:, 128:256], lhsT=TTb[:], rhs=Vb[:], start=True, stop=True), [TTb, Vb], [B0])
                        else:
                            c.mm(lambda e: e.matmul(B0[:, 128:256], lhsT=Vb[:], rhs=TTb[:], start=True, stop=True), [TTb, Vb], [B0])
                        yield
                        A(lambda e: e.copy(wdT[:], B0[:, 0:128]), [B0], [wdT])
                        A(lambda e: e.copy(ub[:], B0[:, 128:256]), [B0], [ub])
                        yield
                        if not is_s:
                            c.mm(lambda e: e.matmul(B1[:, 0:128], lhsT=wdT[:], rhs=Sb[:], start=True, stop=True), [wdT, Sb], [B1])
                            V(lambda e: e.tensor_tensor(u[:], ub[:], B1[:, 0:128], ALU.subtract), [ub, B1], [u])
                            c.mm(lambda e: e.matmul(B1[:, 128:256], lhsT=Sb[:], rhs=qh[:], start=True, stop=False), [Sb, qh], [B1], inc=False)
                            c.mm(lambda e: e.matmul(B1[:, 128:256], lhsT=u[:], rhs=qkTm[:], start=False, stop=True), [u, qkTm], [B1])
                            c.mm(lambda e: e.matmul(B0[:, 0:128], lhsT=ktl[:], rhs=u[:], start=True, stop=True), [ktl, u], [B0])
                            V(lambda e: e.scalar_tensor_tensor(S[:], S[:], eA2[:, 127:128], B0[:, 0:128], ALU.mult, ALU.add), [S, eA2, B0], [S])
                            A(lambda e: e.copy(Sb[:], S[:]), [S], [Sb])
                            yield
                            out_norm(B1[:, 128:256], B1)
                        else:
                            uT, osum, ktm = T_["uT"], T_["osum"], T_["ktm"]
                            for s_ in range(16):
                                c.mm(lambda e, s_=s_: e.matmul(B1[:, s_ * 8:s_ * 8 + 8], lhsT=Sbs[:, s_, :], rhs=wdT[:, s_ * 8:s_ * 8 + 8], start=True, stop=True),
                                     [Sbs, wdT], [B1], inc=(s_ == 15))
                            for s_ in range(16):
                                c.mm(lambda e, s_=s_: e.matmul(B1[:, 128 + s_ * 8:128 + s_ * 8 + 8], lhsT=Sbs[:, s_, :], rhs=qh[:, s_ * 8:s_ * 8 + 8], start=True, stop=True),
                                     [Sbs, qh], [B1], inc=(s_ == 15))
                            yield
                            V(lambda e: e.tensor_tensor(uT[:], ub[:], B1[:, 0:128], ALU.subtract), [ub, B1], [uT])
                            c.mm(lambda e: e.transpose(B0[:, 0:128], uT[:], ident[:]), [uT, ident], [B0])
                            yield
                            A(lambda e: e.copy(u[:], B0[:, 0:128]), [B0], [u])
                            c.mm(lambda e: e.matmul(B0[:, 128:256], lhsT=u[:], rhs=qkTm[:], start=True, stop=True), [u, qkTm], [B0])
                            yield
                            A(lambda e: e.copy(osum[:], B0[:, 128:256]), [B0], [osum])
                            V(lambda e: e.tensor_tensor(osum[:], osum[:], B1[:, 128:256], ALU.add), [osum, B1], [osum])
                            out_norm(osum[:], osum)
                            for s_ in range(16):
                                kt = ktm[s_ % 2]
                                V(lambda e, s_=s_, kt=kt: e.tensor_scalar(kt[:], ktl[:], selcol[:, s_:s_ + 1], None, ALU.mult), [ktl, selcol], [kt])
                                BS = B0 if s_ % 2 == 0 else B1
                                c.mm(lambda e, kt=kt, BS=BS: e.matmul(BS[:, 384:512], lhsT=kt[:], rhs=u[:], start=True, stop=True), [kt, u], [BS])
                                V(lambda e, s_=s_, BS=BS: e.scalar_tensor_tensor(Sout[:, s_, :], S0s[:, s_, :], eA2[:, s_ * 8 + 7:s_ * 8 + 8], BS[:, 384:512], ALU.mult, ALU.add),
                                  [S0s, eA2, BS], [Sout])
                                if s_ % 4 == 3:
                                    yield

                    def run_interleaved(gens):
                        active = list(gens)
                        while active:
                            for g_ in list(active):
                                try:
                                    next(g_)
                                except StopIteration:
                                    active.remove(g_)

                    for hd in range(4):
                        W = Wh[hd % 2]
                        for j, base in enumerate((0, 512, 1024, 1536)):
                            load_w(W, W[:, :, j * 128:(j + 1) * 128], wview(w_in_ab, base + hd * 128, 128))
                        c.dma("sp", S0s[:], st_delta[:, hd].rearrange("s p d -> p s d"), writes=[S0s])
                        with contextlib.ExitStack() as PJ:
                            sbp = lambda name, shape, dt=F32: c.sb(name, shape, dt, PJ)
                            zc = [sbp("zc%d" % i, [128, 515]) for i in range(2)]
                            zcs = sbp("zcs", [128, 16, 11])
                            accs = [sbp("acc%d" % i, [128, 512]) for i in range(2)]
                            sqs = [sbp("sq%d" % i, [128, 512], BF16) for i in range(2)]
                            rs1s = [sbp("rs1%d" % i, [128, 512]) for i in range(2)]
                            vTbs = [sbp("vTb%d" % i, [128, 512], BF16) for i in range(2)]
                            sstok = sbp("sstok", [128, NT]); rstok = sbp("rstok", [128, NT]); qtok = sbp("qtok", [128, 4, 128], BF16)
                            pend_tail = []
                            cntb = 0
                            for which in range(4):
                                chq = which * 4 + hd
                                for bi, (c0, n) in enumerate(TB):
                                    pi = nextpb()
                                    proj_fm(pi, W, which * 128, c0, n)
                                    while pend_tail:
                                        pend_tail.pop(0)()
                                    if which == 3:
                                        A(lambda e, pi=pi, c0=c0, n=n: e.activation(out=gT[:, c0:c0 + n], in_=PB[pi][:, 0:n], func=AF.Silu), [PB[pi]], [gT])
                                        continue
                                    cntb += 1
                                    acc = accs[cntb % 2]; sq = sqs[cntb % 2]; rs1 = rs1s[cntb % 2]; vTb = vTbs[cntb % 2]
                                    if bi < 4:
                                        z = zc[bi % 2]
                                        if bi == 0:
                                            G(lambda e, z=z: e.memset(z[:, 0:3], 0.0), [], [z])
                                        A(lambda e, z=z, pi=pi: e.copy(z[:, 3:515], PB[pi][:, 0:512]), [PB[pi]], [z])
                                        if bi < 3:
                                            zn = zc[(bi + 1) % 2]
                                            V(lambda e, z=z, zn=zn: e.tensor_copy(zn[:, 0:3], z[:, 512:515]), [z], [zn])
                                        else:
                                            G(lambda e, z=z, chq=chq: e.tensor_copy(cnew_p[:, chq, :], z[:, 512:515]), [z], [cnew_p])
                                        srcs = [z[:, i:i + 512] for i in range(4)]
                                        accv = acc[:, 0:512]
                                        zb_ = z
                                    else:
                                        V(lambda e, chq=chq: e.tensor_copy(zcs[:, :, 0:3], cst[:, chq, :].rearrange("p (s j) -> p s j", j=3)), [cst], [zcs])
                                        A(lambda e, pi=pi: e.copy(zcs[:, :, 3:11], PB[pi][:, 0:128].rearrange("p (s j) -> p s j", j=8)), [PB[pi]], [zcs])
                                        G(lambda e, chq=chq: e.tensor_copy(cnew_s[:, chq, :].rearrange("p (s j) -> p s j", j=3), zcs[:, :, 8:11]), [zcs], [cnew_s])
                                        srcs = [zcs[:, :, i:i + 8] for i in range(4)]
                                        accv = acc[:, 0:128].rearrange("p (s j) -> p s j", j=8)
                                        zb_ = zcs
                                    V(lambda e, accv=accv, srcs=srcs, chq=chq: e.tensor_scalar(accv, srcs[0], cw[:, chq, 0:1], None, ALU.mult), [zb_, cw], [acc])
                                    for i in range(1, 4):
                                        V(lambda e, i=i, accv=accv, srcs=srcs, chq=chq: e.scalar_tensor_tensor(accv, srcs[i], cw[:, chq, i:i + 1], accv, ALU.mult, ALU.add), [zb_, cw, acc], [acc])
                                    if which == 2:
                                        A(lambda e, n=n, acc=acc, vTb=vTb: e.activation(out=vTb[:, 0:n], in_=acc[:, 0:n], func=AF.Silu), [acc], [vTb])

                                        def tail_v(c0=c0, n=n, vTb=vTb):
                                            pj = nextpb()
                                            nt_ = n // 128
                                            for tt in range(nt_):
                                                c.mm(lambda e, tt=tt: e.transpose(PBb[pj][:, tt * 128:(tt + 1) * 128], vTb[:, tt * 128:(tt + 1) * 128], identb[:]), [vTb, identb], [PB[pj]], inc=(tt == nt_ - 1))
                                            V(lambda e: e.tensor_copy(vtok[:, c0 // 128:c0 // 128 + nt_, :], PBb[pj][:, 0:n].rearrange("p (t f) -> p t f", f=128)), [PB[pj]], [vtok])
                                        pend_tail.append(tail_v)
                                        continue
                                    dst = qnT if which == 0 else knT
                                    A(lambda e, n=n, acc=acc, dst=dst, c0=c0: e.activation(out=dst[:, c0:c0 + n], in_=acc[:, 0:n], func=AF.Silu), [acc], [dst])
                                    A(lambda e, n=n, sq=sq, dst=dst, c0=c0: e.activation(out=sq[:, 0:n], in_=dst[:, c0:c0 + n], func=AF.Square), [dst], [sq])
                                    pj = nextpb()
                                    nt_ = n // 128
                                    for tt in range(nt_):
                                        c.mm(lambda e, tt=tt, sq=sq: e.matmul(PB[pj][:, tt:tt + 1], lhsT=sq[:, tt * 128:(tt + 1) * 128], rhs=onesb[:, 0:1], start=True, stop=True),
                                             [sq, onesb], [PB[pj]], inc=(tt == nt_ - 1))
                                    V(lambda e, pj=pj, c0=c0, nt_=nt_: e.tensor_copy(sstok[:, c0 // 128:c0 // 128 + nt_], PB[pj][:, 0:nt_]), [PB[pj]], [sstok])
                                if which in (0, 1):
                                    sc = 128.0 if which == 0 else 1.0
                                    dst = qnT if which == 0 else knT
                                    while pend_tail:
                                        pend_tail.pop(0)()
                                    V(lambda e, sc=sc: e.tensor_scalar(rstok[:], sstok[:], sc, EPS * sc, ALU.mult, ALU.add), [sstok], [rstok])
                                    A(lambda e: e.activation(out=rstok[:], in_=rstok[:], func=AF.Sqrt), [rstok], [rstok])
                                    V(lambda e: e.reciprocal(rstok[:], rstok[:]), [rstok], [rstok])
                                    for (c0, n) in TB:
                                        nt_ = n // 128
                                        t0_ = c0 // 128
                                        pk = nextpb()
                                        for tt in range(nt_):
                                            c.mm(lambda e, tt=tt, c0=c0, dst=dst: e.transpose(PBb[pk][:, tt * 128:(tt + 1) * 128], dst[:, c0 + tt * 128:c0 + (tt + 1) * 128], identb[:]),
                                                 [dst, identb], [PB[pk]], inc=(tt == nt_ - 1))
                                        tok = ktok[:, t0_:t0_ + nt_, :] if which == 1 else qtok[:, 0:nt_, :]
                                        tokb = ktok if which == 1 else qtok
                                        V(lambda e, pk=pk, n=n, nt_=nt_, t0_=t0_, tok=tok: e.tensor_tensor(
                                            tok, PBb[pk][:, 0:n].rearrange("p (t f) -> p t f", f=128),
                                            rstok[:, t0_:t0_ + nt_].unsqueeze(2).to_broadcast([128, nt_, 128]), ALU.mult), [PB[pk], rstok], [tokb])
                                        pk2 = nextpb()
                                        for tt in range(nt_):
                                            src_t = ktok[:, t0_ + tt, :] if which == 1 else qtok[:, tt, :]
                                            c.mm(lambda e, tt=tt, src_t=src_t: e.transpose(PBb[pk2][:, tt * 128:(tt + 1) * 128], src_t, identb[:]),
                                                 [tokb, identb], [PB[pk2]], inc=(tt == nt_ - 1))
                                        A(lambda e, pk2=pk2, c0=c0, n=n, dst=dst: e.copy(dst[:, c0:c0 + n], PBb[pk2][:, 0:n]), [PB[pk2]], [dst])
                            while pend_tail:
                                pend_tail.pop(0)()
                            c.barrier()
                        if hd == 0:
                            stage(3)
                            tap("qnT", qnT, qnT[:, :]); tap("knT", knT, knT[:, :]); tap("vtok", vtok, vtok[:, :, :]); tap("gT", gT, gT[:, :])
                        with contextlib.ExitStack() as TL:
                            sbt = lambda name, shape, dt=F32: c.sb(name, shape, dt, TL)
                            TS = []
                            NSET = 3
                            for p_ in range(NSET):
                                d_ = {}
                                for nm in ("gTri", "nd", "nd2", "eA2", "ub"):
                                    d_[nm] = sbt(nm + str(p_), [128, 128])
                                for nm in ("TTb", "Vb", "Kb", "ktl", "qh", "wdT", "qkTm", "u", "osq"):
                                    d_[nm] = sbt(nm + str(p_), [128, 128], BF16)
                                d_["XX"] = [sbt("XX%d%d" % (p_, i), [128, 256], F32) for i in range(2)]
                                d_["PT"] = [sbt("PT%d%d" % (p_, i), [128, 128], F32) for i in range(2)]
                                TS.append(d_)
                            G(lambda e: e.memset(S[:], 0.0), [], [S])
                            G(lambda e: e.memset(Sb[:], 0.0), [], [Sb])
                            V(lambda e: e.tensor_copy(Sbs[:], S0s[:]), [S0s], [Sbs])
                            pending = list(range(NT)); free_sets = list(range(NSET)); active = []
                            while pending or active:
                                while pending and free_sets:
                                    t_ = pending.pop(0); s_ = free_sets.pop(0)
                                    T_ = TS[s_]
                                    if t_ == 16:
                                        T_ = dict(T_, uT=T_["nd"], osum=T_["nd2"], ktm=[T_["Vb"], T_["Kb"]])
                                    active.append((gdn_tile(hd, t_, T_, s_), s_))
                                for ent in list(active):
                                    try:
                                        next(ent[0])
                                    except StopIteration:
                                        active.remove(ent); free_sets.append(ent[1])
                            c.dma("sp", o_delta_p[hd], S[:], reads=[S], is_output=True)
                            c.dma("sp", o_delta_s[:, hd].rearrange("s p d -> p s d"), Sout[:], reads=[Sout], is_output=True)
                            c.barrier()
                        stage(4 + hd)
                    c.barrier()
                    HD.close()
                    cno_p = c.sb("cno_p", [3, 1536], F32, GD); cno_s = c.sb("cno_s", [48, 1536], F32, GD)
                    for ch in range(12):
                        pi = nextpb()
                        c.mm(lambda e, ch=ch, pi=pi: e.transpose(PB[pi][0:3, 0:128], cnew_p[:, ch, :], ident[:]), [cnew_p, ident], [PB[pi]])
                        c.mm(lambda e, ch=ch, pi=pi: e.transpose(PB[pi][0:48, 128:256], cnew_s[:, ch, :], ident[:]), [cnew_s, ident], [PB[pi]])
                        V(lambda e, ch=ch, pi=pi: e.tensor_copy(cno_p[:, ch * 128:(ch + 1) * 128], PB[pi][0:3, 0:128]), [PB[pi]], [cno_p])
                        V(lambda e, ch=ch, pi=pi: e.tensor_copy(cno_s[:, ch * 128:(ch + 1) * 128], PB[pi][0:48, 128:256]), [PB[pi]], [cno_s])
                    c.dma("sp", o_conv_p, cno_p[:], reads=[cno_p], is_output=True)
                    c.dma("sp", o_conv_s, cno_s[:], reads=[cno_s], is_output=True)
                    c.barrier()
                tap("omT", omT, omT[:, 0, :])

                omTb = c.sb("omTb", [128, 4, TT], BF16, L0)
                with contextlib.ExitStack() as SG:
                    sb = lambda name, shape, dt=F32: c.sb(name, shape, dt, SG)
                    uT = sb("uT", [128, 4, TT], BF16)
                    Wzu = sb("Wzu", [128, 8, 512], BF16); Wzv = sb("Wzv", [128, 8, 512], BF16)
                    wspn = sb("wspn", [128, 4, 128]); WspT = sb("WspT", [128, 4, 128], BF16); WspTf = sb("WspTf", [128, 4, 128])
                    WspS = sb("WspS", [128, 4, 128], BF16); x8 = sb("x8", [8, 128])
                    bspb = sb("bspb", [128, 4, 128]); bsps = sb("bsps", [128, 4, 128])
                    lngb = sb("lngb", [128, 512]); lnbb = sb("lnbb", [128, 512])
                    vg = sb("vg", [128, 512]); vsq = sb("vsq", [128, 512], BF16); vf = sb("vf", [128, 512]); vb = sb("vb", [128, 512], BF16)
                    st1 = sb("st1", [128, 8]); mx = sb("mx", [128, 4, 128])
                    load_w(Wzu, Wzu[:], wview(w_in_ab, 2056, 512))
                    load_w(Wzv, Wzv[:], wview(w_in_ab, 2568, 512))
                    c.dma("sp", wspn[:], w_sp.rearrange("g t s -> t g s"), writes=[wspn])
                    c.dma("sp", bspb[:].rearrange("p g t -> p (g t)"), b_sp.rearrange("g t -> (g t)").partition_broadcast(128), writes=[bspb])
                    c.dma("sp", lngb[:], ln_g.partition_broadcast(128), writes=[lngb])
                    c.dma("sp", lnbb[:], ln_b.partition_broadcast(128), writes=[lnbb])
                    for g in range(4):
                        pi = nextpb()
                        c.mm(lambda e, g=g: e.transpose(PB[pi][:, 0:128], wspn[:, g, :], ident[:]), [wspn, ident], [PB[pi]])
                        V(lambda e, g=g: e.tensor_tensor(WspTf[:, g, :], PB[pi][:, 0:128], masks[:, M_INCLU, :], ALU.mult), [PB[pi], masks], [WspTf])
                        V(lambda e, g=g: e.tensor_copy(WspT[:, g, :], WspTf[:, g, :]), [WspTf], [WspT])
                        V(lambda e, g=g: e.tensor_copy(x8[0:8, :].rearrange("p (s j) -> p s j", j=8), WspTf[0:8, g, 0:8].unsqueeze(1).to_broadcast([8, 16, 8])), [WspTf], [x8])
                        pj = nextpb()
                        c.mm(lambda e: e.matmul(PB[pj][:, 0:128], lhsT=masks[0:8, M_SPS, :], rhs=x8[0:8, :], start=True, stop=True), [masks, x8], [PB[pj]])
                        V(lambda e, g=g: e.tensor_tensor(WspS[:, g, :], PB[pj][:, 0:128], masks[:, M_INCLU_S, :], ALU.mult), [PB[pj], masks], [WspS])
                    V(lambda e: e.tensor_copy(bsps[:].rearrange("p g (s j) -> p g s j", j=8), bspb[:, :, 0:8].unsqueeze(2).to_broadcast([128, 4, 16, 8])), [bspb], [bsps])
                    for g in range(4):
                        for (c0, n) in TB:
                            pi = nextpb()
                            proj_fm(pi, Wzu, g * 128, c0, n)
                            A(lambda e, g=g, c0=c0, n=n, pi=pi: e.activation(out=uT[:, g, c0:c0 + n], in_=PB[pi][:, 0:n], func=AF.Gelu_apprx_tanh), [PB[pi]], [uT])
                    vgs = [vg, sb("vg2", [128, 512])]; vbs = [vb, vb]; st1s = [st1, sb("st1b", [128, 8])]
                    pzv = {}

                    def zv_proj(t):
                        pzv[t] = nextpb()
                        proj_tm(pzv[t], Wzv, 0, 512, t)
                    zv_proj(0)
                    for t in range(NT):
                        ts = slice(t * 128, (t + 1) * 128)
                        vg_, vb_, s1 = vgs[t % 2], vbs[t % 2], st1s[t % 2]
                        pi = pzv.pop(t)
                        if t + 1 < NT:
                            zv_proj(t + 1)
                        A(lambda e, pi=pi: e.activation(out=vg_[:], in_=PB[pi][:, 0:512], func=AF.Gelu_apprx_tanh, accum_out=s1[:, 0:1]), [PB[pi]], [vg_, s1])
                        A(lambda e: e.activation(out=vsq[:], in_=vg_[:], func=AF.Square, accum_out=s1[:, 1:2]), [vg_, s1], [vsq, s1])
                        V(lambda e: e.tensor_scalar(s1[:, 2:3], s1[:, 0:1], 1.0 / 512, None, ALU.mult), [s1], [s1])
                        V(lambda e: e.tensor_tensor(s1[:, 3:4], s1[:, 2:3], s1[:, 2:3], ALU.mult), [s1], [s1])
                        V(lambda e: e.scalar_tensor_tensor(s1[:, 4:5], s1[:, 1:2], 1.0 / 512, s1[:, 3:4], ALU.mult, ALU.subtract), [s1], [s1])
                        A(lambda e: e.activation(out=s1[:, 5:6], in_=s1[:, 4:5], func=AF.Sqrt, bias=EPS), [s1], [s1])
                        V(lambda e: e.reciprocal(s1[:, 5:6], s1[:, 5:6]), [s1], [s1])
                        V(lambda e: e.scalar_tensor_tensor(s1[:, 6:7], s1[:, 2:3], -1.0, s1[:, 5:6], ALU.mult, ALU.mult), [s1], [s1])
                        A(lambda e: e.activation(out=vg_[:], in_=vg_[:], func=AF.Identity, scale=s1[:, 5:6], bias=s1[:, 6:7]), [vg_, s1], [vg_])
                        V(lambda e: e.tensor_tensor(vg_[:], vg_[:], lngb[:], ALU.mult), [vg_, lngb], [vg_])
                        if t >= 15:
                            V(lambda e: e.tensor_tensor(vf[:], vg_[:], lnbb[:], ALU.add), [vg_, lnbb], [vf])
                            c.dma("sp", o_sguv_p if t == 15 else o_sguv_s, vf[:], reads=[vf], is_output=True)
                        V(lambda e: e.tensor_tensor(vb_[:], vg_[:], lnbb[:], ALU.add), [vg_, lnbb], [vb_])
                        Wm = WspS if t == 16 else WspT
                        bb = bsps if t == 16 else bspb
                        pj = nextpb()
                        for g in range(4):
                            c.mm(lambda e, g=g: e.matmul(PB[pj][:, g * 128:(g + 1) * 128], lhsT=vb_[:, g * 128:(g + 1) * 128], rhs=Wm[:, g, :], start=True, stop=True),
                                 [vb_, Wm], [PB[pj]], inc=(g == 3))
                        V(lambda e, pj=pj: e.tensor_tensor(mx[:], PB[pj][:, 0:512].rearrange("p (g t) -> p g t", g=4), bb[:], ALU.add), [PB[pj], bb], [mx])
                        G(lambda e, ts=ts: e.tensor_tensor(omTb[:, :, ts], mx[:], uT[:, :, ts], ALU.mult), [mx, uT], [omTb])
                    tap("omB", omTb, omTb[:, :, :].rearrange("p g t -> p (g t)"))
                    c.barrier()
                with contextlib.ExitStack() as WO:
                    Wo = c.sb("Wo", [128, 8, 1024], BF16, WO)
                    load_w(Wo, Wo[:], wview(w_out_ab, 0, 1024))
                    for t in range(NT):
                        for cb in range(2):
                            pi = nextpb()
                            for k in range(8):
                                src = omT if k < 4 else omTb
                                c.mm(lambda e, k=k, src=src: e.matmul(PB[pi][:, 0:512], lhsT=src[:, k % 4, t * 128:(t + 1) * 128], rhs=Wo[:, k, cb * 512:(cb + 1) * 512],
                                                                   start=(k == 0), stop=(k == 7)), [src, Wo], [PB[pi]], inc=(k == 7))
                            V(lambda e, t=t, cb=cb, pi=pi: e.tensor_tensor(h[:, t, cb * 512:(cb + 1) * 512], h[:, t, cb * 512:(cb + 1) * 512], PB[pi][:, 0:512], ALU.add), [h, PB[pi]], [h])
                    c.barrier()
            tap("hA", h, h[:, :, :])

            def ffn(li, gi):
                with contextlib.ExitStack() as FF:
                    Wu = [c.sb("Wu%d" % i, [128, 8, 512], BF16, FF) for i in range(2)]
                    Wd = [c.sb("Wd%d" % i, [128, 4, 1024], BF16, FF) for i in range(2)]
                    aT = [c.sb("aT%d" % i, [128, 4, 512], BF16, FF) for i in range(3)]
                    rl = [c.sb("rl%d" % i, [128, 512], BF16, FF) for i in range(2)]

                    def load_pass(p):
                        load_w(Wu[p % 2], Wu[p % 2][:], wview(w_up[li], p * 512, 512))
                        load_w(Wd[p % 2], Wd[p % 2][:], wview(w_down[li], 0, 1024, k0=p * 4, nk=4))

                    seq = [(p, c0, n) for p in range(8) for (c0, n) in TB]

                    def up(i):
                        p, c0, n = seq[i]
                        a = aT[i % 3]
                        for j in range(4):
                            pi = nextpb()
                            proj_fm(pi, Wu[p % 2], j * 128, c0, n)
                            r_ = rl[j % 2]
                            A(lambda e, pi=pi, n=n, r_=r_: e.activation(out=r_[:, 0:n], in_=PB[pi][:, 0:n], func=AF.Relu), [PB[pi]], [r_])
                            G(lambda e, j=j, n=n, r_=r_, a=a: e.tensor_tensor(a[:, j, 0:n], r_[:, 0:n], r_[:, 0:n], ALU.mult), [r_], [a])

                    def down(i):
                        p, c0, n = seq[i]
                        a = aT[i % 3]; wd = Wd[p % 2]
                        for tt in range(n // 128):
                            t = c0 // 128 + tt
                            for cb in range(2):
                                pi = nextpb()
                                for j in range(4):
                                    c.mm(lambda e, j=j, tt=tt, cb=cb: e.matmul(PB[pi][:, 0:512], lhsT=a[:, j, tt * 128:(tt + 1) * 128], rhs=wd[:, j, cb * 512:(cb + 1) * 512],
                                                                           start=(j == 0), stop=(j == 3)), [a, wd], [PB[pi]], inc=(j == 3))
                                V(lambda e, t=t, cb=cb, pi=pi: e.tensor_tensor(h[:, t, cb * 512:(cb + 1) * 512], h[:, t, cb * 512:(cb + 1) * 512], PB[pi][:, 0:512], ALU.add), [h, PB[pi]], [h])

                    load_pass(0)
                    load_pass(1)
                    rmsnorm_T(gi)
                    up(0)
                    for i in range(len(seq)):
                        if i + 1 < len(seq):
                            up(i + 1)
                        down(i)
                        if seq[i][1] == 2048 and seq[i][0] + 2 < 8:
                            load_pass(seq[i][0] + 2)
                    c.barrier()

            def ple(li, gi):
                with contextlib.ExitStack() as PE_:
                    Wg = c.sb("Wg", [128, 8, 1024], BF16, PE_); Wp = c.sb("Wp", [128, 2, 1024], BF16, PE_)
                    ptok = c.sb("ptok", [128, NT, 256], BF16, PE_); pT = c.sb("pT", [128, 2, TT], BF16, PE_)
                    gt = [c.sb("gt%d" % i, [128, 512], F32, PE_) for i in range(2)]
                    load_w(Wg, Wg[:], wview(w_pe_gate[li], 0, 1024))
                    load_w(Wp, Wp[:], wview(w_pe_proj[li], 0, 1024, nk=2))
                    c.dma("pool", ptok[:, 0:16, :], pp[li].rearrange("(n p) d -> p n d", p=128), writes=[ptok])
                    c.dma("pool", ptok[:, 16, :], psm[li], writes=[ptok])
                    for t in range(NT):
                        pi = nextpb()
                        for k in range(2):
                            c.mm(lambda e, k=k, t=t: e.transpose(PBb[pi][:, k * 128:(k + 1) * 128], ptok[:, t, k * 128:(k + 1) * 128], identb[:]), [ptok, identb], [PB[pi]], inc=(k == 1))
                        V(lambda e, t=t, pi=pi: e.tensor_copy(pT[:, :, t * 128:(t + 1) * 128], PBb[pi][:, 0:256].rearrange("p (k f) -> p k f", k=2)), [PB[pi]], [pT])
                    rmsnorm_T(gi)
                    for t in range(NT):
                        for cb in range(2):
                            g_ = gt[(2 * t + cb) % 2]
                            pi = nextpb()
                            proj_tm(pi, Wg, cb * 512, 512, t)
                            A(lambda e, pi=pi, g_=g_: e.activation(out=g_[:], in_=PB[pi][:, 0:512], func=AF.Sigmoid), [PB[pi]], [g_])
                            pj = nextpb()
                            proj_tm(pj, Wp, cb * 512, 512, t, src=pT, nk=2)
                            V(lambda e, pj=pj, g_=g_: e.tensor_tensor(g_[:], g_[:], PB[pj][:, 0:512], ALU.mult), [g_, PB[pj]], [g_])
                            G(lambda e, t=t, cb=cb, g_=g_: e.tensor_tensor(h[:, t, cb * 512:(cb + 1) * 512], h[:, t, cb * 512:(cb + 1) * 512], g_[:], ALU.add), [h, g_], [h])
                    c.barrier()

            ffn(0, 1)
            tap("hF0", h, h[:, :, :])
            ple(0, 2)
            tap("hP0", h, h[:, :, :])
            stage(30)

            rmsnorm_T(3)
            with contextlib.ExitStack() as L1:
                omC = c.sb("omC", [128, 8, TT], BF16, L1)
                with contextlib.ExitStack() as PL:
                    sb = lambda name, shape, dt=F32: c.sb(name, shape, dt, PL)
                    Wc = sb("Wc", [128, 8, 512], BF16)
                    load_w(Wc, Wc[:], wview(w_in_cd, 0, 512))
                    wpl = sb("wpl", [128, 4, 128], BF16)
                    c.dma("pool", wpl[:], w_pool.rearrange("g i o -> i g o"), writes=[wpl])
                    psc = sb("psc", [128, 4])
                    c.dma("sp", psc[:], pool_scale.rearrange("(g p) -> p g", p=128), writes=[psc], allow_slow_non_contiguous=True)
                    pst = sb("pst", [128, 4, 240])
                    with contextlib.ExitStack() as TMP:
                        stp = [c.sb("stp%d" % i, [120, 512], F32, TMP) for i in range(2)]
                        for i in range(2):
                            c.dma("sp", stp[i][:], st_pool.rearrange("s j d -> (s j) d")[i * 120:(i + 1) * 120, :], writes=[stp[i]])
                        for gi in range(4):
                            pi = nextpb()
                            for i in range(2):
                                c.mm(lambda e, i=i, gi=gi: e.transpose(PB[pi][:, i * 120:(i + 1) * 120], stp[i][0:120, gi * 128:(gi + 1) * 128], ident[0:120, 0:120]), [stp[i], ident], [PB[pi]])
                            V(lambda e, gi=gi, pi=pi: e.tensor_copy(pst[:, gi, :], PB[pi][:, 0:240]), [PB[pi]], [pst])
                        c.barrier()
                    stage(31)
                    xe = sb("xe", [128, 2063]); xa = [sb("xa%d" % i, [128, 2063]) for i in range(2)]
                    xes = sb("xes", [128, 16, 23]); xas = [sb("xas%d" % i, [128, 16, 23]) for i in range(2)]
                    mT = sb("mT", [128, TT], BF16); tmpf = sb("tmpf", [128, 16]); t240 = sb("t240", [128, 240])
                    pno_p = sb("pno_p", [15, 512]); pno_s = sb("pno_s", [120, 2, 512])
                    G(lambda e: e.memset(xe[:, 0:15], 0.0), [], [xe])
                    for b_ in xa + xas:
                        G(lambda e, b_=b_: e.memset(b_[:], 0.0), [], [b_])
                    for gi in range(4):
                        win = 2 ** (gi + 1)
                        for (c0, n) in TB:
                            pi = nextpb()
                            proj_fm(pi, Wc, gi * 128, c0, n)
                            if c0 < 2048:
                                A(lambda e, pi=pi, c0=c0: e.copy(xe[:, 15 + c0:15 + c0 + 512], PB[pi][:, 0:512]), [PB[pi]], [xe])
                            else:
                                A(lambda e, pi=pi: e.copy(xes[:, :, 15:23], PB[pi][:, 0:128].rearrange("p (s j) -> p s j", j=8)), [PB[pi]], [xes])
                        G(lambda e, gi=gi: e.tensor_copy(xes[:, :, 0:15], pst[:, gi, :].rearrange("p (s j) -> p s j", j=15)), [pst], [xes])
                        src, srcs = xe, xes
                        for k in range(gi + 1):
                            sh = 2 ** k
                            dst, dsts = xa[k % 2], xas[k % 2]
                            V(lambda e, src=src, dst=dst, sh=sh: e.tensor_tensor(dst[:, sh:2063], src[:, sh:2063], src[:, 0:2063 - sh], ALU.add), [src], [dst])
                            V(lambda e, srcs=srcs, dsts=dsts, sh=sh: e.tensor_tensor(dsts[:, :, sh:23], srcs[:, :, sh:23], srcs[:, :, 0:23 - sh], ALU.add), [srcs], [dsts])
                            src, srcs = dst, dsts
                        V(lambda e, src=src, win=win: e.scalar_tensor_tensor(mT[:, 0:2048], src[:, 15:2063], 1.0 / win, xe[:, 15:2063], ALU.mult, ALU.subtract), [src, xe], [mT])
                        V(lambda e, src=src, win=win: e.tensor_tensor(tmpf[:, 0:win - 1], src[:, 15:15 + win - 1], misc[:, 0:win - 1], ALU.mult), [src, misc], [tmpf])
                        V(lambda e, win=win: e.tensor_tensor(mT[:, 0:win - 1], tmpf[:, 0:win - 1], xe[:, 15:15 + win - 1], ALU.subtract), [tmpf, xe], [mT])
                        V(lambda e, srcs=srcs, win=win: e.scalar_tensor_tensor(mT[:, 2048:2176].rearrange("p (s j) -> p s j", j=8), srcs[:, :, 15:23], 1.0 / win, xes[:, :, 15:23], ALU.mult, ALU.subtract), [srcs, xes], [mT])
                        pi = nextpb()
                        c.mm(lambda e: e.transpose(PB[pi][0:15, 0:128], xe[:, 2048:2063], ident[:]), [xe, ident], [PB[pi]])
                        V(lambda e, gi=gi, pi=pi: e.tensor_copy(pno_p[0:15, gi * 128:(gi + 1) * 128], PB[pi][0:15, 0:128]), [PB[pi]], [pno_p])
                        G(lambda e: e.tensor_copy(t240[:].rearrange("p (s j) -> p s j", j=15), xes[:, :, 8:23]), [xes], [t240])
                        for i in range(2):
                            pj = nextpb()
                            c.mm(lambda e, i=i, pj=pj: e.transpose(PB[pj][0:120, 0:128], t240[:, i * 120:(i + 1) * 120], ident[:]), [t240, ident], [PB[pj]])
                            V(lambda e, i=i, gi=gi, pj=pj: e.tensor_copy(pno_s[0:120, i, gi * 128:(gi + 1) * 128], PB[pj][0:120, 0:128]), [PB[pj]], [pno_s])
                        for (c0, n) in TB:
                            pi = nextpb()
                            c.mm(lambda e, gi=gi, c0=c0, n=n, pi=pi: e.matmul(PB[pi][:, 0:n], lhsT=wpl[:, gi, :], rhs=mT[:, c0:c0 + n], start=True, stop=True), [wpl, mT], [PB[pi]])
                            V(lambda e, gi=gi, c0=c0, n=n, pi=pi: e.tensor_scalar(omC[:, gi, c0:c0 + n], PB[pi][:, 0:n], psc[:, gi:gi + 1], None, ALU.mult), [PB[pi], psc], [omC])
                        if gi == 0: stage(32)
                    c.dma("sp", o_pool_p, pno_p[:], reads=[pno_p], is_output=True)
                    for i in range(2):
                        c.dma("sp", o_pool_s.rearrange("s j d -> (s j) d")[i * 120:(i + 1) * 120, :], pno_s[:, i, :], reads=[pno_s], is_output=True)
                    c.barrier()

                with contextlib.ExitStack() as S5:
                    sb = lambda name, shape, dt=F32: c.sb(name, shape, dt, S5)
                    PI2 = 6.283185307179586
                    xdT = sb("xdT", [128, 4, TT], BF16)
                    with contextlib.ExitStack() as TMP:
                        Wd5 = c.sb("Wd5", [128, 8, 512], BF16, TMP)
                        load_w(Wd5, Wd5[:], wview(w_in_cd, 512, 512))
                        for cc in range(4):
                            for (c0, n) in TB:
                                pi = nextpb()
                                proj_fm(pi, Wd5, cc * 128, c0, n)
                                A(lambda e, cc=cc, c0=c0, n=n, pi=pi: e.copy(xdT[:, cc, c0:c0 + n], PB[pi][:, 0:n]), [PB[pi]], [xdT])
                        c.barrier()
                    flat = xnT.t.bitcast(F32).rearrange("p k f -> p (k f)")
                    EXr = Buf(flat[:, 0:2048].rearrange("p (m f) -> p m f", m=16), "EXr"); EXi = Buf(flat[:, 2048:4096].rearrange("p (m f) -> p m f", m=16), "EXi")
                    Er = sb("Er", [128, 16, 128]); Ei = sb("Ei", [128, 16, 128])
                    cv = lambda k: xnT.t[:, k, 0:2048].rearrange("p (m f) -> p m f", m=16)
                    CexpR = Buf(cv(4), "CexpR"); CexpIn = Buf(cv(5), "CexpIn"); BexpR = Buf(cv(6), "BexpR"); BexpI = Buf(cv(7), "BexpI")
                    SET = S5.enter_context(contextlib.ExitStack())
                    sbs = lambda name, shape, dt=F32: c.sb(name, shape, dt, SET)
                    diagD = sb("diagD", [128, 4, 128], BF16); Wgl = sb("Wgl", [128, 4, 512], BF16)
                    prm = sb("prm", [128, 24, 16])
                    prmi = sb("prmi", [128, 16], I32)
                    dsk = sb("dsk", [128, 4]); bgl = sb("bgl", [128, 4])
                    s0 = sb("s0", [128, 2, 16, 16])
                    sst = sb("sst", [128, 2, 16]); tiny = sb("tiny", [128, 4]); ee2 = sb("ee2", [128, 16, 2])
                    ldtb = sbs("ldtb", [128, 32]); l16 = sbs("l16", [16, 2, 128]); d4 = sbs("d4", [4, 2, 128])
                    Ball = sbs("Ball", [128, 2, 16, 16]); Bs = sbs("Bs", [128, 2, 16, 16]); Btmp = sbs("Btmp", [128, 2, 16, 16])
                    Cn = sbs("Cn", [128, 2, 64]); CX = sbs("CX", [128, 2, 128])
                    s0n = Buf(flat[0:16, 0:2048], "s0n")
                    P_ = lambda i: prm[:, i, :]
                    LRE, LIM, LDT, DTV, RHO, TH, Q, QF, R_, SN, CS, AB, LBR, LBI, AA, DEN, CR, CI, T1, T2 = range(20)
                    c.dma("sp", l16[:, 0, :], lam_re.rearrange("(m p) -> m p", p=128), writes=[l16])
                    c.dma("sp", l16[:, 1, :], lam_im.rearrange("(m p) -> m p", p=128), writes=[l16])
                    c.dma("sp", ldtb[:], log_dt.partition_broadcast(128), writes=[ldtb])
                    c.dma("sp", d4[:, 0, :], d_skip.rearrange("(c p) -> c p", p=128), writes=[d4])
                    c.dma("sp", d4[:, 1, :], b_glu.rearrange("(c p) -> c p", p=128), writes=[d4])
                    c.dma("sp", Ball[:, 0, :, :], b_re.rearrange("(m two) n c -> (two n) m c", two=2), writes=[Ball])
                    c.dma("sp", Ball[:, 1, :, :], b_im.rearrange("(m two) n c -> (two n) m c", two=2), writes=[Ball])
                    load_w(Wgl, Wgl[:], wview(w_glu, 0, 512, nk=4))
                    pi = nextpb()
                    c.mm(lambda e: e.transpose(PB[pi][:, 0:16], l16[0:16, 0, :], ident[0:16, 0:16]), [l16, ident], [PB[pi]])
                    c.mm(lambda e: e.transpose(PB[pi][:, 16:32], l16[0:16, 1, :], ident[0:16, 0:16]), [l16, ident], [PB[pi]])
                    c.mm(lambda e: e.transpose(PB[pi][:, 32:36], d4[0:4, 0, :], ident[0:4, 0:4]), [d4, ident], [PB[pi]])
                    c.mm(lambda e: e.transpose(PB[pi][:, 36:40], d4[0:4, 1, :], ident[0:4, 0:4]), [d4, ident], [PB[pi]])
                    V(lambda e: e.tensor_copy(prm[:, 0:2, :], PB[pi][:, 0:32].rearrange("p (a m) -> p a m", a=2)), [PB[pi]], [prm])
                    V(lambda e: e.tensor_copy(dsk[:], PB[pi][:, 32:36]), [PB[pi]], [dsk])
                    V(lambda e: e.tensor_copy(bgl[:], PB[pi][:, 36:40]), [PB[pi]], [bgl])
                    V(lambda e: e.tensor_copy(prm[0:64, LDT, :], ldtb[0:64, 0:32:2]), [ldtb], [prm])
                    V(lambda e: e.tensor_copy(prm[64:128, LDT, :], ldtb[64:128, 1:32:2]), [ldtb], [prm])
                    pp_ = [prm]
                    A(lambda e: e.activation(out=P_(DTV), in_=P_(LDT), func=AF.Exp), pp_, pp_)
                    V(lambda e: e.tensor_tensor(P_(RHO), P_(LRE), P_(DTV), ALU.mult), pp_, pp_)
                    A(lambda e: e.activation(out=P_(RHO), in_=P_(RHO), func=AF.Exp), pp_, pp_)
                    V(lambda e: e.tensor_tensor(P_(TH), P_(LIM), P_(DTV), ALU.mult), pp_, pp_)
                    V(lambda e: e.tensor_scalar(P_(R_), P_(TH), 0.125, None, ALU.mult), pp_, pp_)
                    V(lambda e: e.tensor_scalar(P_(R_), P_(R_), -3.14159, 3.14159, ALU.max, ALU.min), pp_, pp_)
                    A(lambda e: e.activation(out=P_(SN), in_=P_(R_), func=AF.Sin), pp_, pp_)
                    V(lambda e: e.tensor_scalar(P_(T1), P_(R_), -1.0, None, ALU.mult), pp_, pp_)
                    V(lambda e: e.tensor_tensor(P_(AB), P_(R_), P_(T1), ALU.max), pp_, pp_)
                    V(lambda e: e.tensor_scalar(P_(AB), P_(AB), -1.0, 1.5707963, ALU.mult, ALU.add), pp_, pp_)
                    A(lambda e: e.activation(out=P_(CS), in_=P_(AB), func=AF.Sin), pp_, pp_)
                    for _ in range(3):
                        V(lambda e: e.tensor_tensor(P_(T1), P_(CS), P_(CS), ALU.mult), pp_, pp_)
                        V(lambda e: e.tensor_tensor(P_(T2), P_(SN), P_(SN), ALU.mult), pp_, pp_)
                        V(lambda e: e.scalar_tensor_tensor(P_(SN), P_(CS), 2.0, P_(SN), ALU.mult, ALU.mult), pp_, pp_)
                        V(lambda e: e.tensor_tensor(P_(CS), P_(T1), P_(T2), ALU.subtract), pp_, pp_)
                    V(lambda e: e.tensor_tensor(P_(LBR), P_(RHO), P_(CS), ALU.mult), pp_, pp_)
                    V(lambda e: e.tensor_tensor(P_(LBI), P_(RHO), P_(SN), ALU.mult), pp_, pp_)
                    V(lambda e: e.tensor_scalar(P_(AA), P_(LBR), -1.0, None, ALU.add), pp_, pp_)
                    V(lambda e: e.tensor_tensor(P_(T1), P_(LRE), P_(LRE), ALU.mult), pp_, pp_)
                    V(lambda e: e.tensor_tensor(P_(T2), P_(LIM), P_(LIM), ALU.mult), pp_, pp_)
                    V(lambda e: e.tensor_tensor(P_(DEN), P_(T1), P_(T2), ALU.add), pp_, pp_)
                    V(lambda e: e.reciprocal(P_(DEN), P_(DEN)), pp_, pp_)
                    V(lambda e: e.tensor_tensor(P_(T1), P_(AA), P_(LRE), ALU.mult), pp_, pp_)
                    V(lambda e: e.tensor_tensor(P_(T2), P_(LBI), P_(LIM), ALU.mult), pp_, pp_)
                    V(lambda e: e.tensor_tensor(P_(CR), P_(T1), P_(T2), ALU.add), pp_, pp_)
                    V(lambda e: e.tensor_tensor(P_(CR), P_(CR), P_(DEN), ALU.mult), pp_, pp_)
                    V(lambda e: e.tensor_tensor(P_(T1), P_(LBI), P_(LRE), ALU.mult), pp_, pp_)
                    V(lambda e: e.tensor_tensor(P_(T2), P_(AA), P_(LIM), ALU.mult), pp_, pp_)
                    V(lambda e: e.tensor_tensor(P_(CI), P_(T1), P_(T2), ALU.subtract), pp_, pp_)
                    V(lambda e: e.tensor_tensor(P_(CI), P_(CI), P_(DEN), ALU.mult), pp_, pp_)
                    V(lambda e: e.tensor_copy(Er[:, :, 0], P_(CS)), pp_, [Er])
                    V(lambda e: e.tensor_copy(Ei[:, :, 0], P_(SN)), pp_, [Ei])
                    ta = Buf(flat[:, 0:1024].rearrange("p (m f) -> p m f", m=16), "ta")
                    tb = Buf(flat[:, 1024:2048].rearrange("p (m f) -> p m f", m=16), "tb")
                    L = 1
                    while L < 128:
                        bc = lambda T_: T_[:, :, L - 1:L].to_broadcast([128, 16, L])
                        V(lambda e, L=L: e.tensor_tensor(ta[:, :, 0:L], Er[:, :, 0:L], Er[:, :, L - 1:L].to_broadcast([128, 16, L]), ALU.mult), [Er], [ta])
                        V(lambda e, L=L: e.tensor_tensor(tb[:, :, 0:L], Ei[:, :, 0:L], Ei[:, :, L - 1:L].to_broadcast([128, 16, L]), ALU.mult), [Ei], [tb])
                        V(lambda e, L=L: e.tensor_tensor(Er[:, :, L:2 * L], ta[:, :, 0:L], tb[:, :, 0:L], ALU.subtract), [ta, tb], [Er])
                        V(lambda e, L=L: e.tensor_tensor(ta[:, :, 0:L], Er[:, :, 0:L], Ei[:, :, L - 1:L].to_broadcast([128, 16, L]), ALU.mult), [Er, Ei], [ta])
                        V(lambda e, L=L: e.tensor_tensor(tb[:, :, 0:L], Ei[:, :, 0:L], Er[:, :, L - 1:L].to_broadcast([128, 16, L]), ALU.mult), [Er, Ei], [tb])
                        V(lambda e, L=L: e.tensor_tensor(Ei[:, :, L:2 * L], ta[:, :, 0:L], tb[:, :, 0:L], ALU.add), [ta, tb], [Ei])
                        L *= 2
                    c.barrier()
                    crb = prm[:, CR, :].unsqueeze(2).to_broadcast([128, 16, 16]); cib = prm[:, CI, :].unsqueeze(2).to_broadcast([128, 16, 16])
                    V(lambda e: e.tensor_tensor(Bs[:, 0], Ball[:, 0], crb, ALU.mult), [Ball, prm], [Bs])
                    V(lambda e: e.tensor_tensor(Btmp[:, 0], Ball[:, 1], cib, ALU.mult), [Ball, prm], [Btmp])
                    V(lambda e: e.tensor_tensor(Bs[:, 0], Bs[:, 0], Btmp[:, 0], ALU.subtract), [Bs, Btmp], [Bs])
                    V(lambda e: e.tensor_tensor(Bs[:, 1], Ball[:, 1], crb, ALU.mult), [Ball, prm], [Bs])
                    V(lambda e: e.tensor_tensor(Btmp[:, 1], Ball[:, 0], cib, ALU.mult), [Ball, prm], [Btmp])
                    V(lambda e: e.tensor_tensor(Bs[:, 1], Bs[:, 1], Btmp[:, 1], ALU.add), [Bs, Btmp], [Bs])
                    V(lambda e: e.memset(EXr[:], 0.0), [], [EXr])
                    V(lambda e: e.memset(EXi[:], 0.0), [], [EXi])
                    for a_, EX in ((0, EXr), (1, EXi)):
                        for j in range(4):
                            V(lambda e, a_=a_, EX=EX, j=j: e.tensor_copy(EX[0:64, j:16:4, 32 * j:32 * j + 16], Bs[0:64, a_, j:16:4, :]), [Bs], [EX])
                            V(lambda e, a_=a_, EX=EX, j=j: e.tensor_copy(EX[64:128, j:16:4, 32 * j + 16:32 * j + 32], Bs[64:128, a_, j:16:4, :]), [Bs], [EX])
                    for a_, EX, BX in ((0, EXr, BexpR), (1, EXi, BexpI)):
                        for m4 in range(4):
                            pi = nextpb()
                            for j in range(4):
                                c.mm(lambda e, j=j, m4=m4, EX=EX: e.transpose(PB[pi][:, j * 128:(j + 1) * 128], EX[:, m4 * 4 + j, :], ident[:]), [EX, ident], [PB[pi]], inc=(j == 3))
                            V(lambda e, m4=m4, BX=BX, pi=pi: e.tensor_copy(BX[:, m4 * 4:m4 * 4 + 4, :], PB[pi][:, 0:512].rearrange("p (j f) -> p j f", j=4)), [PB[pi]], [BX])
                    c.barrier()
                    V(lambda e: e.memset(CexpR[:], 0.0), [], [CexpR])
                    V(lambda e: e.memset(CexpIn[:], 0.0), [], [CexpIn])
                    for cc in range(4):
                        c.dma("sp", Cn[:, 0, :], c_re.rearrange("g c n -> (g c) n")[cc * 128:(cc + 1) * 128, :], writes=[Cn])
                        c.dma("sp", Cn[:, 1, :], c_im.rearrange("g c n -> (g c) n")[cc * 128:(cc + 1) * 128, :], writes=[Cn])
                        for a_ in range(2):
                            V(lambda e, a_=a_: e.tensor_scalar(CX[:, a_, 0:64], Cn[:, a_, :], misc[:, 16:17], None, ALU.mult), [Cn, misc], [CX])
                            V(lambda e, a_=a_: e.tensor_scalar(CX[:, a_, 64:128], Cn[:, a_, :], misc[:, 17:18], None, ALU.mult), [Cn, misc], [CX])
                        pi = nextpb()
                        c.mm(lambda e: e.transpose(PB[pi][:, 0:128], CX[:, 0, :], ident[:]), [CX, ident], [PB[pi]], inc=False)
                        c.mm(lambda e: e.transpose(PB[pi][:, 128:256], CX[:, 1, :], ident[:]), [CX, ident], [PB[pi]])
                        for j in range(4):
                            V(lambda e, cc=cc, j=j, pi=pi: e.tensor_copy(CexpR[:, cc * 4 + j, 32 * j:32 * j + 32], PB[pi][:, 32 * j:32 * j + 32]), [PB[pi]], [CexpR])
                            V(lambda e, cc=cc, j=j, pi=pi: e.tensor_scalar(CexpIn[:, cc * 4 + j, 32 * j:32 * j + 32], PB[pi][:, 128 + 32 * j:128 + 32 * j + 32], -1.0, None, ALU.mult), [PB[pi]], [CexpIn])
                        V(lambda e, cc=cc: e.tensor_scalar(diagD[:, cc, :], ident[:], dsk[:, cc:cc + 1], None, ALU.mult), [ident, dsk], [diagD])
                    for a_ in range(2):
                        c.dma("sp", s0n[:, :], st_s5re if a_ == 0 else st_s5im, writes=[s0n])
                        for m4 in range(4):
                            pi = nextpb()
                            for j in range(4):
                                m = m4 * 4 + j
                                c.mm(lambda e, a_=a_, m=m, j=j: e.transpose(PB[pi][:, j * 16:(j + 1) * 16], s0n[0:16, m * 128:(m + 1) * 128], ident[0:16, 0:16]), [s0n, ident], [PB[pi]], inc=(j == 3))
                            V(lambda e, a_=a_, m4=m4, pi=pi: e.tensor_copy(s0[:, a_, m4 * 4:m4 * 4 + 4, :], PB[pi][:, 0:64].rearrange("p (j s) -> p j s", j=4)), [PB[pi]], [s0])
                    V(lambda e: e.memset(sst[:], 0.0), [], [sst])
                    V(lambda e: e.tensor_scalar(ee2[:, :, 0], Ei[:, :, 127], -1.0, None, ALU.mult), [Ei], [ee2])
                    V(lambda e: e.tensor_copy(ee2[:, :, 1], Ei[:, :, 127]), [Ei], [ee2])
                    c.barrier()
                    SET.close()
                    sfin = sb("sfin", [128, 2, 16, 16])
                    sbr = [sb("sbr%d" % i, [128, 512], BF16) for i in range(2)]; sbi = [sb("sbi%d" % i, [128, 512], BF16) for i in range(2)]
                    ygT = sb("ygT", [128, 4, 512], BF16); gate = sb("gate", [128, 512], BF16)
                    sop = sb("sop", [16, 2, 128])
                    WT = [Buf(flat[:, i * 512:(i + 1) * 512], "wt%d" % i) for i in range(8)]
                    WX = [Buf(xsb[0].t.bitcast(F32)[:, 0:512], "wx0"), Buf(xsb[1].t.bitcast(F32)[:, 0:512], "wx1")]
                    bank = [0]

                    def nb():
                        bank[0] = (bank[0] + 1) % 6
                        return bank[0]
                    it = 0
                    for bi, (c0, n) in enumerate(TB):
                        is_s = (c0 == 2048)
                        nt_ = n // 128
                        if is_s:
                            v3 = lambda ap: ap[:, 0:128].rearrange("p (s j) -> p s j", j=8)
                            eb = lambda T_, m: T_[:, m, 0:8].unsqueeze(1).to_broadcast([128, 16, 8])
                        else:
                            v3 = lambda ap: ap[:, 0:512].rearrange("p (t f) -> p t f", f=128)
                            eb = lambda T_, m: T_[:, m, :].unsqueeze(1).to_broadcast([128, 4, 128])
                        def emit_b(m_):
                            cc_ = m_ // 4
                            pr_ = nb()
                            c.mm(lambda e: e.matmul(PB[pr_][:, 0:n], lhsT=BexpR[:, m_, :], rhs=xdT[:, cc_, c0:c0 + n], start=True, stop=True), [BexpR, xdT], [PB[pr_]])
                            pq_ = nb()
                            c.mm(lambda e: e.matmul(PB[pq_][:, 0:n], lhsT=BexpI[:, m_, :], rhs=xdT[:, cc_, c0:c0 + n], start=True, stop=True), [BexpI, xdT], [PB[pq_]])
                            return pr_, pq_
                        bq = {0: emit_b(0)}
                        for cc in range(4):
                            pY = 6 + (cc % 2)
                            for j in range(4):
                                m = cc * 4 + j
                                it += 1
                                if m + 1 < 16:
                                    bq[m + 1] = emit_b(m + 1)
                                t1, t2, t3, q1, q2, q3 = WT[0], WT[1], WT[2], WX[0], WX[1], WT[3]
                                rbase = 3072 if it % 2 == 0 else 2048
                                rre, rim = (WT[6], WT[7]) if it % 2 == 0 else (WT[4], WT[5])
                                pr, pq = bq.pop(m)
                                V(lambda e, m=m: e.tensor_tensor(v3(t1), v3(PB[pr]), eb(Er, m), ALU.mult), [PB[pr], Er], [t1])
                                V(lambda e, m=m: e.tensor_tensor(v3(t2), v3(PB[pq]), eb(Ei, m), ALU.mult), [PB[pq], Ei], [t2])
                                V(lambda e: e.tensor_tensor(t1[:, 0:n], t1[:, 0:n], t2[:, 0:n], ALU.add), [t1, t2], [t1])
                                V(lambda e, m=m: e.tensor_tensor(v3(t2), v3(PB[pq]), eb(Er, m), ALU.mult), [PB[pq], Er], [t2])
                                V(lambda e, m=m: e.tensor_tensor(v3(t3), v3(PB[pr]), eb(Ei, m), ALU.mult), [PB[pr], Ei], [t3])
                                V(lambda e: e.tensor_tensor(t2[:, 0:n], t2[:, 0:n], t3[:, 0:n], ALU.subtract), [t2, t3], [t2])
                                rho_b = prm[:, RHO, m:m + 1]
                                if not is_s:
                                    for tt in range(nt_):
                                        sl = slice(tt * 128, (tt + 1) * 128)
                                        V(lambda e, sl=sl, m=m: e.tensor_tensor_scan(rre[:, sl], rho_b.to_broadcast([128, 128]), t1[:, sl], sst[:, 0, m:m + 1], ALU.mult, ALU.add), [prm, t1, sst], [rre])
                                        V(lambda e, sl=sl, m=m: e.tensor_tensor_scan(rim[:, sl], rho_b.to_broadcast([128, 128]), t2[:, sl], sst[:, 1, m:m + 1], ALU.mult, ALU.add), [prm, t2, sst], [rim])
                                        last = rbase + tt * 128 + 127
                                        a_fwd = flat[:, last:last + 513:512]
                                        a_rev = flat[:, last + 512:last - 1:-512]
                                        V(lambda e, a_rev=a_rev, m=m: e.tensor_tensor(tiny[:, 0:2], a_rev, ee2[:, m, :], ALU.mult), [rre, rim, ee2], [tiny])
                                        V(lambda e, a_fwd=a_fwd, m=m: e.scalar_tensor_tensor(sst[:, :, m], a_fwd, Er[:, m, 127:128], tiny[:, 0:2], ALU.mult, ALU.add), [rre, rim, Er, tiny], [sst])
                                else:
                                    for s_ in range(16):
                                        sl = slice(s_ * 8, s_ * 8 + 8)
                                        V(lambda e, sl=sl, m=m, s_=s_: e.tensor_tensor_scan(rre[:, sl], rho_b.to_broadcast([128, 8]), t1[:, sl], s0[:, 0, m, s_:s_ + 1], ALU.mult, ALU.add), [prm, t1, s0], [rre])
                                        V(lambda e, sl=sl, m=m, s_=s_: e.tensor_tensor_scan(rim[:, sl], rho_b.to_broadcast([128, 8]), t2[:, sl], s0[:, 1, m, s_:s_ + 1], ALU.mult, ALU.add), [prm, t2, s0], [rim])
                                G(lambda e, m=m: e.tensor_tensor(v3(q1), v3(rre), eb(Er, m), ALU.mult), [rre, Er], [q1])
                                G(lambda e, m=m: e.tensor_tensor(v3(q2), v3(rim), eb(Ei, m), ALU.mult), [rim, Ei], [q2])
                                br_, bi_ = sbr[it % 2], sbi[it % 2]
                                if is_s:
                                    G(lambda e: e.tensor_tensor(q3[:, 0:n], q1[:, 0:n], q2[:, 0:n], ALU.subtract), [q1, q2], [q3])
                                    G(lambda e, m=m: e.tensor_copy(sfin[:, 0, m, :], q3[:, 7:128:8]), [q3], [sfin])
                                G(lambda e, br_=br_: e.tensor_tensor(br_[:, 0:n], q1[:, 0:n], q2[:, 0:n], ALU.subtract), [q1, q2], [br_])
                                G(lambda e, m=m: e.tensor_tensor(v3(q2), v3(rim), eb(Er, m), ALU.mult), [rim, Er], [q2])
                                G(lambda e, m=m: e.tensor_tensor(v3(q3), v3(rre), eb(Ei, m), ALU.mult), [rre, Ei], [q3])
                                if is_s:
                                    G(lambda e: e.tensor_tensor(q1[:, 0:n], q2[:, 0:n], q3[:, 0:n], ALU.add), [q2, q3], [q1])
                                    G(lambda e, m=m: e.tensor_copy(sfin[:, 1, m, :], q1[:, 7:128:8]), [q1], [sfin])
                                G(lambda e, bi_=bi_: e.tensor_tensor(bi_[:, 0:n], q2[:, 0:n], q3[:, 0:n], ALU.add), [q2, q3], [bi_])
                                c.mm(lambda e, m=m, j=j, br_=br_: e.matmul(PB[pY][:, 0:n], lhsT=CexpR[:, m, :], rhs=br_[:, 0:n], start=(j == 0), stop=False), [CexpR, br_], [PB[pY]], inc=False)
                                c.mm(lambda e, m=m, bi_=bi_: e.matmul(PB[pY][:, 0:n], lhsT=CexpIn[:, m, :], rhs=bi_[:, 0:n], start=False, stop=False), [CexpIn, bi_], [PB[pY]], inc=True)
                            c.mm(lambda e, cc=cc: e.matmul(PB[pY][:, 0:n], lhsT=diagD[:, cc, :], rhs=xdT[:, cc, c0:c0 + n], start=False, stop=True), [diagD, xdT], [PB[pY]])
                            A(lambda e, cc=cc, pY=pY: e.activation(out=ygT[:, cc, 0:n], in_=PB[pY][:, 0:n], func=AF.Gelu_apprx_tanh), [PB[pY]], [ygT])
                        for oc in range(4):
                            pg = nb()
                            for cc in range(4):
                                c.mm(lambda e, cc=cc, oc=oc: e.matmul(PB[pg][:, 0:n], lhsT=Wgl[:, cc, oc * 128:(oc + 1) * 128], rhs=ygT[:, cc, 0:n], start=(cc == 0), stop=(cc == 3)),
                                     [Wgl, ygT], [PB[pg]], inc=(cc == 3))
                            A(lambda e, oc=oc, pg=pg: e.activation(out=gate[:, 0:n], in_=PB[pg][:, 0:n], func=AF.Sigmoid, bias=bgl[:, oc:oc + 1]), [PB[pg], bgl], [gate])
                            V(lambda e, oc=oc: e.tensor_tensor(omC[:, 4 + oc, c0:c0 + n], ygT[:, oc, 0:n], gate[:, 0:n], ALU.mult), [ygT, gate], [omC])
                    c.barrier()
                    so = Buf(flat[0:16, 0:4096].rearrange("p (a f) -> p a f", a=2), "so")
                    pi = nextpb()
                    c.mm(lambda e: e.transpose(PB[pi][0:16, 0:128], sst[:, 0, :], ident[:]), [sst, ident], [PB[pi]], inc=False)
                    c.mm(lambda e: e.transpose(PB[pi][0:16, 128:256], sst[:, 1, :], ident[:]), [sst, ident], [PB[pi]])
                    V(lambda e: e.tensor_copy(sop[:], PB[pi][0:16, 0:256].rearrange("p (a f) -> p a f", a=2)), [PB[pi]], [sop])
                    c.dma("sp", o_s5re_p.rearrange("(m p) -> m p", p=128), sop[:, 0, :], reads=[sop], is_output=True)
                    c.dma("sp", o_s5im_p.rearrange("(m p) -> m p", p=128), sop[:, 1, :], reads=[sop], is_output=True)
                    for a_ in range(2):
                        for m4 in range(4):
                            pi = nextpb()
                            for j in range(4):
                                c.mm(lambda e, a_=a_, m4=m4, j=j: e.transpose(PB[pi][0:16, j * 128:(j + 1) * 128], sfin[:, a_, m4 * 4 + j, :], ident[:]), [sfin, ident], [PB[pi]], inc=(j == 3))
                            V(lambda e, a_=a_, m4=m4, pi=pi: e.tensor_copy(so[:, a_, m4 * 512:(m4 + 1) * 512], PB[pi][0:16, 0:512]), [PB[pi]], [so])
                    c.dma("sp", o_s5re_s, so[:, 0, :], reads=[so], is_output=True)
                    c.dma("sp", o_s5im_s, so[:, 1, :], reads=[so], is_output=True)
                    c.barrier()
                stage(33)
                tap("omC", omC, omC[:, :, :].rearrange("p g t -> p (g t)"))
                with contextlib.ExitStack() as WO:
                    Wo = c.sb("Wo1", [128, 8, 1024], BF16, WO)
                    load_w(Wo, Wo[:], wview(w_out_cd, 0, 1024))
                    for t in range(NT):
                        for cb in range(2):
                            pi = nextpb()
                            for k in range(8):
                                c.mm(lambda e, k=k: e.matmul(PB[pi][:, 0:512], lhsT=omC[:, k, t * 128:(t + 1) * 128], rhs=Wo[:, k, cb * 512:(cb + 1) * 512],
                                                             start=(k == 0), stop=(k == 7)), [omC, Wo], [PB[pi]], inc=(k == 7))
                            V(lambda e, t=t, cb=cb, pi=pi: e.tensor_tensor(h[:, t, cb * 512:(cb + 1) * 512], h[:, t, cb * 512:(cb + 1) * 512], PB[pi][:, 0:512], ALU.add), [h, PB[pi]], [h])
                    c.barrier()
            tap("hC", h, h[:, :, :])
            stage(34)
            ffn(1, 4)
            stage(35)
            ple(1, 5)
            stage(36)
            with contextlib.ExitStack() as FN:
                gfb = c.sb("gfb", [128, D], F32, FN)
                yo = [c.sb("yo%d" % i, [128, D], F32, FN) for i in range(2)]
                c.dma("sp", gfb[:], norm_final.partition_broadcast(128), writes=[gfb])
                for t in range(NT):
                    A(lambda e, t=t: e.activation(out=junk[:], in_=h[:, t, :], func=AF.Square, accum_out=ss[:, t:t + 1]), [h], [junk, ss])
                V(lambda e: e.tensor_scalar(rstd[:], ss[:], 1.0 / D, EPS, ALU.mult, ALU.add), [ss], [rstd])
                A(lambda e: e.activation(out=rstd[:], in_=rstd[:], func=AF.Sqrt), [rstd], [rstd])
                V(lambda e: e.reciprocal(rstd[:], rstd[:]), [rstd], [rstd])
                for t in range(NT):
                    y_ = yo[t % 2]
                    V(lambda e, t=t, y_=y_: e.scalar_tensor_tensor(y_[:], h[:, t, :], rstd[:, t:t + 1], gfb[:], ALU.mult, ALU.mult), [h, rstd, gfb], [y_])
                    dst = y_p[t * 128:(t + 1) * 128, :] if t < 16 else y_s
                    c.dma("sp", dst, y_[:], reads=[y_], is_output=True)

      except _Stop:
        pass
      c.dead = False
      c.finish()
    return nc


def _consts():
    i = np.arange(128)
    blk = (i[:, None] // 8) == (i[None, :] // 8)
    m = np.zeros((10, 128, 128), np.float32)
    le = i[:, None] <= i[None, :]
    gt = i[:, None] > i[None, :]
    m[0] = le; m[1] = le & blk
    m[2] = gt; m[3] = gt & blk
    m[4] = le; m[5] = le & blk
    m[6] = gt; m[7] = gt & blk
    m[8, :8, :] = (i[None, :] % 8 == np.arange(8)[:, None])
    sel = (i[:, None] // 8 == np.arange(16)[None, :]).astype(np.float32)
    misc = np.zeros((128, 32), np.float32)
    misc[:, 0:15] = 1.0 / (np.arange(15)[None, :] + 1.0)
    misc[:, 16] = ((i // 16) % 2 == 0)
    misc[:, 17] = ((i // 16) % 2 == 1)
    return np.eye(128, dtype=np.float32), m, sel, misc


def make_in_maps(inp):
    f = lambda a: np.ascontiguousarray(np.asarray(a, dtype=np.float32))
    ident, masks, sel, misc = _consts()
    shared = {
        "norm_mix": f(inp["norm_mix"]), "norm_ffn": f(inp["norm_ffn"]), "norm_pe": f(inp["norm_pe"]), "norm_final": f(inp["norm_final"]),
        "w_in_ab": f(inp["w_in_ab"][0]), "conv_qkv": f(inp["conv_qkv"][0]), "a_log": f(inp["a_log"][0]), "dt_bias": f(inp["dt_bias"][0]),
        "norm_o": f(inp["norm_o"][0]), "ln_g": f(inp["ln_v_gain"][0]), "ln_b": f(inp["ln_v_bias"][0]), "w_sp": f(inp["w_spatial"][0]),
        "b_sp": f(inp["b_spatial"][0]), "w_out_ab": f(inp["w_out_ab"][0]), "w_in_cd": f(inp["w_in_cd"][0]), "w_pool": f(inp["w_pool"][0]),
        "pool_scale": f(inp["pool_scale"][0]), "lam_re": f(inp["lam_re"][0]).reshape(2048), "lam_im": f(inp["lam_im"][0]).reshape(2048),
        "log_dt": f(inp["log_dt"][0]), "b_re": f(inp["b_re"][0]), "b_im": f(inp["b_im"][0]), "c_re": f(inp["c_re"][0]), "c_im": f(inp["c_im"][0]),
        "d_skip": f(inp["d_skip"][0]), "w_glu": f(inp["w_glu"][0]), "b_glu": f(inp["b_glu"][0]), "w_out_cd": f(inp["w_out_cd"][0]),
        "w_up": f(inp["w_ffn_up"]), "w_down": f(inp["w_ffn_down"]), "w_pe_proj": f(inp["w_pe_proj"]), "w_pe_gate": f(inp["w_pe_gate"]),
        "c_ident": ident, "c_masks": masks, "c_selcol": sel, "c_misc": misc,
    }
    maps = []
    for ci in range(NCORES):
        sl = slice(16 * ci, 16 * ci + 16)
        m = dict(shared)
        m["xp"] = f(inp["x_prompt"][ci]); m["xs"] = f(inp["x_sample"][sl]).reshape(128, D)
        m["pp"] = f(inp["p_prompt"][:, ci]); m["psm"] = f(inp["p_sample"][:, sl]).reshape(2, 128, 256)
        m["st_conv"] = f(inp["state_conv"][0, sl]).reshape(48, 1536); m["st_delta"] = f(inp["state_delta"][0, sl])
        m["st_pool"] = f(inp["state_pool"][0, sl]); m["st_s5re"] = f(inp["state_s5_re"][0, sl]).reshape(16, 2048)
        m["st_s5im"] = f(inp["state_s5_im"][0, sl]).reshape(16, 2048)
        maps.append(m)
    return maps


_NC_CACHE = {}


def kernel(**inputs):
    if "nc" not in _NC_CACHE:
        _NC_CACHE["nc"] = build_nc()
    nc = _NC_CACHE["nc"]
    maps = make_in_maps(inputs)
    res = run_bass_kernel_spmd(nc, maps, core_ids=list(range(NCORES)))
    R = res.results
    cat = lambda k, shp: np.concatenate([np.asarray(r[k], np.float32).reshape(shp) for r in R], axis=0)
    y_prompt = cat("y_p", (1, 2048, D)); y_sample = cat("y_s", (16, 8, D))
    conv_p = cat("o_conv_p", (1, 3, 1536))[None]; delta_p = cat("o_delta_p", (1, 4, 128, 128))[None]
    sguv_p = cat("o_sguv_p", (1, 128, 512))[None]; pool_p = cat("o_pool_p", (1, 15, 512))[None]
    s5re_p = cat("o_s5re_p", (1, 32, 64))[None]; s5im_p = cat("o_s5im_p", (1, 32, 64))[None]
    conv_s = cat("o_conv_s", (16, 3, 1536))[None]; delta_s = cat("o_delta_s", (16, 4, 128, 128))[None]
    sguv_s = cat("o_sguv_s", (16, 8, 512))[None]; pool_s = cat("o_pool_s", (16, 15, 512))[None]
    s5re_s = cat("o_s5re_s", (16, 32, 64))[None]; s5im_s = cat("o_s5im_s", (16, 32, 64))[None]
    return (y_prompt, y_sample, conv_p, delta_p, sguv_p, pool_p, s5re_p, s5im_p,
            conv_s, delta_s, sguv_s, pool_s, s5re_s, s5im_s)
```

```python
import contextlib
import numpy as np
import concourse.bass as bass
import concourse.mybir as mybir
from concourse.bass_utils import run_bass_kernel_spmd

F32 = mybir.dt.float32
BF16 = mybir.dt.bfloat16
I32 = mybir.dt.int32
ALU = mybir.AluOpType
AF = mybir.ActivationFunctionType

NCORES = 8
D = 1024
NT = 17
TT = 2176
EPS = 1e-6
DEBUG = {}


class Buf:
    def __init__(self, t, name="", excl=False):
        self.t = t
        self.name = name
        self.w = None
        self.r = {}
        self.excl = excl

    def __getitem__(self, k):
        return self.t[k]


class Ctx:
    NDMA = 40

    def __init__(self, nc, es):
        self.nc = nc
        self.es = es
        self.eng = {"pe": nc.tensor, "act": nc.scalar, "dve": nc.vector, "pool": nc.gpsimd, "sp": nc.sync}
        self.sems = {}
        self.cnt = {}
        for e in self.eng:
            self.sems[e] = es.enter_context(nc.semaphore("s_" + e))
            self.cnt[e] = 0
        self.dsem = [es.enter_context(nc.semaphore("d%d" % i)) for i in range(self.NDMA)]
        for i, s in enumerate(self.dsem):
            self.sems[("d", i)] = s
        self.dval = [0] * self.NDMA
        self.drr = 0
        self.drr_sw = 0
        self.seen = {e: {} for e in self.eng}
        self.pend = []
        self.out_events = []
        self.uid = 0
        self.dead = False

    def sb(self, name, shape, dtype=F32, es=None):
        self.uid += 1
        t = (es or self.es).enter_context(self.nc.sbuf_tensor("%s_%d" % (name, self.uid), list(shape), dtype))
        return Buf(t, name)

    def _wait(self, e, key, val):
        if val <= 0 or self.seen[e].get(key, 0) >= val:
            return
        self.eng[e].wait_ge(self.sems[key], val)
        self.seen[e][key] = val

    def _deps(self, e, reads, writes):
        for b in reads:
            if b.excl:
                continue
            if b.w is not None:
                self._wait(e, *b.w)
        for b in list(writes) + [b for b in reads if b.excl]:
            if b.w is not None and not (b.excl and e == "pe" and b.w[0] == "pe"):
                self._wait(e, *b.w)
            for k, v in b.r.items():
                self._wait(e, k, v)

    def _commit(self, ev, reads, writes):
        k, v = ev
        for b in reads:
            if b.excl:
                b.w = ev
                b.r = {}
            elif b.r.get(k, 0) < v:
                b.r[k] = v
        for b in writes:
            b.w = ev
            b.r = {}

    def op(self, e, fn, reads=(), writes=()):
        if self.dead:
            return
        self._deps(e, reads, writes)
        ins = fn(self.eng[e])
        self.cnt[e] += 1
        ins.then_inc(self.sems[e], 1)
        self._commit((e, self.cnt[e]), reads, writes)

    def mm(self, fn, reads=(), writes=(), inc=True):
        if self.dead:
            return
        e = "pe"
        self._deps(e, reads, writes)
        ins = fn(self.eng[e])
        self.pend.append((tuple(reads), tuple(writes)))
        if inc:
            self.cnt[e] += 1
            ins.then_inc(self.sems[e], 1)
            ev = (e, self.cnt[e])
            for r, w in self.pend:
                self._commit(ev, r, w)
            self.pend = []

    def dma(self, q, out_ap, in_ap, reads=(), writes=(), is_output=False, **kw):
        if self.dead:
            return
        half = self.NDMA // 2
        if q == "pool":
            i = half + self.drr_sw
            self.drr_sw = (self.drr_sw + 1) % (self.NDMA - half)
        else:
            i = self.drr
            self.drr = (self.drr + 1) % half
        key = ("d", i)
        self._wait(q, key, self.dval[i])
        self._deps(q, reads, writes)
        ins = self.eng[q].dma_start(out=out_ap, in_=in_ap, **kw)
        self.dval[i] += 16
        ins.then_inc(self.dsem[i], 16)
        ev = (key, self.dval[i])
        self._commit(ev, reads, writes)
        if is_output:
            self.out_events.append(ev)

    def barrier(self):
        if self.dead:
            return
        assert not self.pend
        for e in self.eng:
            for e2 in ("pe", "act", "dve", "pool"):
                self._wait(e, e2, self.cnt[e2])
            for i in range(self.NDMA):
                self._wait(e, ("d", i), self.dval[i])

    def finish(self):
        for k, v in self.out_events:
            self._wait("sp", k, v)
        for e in ("pe", "act", "dve", "pool"):
            self._wait("sp", e, self.cnt[e])


class _Stop(Exception):
    pass


def build_nc(dbg=None, stop=None):
    dbg = dbg or {}

    cref = []

    def stage(n):
        if stop == n:
            cref[0].dead = True
    nc = bass.Bass("TRN2", target_bir_lowering=False)

    def din(name, shape):
        return nc.dram_tensor(name, list(shape), F32, kind="ExternalInput").ap()

    def dout(name, shape):
        return nc.dram_tensor(name, list(shape), F32, kind="ExternalOutput").ap()

    xp = din("xp", [2048, D]); xs = din("xs", [128, D])
    pp = din("pp", [2, 2048, 256]); psm = din("psm", [2, 128, 256])
    st_conv = din("st_conv", [48, 1536]); st_delta = din("st_delta", [16, 4, 128, 128])
    st_pool = din("st_pool", [16, 15, 512]); st_s5re = din("st_s5re", [16, 2048]); st_s5im = din("st_s5im", [16, 2048])
    norm_mix = din("norm_mix", [2, D]); norm_ffn = din("norm_ffn", [2, D]); norm_pe = din("norm_pe", [2, D])
    norm_final = din("norm_final", [D])
    w_in_ab = din("w_in_ab", [D, 3080]); conv_qkv = din("conv_qkv", [4, 1536])
    a_log = din("a_log", [4]); dt_bias = din("dt_bias", [4]); norm_o = din("norm_o", [128])
    ln_g = din("ln_g", [512]); ln_b = din("ln_b", [512]); w_sp = din("w_sp", [4, 128, 128]); b_sp = din("b_sp", [4, 128])
    w_out_ab = din("w_out_ab", [D, D]); w_in_cd = din("w_in_cd", [D, D]); w_pool = din("w_pool", [4, 128, 128])
    pool_scale = din("pool_scale", [512])
    lam_re = din("lam_re", [2048]); lam_im = din("lam_im", [2048]); log_dt = din("log_dt", [32])
    b_re = din("b_re", [32, 64, 16]); b_im = din("b_im", [32, 64, 16]); c_re = din("c_re", [32, 16, 64]); c_im = din("c_im", [32, 16, 64])
    d_skip = din("d_skip", [512]); w_glu = din("w_glu", [512, 512]); b_glu = din("b_glu", [512]); w_out_cd = din("w_out_cd", [D, D])
    w_up = din("w_up", [2, D, 4096]); w_down = din("w_down", [2, 4096, D])
    w_pe_proj = din("w_pe_proj", [2, 256, D]); w_pe_gate = din("w_pe_gate", [2, D, D])
    c_ident = din("c_ident", [128, 128]); c_masks = din("c_masks", [10, 128, 128]); c_selcol = din("c_selcol", [128, 16]); c_misc = din("c_misc", [128, 32])

    y_p = dout("y_p", [2048, D]); y_s = dout("y_s", [128, D])
    o_conv_p = dout("o_conv_p", [3, 1536]); o_delta_p = dout("o_delta_p", [4, 128, 128]); o_sguv_p = dout("o_sguv_p", [128, 512])
    o_pool_p = dout("o_pool_p", [15, 512]); o_s5re_p = dout("o_s5re_p", [2048]); o_s5im_p = dout("o_s5im_p", [2048])
    o_conv_s = dout("o_conv_s", [48, 1536]); o_delta_s = dout("o_delta_s", [16, 4, 128, 128]); o_sguv_s = dout("o_sguv_s", [128, 512])
    o_pool_s = dout("o_pool_s", [16, 15, 512]); o_s5re_s = dout("o_s5re_s", [16, 2048]); o_s5im_s = dout("o_s5im_s", [16, 2048])
    dbg_out = {k: nc.dram_tensor("dbg_" + k, list(shp[0]), BF16 if shp[1] == "bf16" else F32, kind="ExternalOutput").ap() for k, shp in dbg.items()}

    with contextlib.ExitStack() as es:
      c = Ctx(nc, es)
      cref.append(c)
      try:
            V = lambda fn, r, w: c.op("dve", fn, r, w)
            A = lambda fn, r, w: c.op("act", fn, r, w)
            G = lambda fn, r, w: c.op("pool", fn, r, w)

            h = c.sb("h", [128, NT, D])
            xnT = c.sb("xnT", [128, 8, TT], BF16)
            ident = c.sb("ident", [128, 128]); identb = c.sb("identb", [128, 128], BF16)
            masks = c.sb("masks", [128, 10, 128])
            selcol = c.sb("selcol", [128, 16]); misc = c.sb("misc", [128, 32])
            onesb = c.sb("onesb", [128, 128], BF16); onesf = c.sb("onesf", [128, 128]); zerof = c.sb("zerof", [128, 128])
            gains = c.sb("gains", [128, 7, 8])
            junk = c.sb("junk", [128, D], BF16)
            ss = c.sb("ss", [128, NT]); rstd = c.sb("rstd", [128, NT])
            xsb = [c.sb("xsb%d" % i, [128, D], BF16) for i in range(2)]
            psum_t = es.enter_context(nc.psum_tensor("psum", [128, 8, 512], F32))
            PB = [Buf(psum_t[:, i, :], "pb%d" % i, excl=True) for i in range(8)]
            PBb = [psum_t.bitcast(BF16)[:, i, :] for i in range(8)]
            pbi = [0]

            def nextpb():
                i = pbi[0]
                pbi[0] = (i + 1) % 8
                return i

            M_TRIU, M_TRIU_S, M_STRICTL, M_STRICTL_S, M_INCLU, M_INCLU_S, M_TAIL, M_TAIL_S, M_SPS = range(9)

            c.dma("sp", h[:, 0:16, :], xp.rearrange("(n p) d -> p n d", p=128), writes=[h])
            c.dma("sp", h[:, 16, :], xs, writes=[h])
            c.dma("sp", ident[:], c_ident, writes=[ident])
            c.dma("sp", masks[:], c_masks.rearrange("m p f -> p m f"), writes=[masks])
            c.dma("sp", selcol[:], c_selcol, writes=[selcol])
            c.dma("sp", misc[:], c_misc, writes=[misc])
            gsrc = [norm_mix[0], norm_ffn[0], norm_pe[0], norm_mix[1], norm_ffn[1], norm_pe[1], norm_final]
            for i, g in enumerate(gsrc):
                c.dma("sp", gains[:, i, :], g.rearrange("(k p) -> p k", p=128), writes=[gains], allow_slow_non_contiguous=True)
            V(lambda e: e.tensor_copy(identb[:], ident[:]), [ident], [identb])
            G(lambda e: e.memset(onesb[:], 1.0), [], [onesb])
            G(lambda e: e.memset(onesf[:], 1.0), [], [onesf])
            G(lambda e: e.memset(zerof[:], 0.0), [], [zerof])
            stage(-1)

            def tap(name, buf, ap):
                if name in dbg_out:
                    c.dma("sp", dbg_out[name], ap, reads=[buf], is_output=True, allow_slow_non_contiguous=True)

            def load_w(dst_buf, dst_ap, src_ap):
                c.dma("pool", dst_ap, src_ap, writes=[dst_buf])

            def wview(w2d, c0, ncols, k0=0, nk=8):
                return w2d[k0 * 128:(k0 + nk) * 128, c0:c0 + ncols].rearrange("(k p) n -> p k n", p=128)

            def rmsnorm_T(gi):
                for t in range(NT):
                    A(lambda e, t=t: e.activation(out=junk[:], in_=h[:, t, :], func=AF.Square, accum_out=ss[:, t:t + 1]), [h], [junk, ss])
                V(lambda e: e.tensor_scalar(rstd[:], ss[:], 1.0 / D, EPS, ALU.mult, ALU.add), [ss], [rstd])
                A(lambda e: e.activation(out=rstd[:], in_=rstd[:], func=AF.Sqrt), [rstd], [rstd])
                V(lambda e: e.reciprocal(rstd[:], rstd[:]), [rstd], [rstd])
                for t in range(NT):
                    xb = xsb[t % 2]
                    A(lambda e, t=t, xb=xb: e.activation(out=xb[:], in_=h[:, t, :], func=AF.Copy, scale=rstd[:, t:t + 1]), [h, rstd], [xb])
                    pi = nextpb()
                    for k in range(8):
                        c.mm(lambda e, k=k, xb=xb, pi=pi: e.transpose(PBb[pi][:, k * 128:(k + 1) * 128], xb[:, k * 128:(k + 1) * 128], identb[:]),
                             [xb, identb], [PB[pi]], inc=(k == 7))
                    V(lambda e, t=t, pi=pi: e.tensor_tensor(
                        xnT[:, :, t * 128:(t + 1) * 128], PBb[pi].rearrange("p (k f) -> p k f", k=8),
                        gains[:, gi, :].unsqueeze(2).to_broadcast([128, 8, 128]), ALU.mult), [PB[pi], gains], [xnT])

            TB = [(0, 512), (512, 512), (1024, 512), (1536, 512), (2048, 128)]

            def proj_fm(pi, W, wcol, c0, n):
                for k in range(8):
                    c.mm(lambda e, k=k: e.matmul(PB[pi][:, 0:n], lhsT=W[:, k, wcol:wcol + 128], rhs=xnT[:, k, c0:c0 + n],
                                                 start=(k == 0), stop=(k == 7)), [W, xnT], [PB[pi]], inc=(k == 7))

            def proj_tm(pi, W, wcol, ncols, t, src=None, nk=8):
                src = src or xnT
                for k in range(nk):
                    c.mm(lambda e, k=k: e.matmul(PB[pi][:, 0:ncols], lhsT=src[:, k, t * 128:(t + 1) * 128], rhs=W[:, k, wcol:wcol + ncols],
                                                 start=(k == 0), stop=(k == nk - 1)), [W, src], [PB[pi]], inc=(k == nk - 1))

            rmsnorm_T(0)
            tap("xnT", xnT, xnT[:, 0, :])
            stage(1)

            with contextlib.ExitStack() as L0:
                omT = c.sb("omT", [128, 4, TT], BF16, L0)
                with contextlib.ExitStack() as GD:
                    sb = lambda name, shape, dt=F32: c.sb(name, shape, dt, GD)
                    Wh = [sb("Wh0", [128, 8, 512], BF16)] * 2
                    Wba = sb("Wba", [128, 8, 8], BF16)
                    cw = sb("cw", [128, 12, 4])
                    cst = sb("cst", [128, 12, 48])
                    cnew_s = sb("cnew_s", [128, 12, 48]); cnew_p = sb("cnew_p", [128, 12, 3])
                    alb = sb("alb", [128, 4]); dtb = sb("dtb", [128, 4]); nob = sb("nob", [128, 1])
                    bg = sb("bg", [128, NT, 8])
                    beta = sb("beta", [128, NT, 4]); nbeta = sb("nbeta", [128, NT, 4]); gg = sb("gg", [128, NT, 4])
                    gc = sb("gc", [128, NT, 4]); egc = sb("egc", [128, NT, 4]); bexp = sb("bexp", [128, NT, 4]); etail = sb("etail", [128, NT, 4])
                    HD = GD.enter_context(contextlib.ExitStack())
                    sb = lambda name, shape, dt=F32: c.sb(name, shape, dt, HD)
                    qnT = sb("qnT", [128, TT], BF16); knT = sb("knT", [128, TT], BF16); gT = sb("gT", [128, TT], BF16)
                    ktok = sb("ktok", [128, NT, 128], BF16); vtok = sb("vtok", [128, NT, 128], BF16)
                    S = sb("S", [128, 128]); Sb = sb("Sb", [128, 128], BF16)
                    S0s = sb("S0s", [128, 16, 128]); Sbs = sb("Sbs", [128, 16, 128], BF16); Sout = S0s

                    c.dma("sp", alb[:], a_log.partition_broadcast(128), writes=[alb])
                    c.dma("sp", dtb[:], dt_bias.partition_broadcast(128), writes=[dtb])
                    c.dma("sp", nob[:], norm_o.rearrange("(p o) -> p o", o=1), writes=[nob])
                    load_w(Wba, Wba[:], wview(w_in_ab, 2048, 8))
                    with contextlib.ExitStack() as TMP:
                        cw4 = c.sb("cw4", [4, 1536], F32, TMP)
                        stc = c.sb("stc", [48, 1536], F32, TMP)
                        c.dma("sp", cw4[:], conv_qkv, writes=[cw4])
                        c.dma("sp", stc[:], st_conv, writes=[stc])
                        for ch in range(12):
                            pi = nextpb()
                            c.mm(lambda e, ch=ch, pi=pi: e.transpose(PB[pi][:, 0:4], cw4[0:4, ch * 128:(ch + 1) * 128], ident[0:4, 0:4]), [cw4, ident], [PB[pi]])
                            c.mm(lambda e, ch=ch, pi=pi: e.transpose(PB[pi][:, 64:112], stc[0:48, ch * 128:(ch + 1) * 128], ident[0:48, 0:48]), [stc, ident], [PB[pi]])
                            V(lambda e, ch=ch, pi=pi: e.tensor_copy(cw[:, ch, :], PB[pi][:, 0:4]), [PB[pi]], [cw])
                            V(lambda e, ch=ch, pi=pi: e.tensor_copy(cst[:, ch, :], PB[pi][:, 64:112]), [PB[pi]], [cst])
                        c.barrier()

                    pi = nextpb()
                    for t in range(NT):
                        for k in range(8):
                            c.mm(lambda e, k=k, t=t: e.matmul(PB[pi][:, t * 8:(t + 1) * 8], lhsT=xnT[:, k, t * 128:(t + 1) * 128], rhs=Wba[:, k, :],
                                                              start=(k == 0), stop=(k == 7)), [Wba, xnT], [PB[pi]], inc=(k == 7))
                    V(lambda e: e.tensor_copy(bg[:], PB[pi][:, 0:NT * 8].rearrange("p (t j) -> p t j", j=8)), [PB[pi]], [bg])
                    A(lambda e: e.activation(out=beta[:], in_=bg[:, :, 0:4], func=AF.Sigmoid), [bg], [beta])
                    V(lambda e: e.tensor_scalar(nbeta[:], beta[:], -1.0, None, ALU.mult), [beta], [nbeta])
                    V(lambda e: e.tensor_tensor(gg[:], bg[:, :, 4:8], dtb[:].unsqueeze(1).to_broadcast([128, NT, 4]), ALU.add), [bg, dtb], [gg])
                    A(lambda e: e.activation(out=gg[:], in_=gg[:], func=AF.Exp), [gg], [gg])
                    A(lambda e: e.activation(out=gg[:], in_=gg[:], func=AF.Ln, bias=1.0), [gg], [gg])
                    A(lambda e: e.activation(out=alb[:], in_=alb[:], func=AF.Exp), [alb], [alb])
                    V(lambda e: e.scalar_tensor_tensor(gg[:], gg[:], -1.0, alb[:].unsqueeze(1).to_broadcast([128, NT, 4]), ALU.mult, ALU.mult), [gg, alb], [gg])
                    pi = nextpb(); pj = nextpb()
                    for t in range(NT):
                        mtri = M_TRIU if t < 16 else M_TRIU_S
                        mtail = M_TAIL if t < 16 else M_TAIL_S
                        c.mm(lambda e, t=t, mtri=mtri: e.matmul(PB[pi][:, t * 4:(t + 1) * 4], lhsT=masks[:, mtri, :], rhs=gg[:, t, :], start=True, stop=True), [masks, gg], [PB[pi]])
                        c.mm(lambda e, t=t, mtail=mtail: e.matmul(PB[pj][:, t * 4:(t + 1) * 4], lhsT=masks[:, mtail, :], rhs=gg[:, t, :], start=True, stop=True), [masks, gg], [PB[pj]])
                    V(lambda e: e.tensor_copy(gc[:], PB[pi][:, 0:NT * 4].rearrange("p (t j) -> p t j", j=4)), [PB[pi]], [gc])
                    A(lambda e: e.activation(out=egc[:], in_=gc[:], func=AF.Exp), [gc], [egc])
                    A(lambda e: e.activation(out=etail[:], in_=PB[pj][:, 0:NT * 4].rearrange("p (t j) -> p t j", j=4), func=AF.Exp), [PB[pj]], [etail])
                    V(lambda e: e.tensor_tensor(bexp[:], beta[:], egc[:], ALU.mult), [beta, egc], [bexp])
                    tap("gc", gc, gc[:, :, 0])
                    tap("beta", beta, beta[:, :, 0])
                    stage(2)


                    def gdn_tile(hd, t, T_, sid):
                        b0, b1 = 2 * sid, 2 * sid + 1
                        B0, B1 = PB[b0], PB[b1]
                        is_s = (t == 16)
                        ts = slice(t * 128, (t + 1) * 128)
                        mtri = M_TRIU_S if is_s else M_TRIU
                        mstr = M_STRICTL_S if is_s else M_STRICTL
                        minc = M_INCLU_S if is_s else M_INCLU
                        gcol = gc[:, t, hd:hd + 1]
                        XX, PTb = T_["XX"], T_["PT"]
                        gTri, nd, nd2, eA2 = T_["gTri"], T_["nd"], T_["nd2"], T_["eA2"]
                        decm, decTm = nd, nd2
                        TTb, Vb, Kb, ktl, qh, wdT, qkTm, ub, u = T_["TTb"], T_["Vb"], T_["Kb"], T_["ktl"], T_["qh"], T_["wdT"], T_["qkTm"], T_["ub"], T_["u"]
                        osq, rr, on = T_["osq"], gTri, ub

                        def out_norm(src_ap, src_buf):
                            A(lambda e: e.activation(out=osq[:], in_=src_ap, func=AF.Square), [src_buf], [osq])
                            c.mm(lambda e: e.matmul(B0[:, 256:384], lhsT=onesb[:], rhs=osq[:], start=True, stop=True), [onesb, osq], [B0])
                            A(lambda e: e.activation(out=rr[:], in_=B0[:, 256:384], func=AF.Ln, scale=1.0 / 128, bias=EPS), [B0], [rr])
                            A(lambda e: e.activation(out=rr[:], in_=rr[:], func=AF.Exp, scale=-0.5), [rr], [rr])
                            V(lambda e: e.tensor_tensor(on[:], src_ap, rr[:], ALU.mult), [src_buf, rr], [on])
                            V(lambda e: e.scalar_tensor_tensor(omT[:, hd, ts], on[:], nob[:, 0:1], gT[:, ts], ALU.mult, ALU.mult), [on, nob, gT], [omT])

                        V(lambda e: e.tensor_scalar(gTri[:], masks[:, mtri, :], gg[:, t, hd:hd + 1], None, ALU.mult), [masks, gg], [gTri])
                        c.mm(lambda e: e.matmul(B0[:, 0:128], lhsT=onesf[:], rhs=gTri[:], start=True, stop=True), [onesf, gTri], [B0])
                        c.mm(lambda e: e.matmul(B0[:, 128:256], lhsT=knT[:, ts], rhs=knT[:, ts], start=True, stop=True), [knT], [B0])
                        c.mm(lambda e: e.matmul(B0[:, 256:384], lhsT=knT[:, ts], rhs=qnT[:, ts], start=True, stop=True), [knT, qnT], [B0])
                        yield
                        V(lambda e: e.scalar_tensor_tensor(nd[:], B0[:, 0:128], gcol, zerof[:], ALU.subtract, ALU.max), [B0, gc, zerof], [nd])
                        V(lambda e: e.scalar_tensor_tensor(nd2[:], B0[:, 0:128], gcol, zerof[:], ALU.subtract, ALU.min), [B0, gc, zerof], [nd2])
                        A(lambda e: e.activation(out=eA2[:], in_=B0[:, 0:128], func=AF.Exp), [B0], [eA2])
                        A(lambda e: e.activation(out=nd[:], in_=nd[:], func=AF.Exp, scale=-1.0), [nd], [nd])
                        A(lambda e: e.activation(out=nd2[:], in_=nd2[:], func=AF.Exp), [nd2], [nd2])
                        yield
                        V(lambda e: e.tensor_tensor(decm[:], nd[:], masks[:, mstr, :], ALU.mult), [nd, masks], [decm])
                        V(lambda e: e.scalar_tensor_tensor(XX[0][:, 0:128], B0[:, 128:256], nbeta[:, t, hd:hd + 1], decm[:], ALU.mult, ALU.mult), [B0, nbeta, decm], [XX[0]])
                        V(lambda e: e.tensor_tensor(decTm[:], nd2[:], masks[:, minc, :], ALU.mult), [nd2, masks], [decTm])
                        V(lambda e: e.tensor_tensor(qkTm[:], B0[:, 256:384], decTm[:], ALU.mult), [B0, decTm], [qkTm])
                        c.mm(lambda e: e.transpose(B1[:, 0:128], XX[0][:, 0:128], ident[:]), [XX[0], ident], [B1])
                        yield
                        A(lambda e: e.copy(XX[0][:, 128:256], B1[:, 0:128]), [B1], [XX[0]])
                        V(lambda e: e.tensor_tensor(PTb[0][:], B1[:, 0:128], ident[:], ALU.add), [B1, ident], [PTb[0]])
                        V(lambda e: e.tensor_scalar(Vb[:], vtok[:, t, :], beta[:, t, hd:hd + 1], None, ALU.mult), [vtok, beta], [Vb])
                        V(lambda e: e.tensor_scalar(Kb[:], ktok[:, t, :], bexp[:, t, hd:hd + 1], None, ALU.mult), [ktok, bexp], [Kb])
                        V(lambda e: e.tensor_scalar(ktl[:], ktok[:, t, :], etail[:, t, hd:hd + 1], None, ALU.mult), [ktok, etail], [ktl])
                        V(lambda e: e.tensor_tensor(qh[:], qnT[:, ts], eA2[:], ALU.mult), [qnT, eA2], [qh])
                        nlev = 2 if is_s else 6
                        for k in range(nlev):
                            a, b = k % 2, (k + 1) % 2
                            c.mm(lambda e: e.matmul(B0[:, 0:128], lhsT=XX[a][:, 128:256], rhs=XX[a][:, 0:128], start=True, stop=True), [XX[a]], [B0])
                            if k < nlev - 1:
                                c.mm(lambda e: e.matmul(B0[:, 128:256], lhsT=XX[a][:, 0:128], rhs=XX[a][:, 128:256], start=True, stop=True), [XX[a]], [B0])
                            yield
                            ncp = 256 if k < nlev - 1 else 128
                            A(lambda e: e.copy(XX[b][:, 0:ncp], B0[:, 0:ncp]), [B0], [XX[b]])
                            c.mm(lambda e: e.matmul(B1[:, 0:128], lhsT=XX[b][:, 0:128], rhs=PTb[a][:], start=True, stop=True), [XX[b], PTb[a]], [B1])
                            yield
                            dstP = PTb[b] if k < nlev - 1 else TTb
                            V(lambda e: e.tensor_tensor(dstP[:], PTb[a][:], B1[:, 0:128], ALU.add), [PTb[a], B1], [dstP])
                        c.mm(lambda e: e.matmul(B0[:, 0:128], lhsT=Kb[:], rhs=TTb[:], start=True, stop=True), [Kb, TTb], [B0])
                        if not is_s:
                            c.mm(lambda e: e.matmul(B0[:, 128:256], lhsT=TTb[:], rhs=Vb[:], start=True, stop=True), [TTb, Vb], [B0])
                        else:
                            c.mm(lambda e: e.matmul(B0[:, 128:256], lhsT=Vb[:], rhs=TTb[:], start=True, stop=True), [TTb, Vb], [B0])
                        yield
                        A(lambda e: e.copy(wdT[:], B0[:, 0:128]), [B0], [wdT])
                        A(lambda e: e.copy(ub[:], B0[:, 128:256]), [B0], [ub])
                        yield
                        if not is_s:
                            c.mm(lambda e: e.matmul(B1[:, 0:128], lhsT=wdT[:], rhs=Sb[:], start=True, stop=True), [wdT, Sb], [B1])
                            V(lambda e: e.tensor_tensor(u[:], ub[:], B1[:, 0:128], ALU.subtract), [ub, B1], [u])
                            c.mm(lambda e: e.matmul(B1[:, 128:256], lhsT=Sb[:], rhs=qh[:], start=True, stop=False), [Sb, qh], [B1], inc=False)
                            c.mm(lambda e: e.matmul(B1[:, 128:256], lhsT=u[:], rhs=qkTm[:], start=False, stop=True), [u, qkTm], [B1])
                            c.mm(lambda e: e.matmul(B0[:, 0:128], lhsT=ktl[:], rhs=u[:], start=True, stop=True), [ktl, u], [B0])
                            V(lambda e: e.scalar_tensor_tensor(S[:], S[:], eA2[:, 127:128], B0[:, 0:128], ALU.mult, ALU.add), [S, eA2, B0], [S])
                            A(lambda e: e.copy(Sb[:], S[:]), [S], [Sb])
                            yield
                            out_norm(B1[:, 128:256], B1)
                        else:
                            uT, osum, ktm = T_["uT"], T_["osum"], T_["ktm"]
                            for s_ in range(16):
                                c.mm(lambda e, s_=s_: e.matmul(B1[:, s_ * 8:s_ * 8 + 8], lhsT=Sbs[:, s_, :], rhs=wdT[:, s_ * 8:s_ * 8 + 8], start=True, stop=True),
                                     [Sbs, wdT], [B1], inc=(s_ == 15))
                            for s_ in range(16):
                                c.mm(lambda e, s_=s_: e.matmul(B1[:, 128 + s_ * 8:128 + s_ * 8 + 8], lhsT=Sbs[:, s_, :], rhs=qh[:, s_ * 8:s_ * 8 + 8], start=True, stop=True),
                                     [Sbs, qh], [B1], inc=(s_ == 15))
                            yield
                            V(lambda e: e.tensor_tensor(uT[:], ub[:], B1[:, 0:128], ALU.subtract), [ub, B1], [uT])
                            c.mm(lambda e: e.transpose(B0[:, 0:128], uT[:], ident[:]), [uT, ident], [B0])
                            yield
                            A(lambda e: e.copy(u[:], B0[:, 0:128]), [B0], [u])
                            c.mm(lambda e: e.matmul(B0[:, 128:256], lhsT=u[:], rhs=qkTm[:], start=True, stop=True), [u, qkTm], [B0])
                            yield
                            A(lambda e: e.copy(osum[:], B0[:, 128:256]), [B0], [osum])
                            V(lambda e: e.tensor_tensor(osum[:], osum[:], B1[:, 128:256], ALU.add), [osum, B1], [osum])
                            out_norm(osum[:], osum)
                            for s_ in range(16):
                                kt = ktm[s_ % 2]
                                V(lambda e, s_=s_, kt=kt: e.tensor_scalar(kt[:], ktl[:], selcol[:, s_:s_ + 1], None, ALU.mult), [ktl, selcol], [kt])
                                BS = B0 if s_ % 2 == 0 else B1
                                c.mm(lambda e, kt=kt, BS=BS: e.matmul(BS[:, 384:512], lhsT=kt[:], rhs=u[:], start=True, stop=True), [kt, u], [BS])
                                V(lambda e, s_=s_, BS=BS: e.scalar_tensor_tensor(Sout[:, s_, :], S0s[:, s_, :], eA2[:, s_ * 8 + 7:s_ * 8 + 8], BS[:, 384:512], ALU.mult, ALU.add),
                                  [S0s, eA2, BS], [Sout])
                                if s_ % 4 == 3:
                                    yield

                    def run_interleaved(gens):
                        active = list(gens)
                        while active:
                            for g_ in list(active):
                                try:
                                    next(g_)
                                except StopIteration:
                                    active.remove(g_)

                    def load_head_w(hd_):
                        W_ = Wh[hd_ % 2]
                        for j, base in enumerate((0, 512, 1024, 1536)):
                            load_w(W_, W_[:, :, j * 128:(j + 1) * 128], wview(w_in_ab, base + hd_ * 128, 128))
                    load_head_w(0)
                    for hd in range(4):
                        W = Wh[hd % 2]
                        c.dma("sp", S0s[:], st_delta[:, hd].rearrange("s p d -> p s d"), writes=[S0s])
                        with contextlib.ExitStack() as PJ:
                            sbp = lambda name, shape, dt=F32: c.sb(name, shape, dt, PJ)
                            zc = [sbp("zc%d" % i, [128, 515]) for i in range(2)]
                            zcs = sbp("zcs", [128, 16, 11])
                            accs = [sbp("acc%d" % i, [128, 512]) for i in range(2)]
                            sqs = [sbp("sq%d" % i, [128, 512], BF16) for i in range(2)]
                            rs1s = [sbp("rs1%d" % i, [128, 512]) for i in range(2)]
                            vTbs = [sbp("vTb%d" % i, [128, 512], BF16) for i in range(2)]
                            sstok = sbp("sstok", [128, NT]); rstok = sbp("rstok", [128, NT]); qtok = sbp("qtok", [128, 4, 128], BF16)
                            pend_tail = []
                            cntb = 0
                            for which in range(4):
                                chq = which * 4 + hd
                                for bi, (c0, n) in enumerate(TB):
                                    pi = nextpb()
                                    proj_fm(pi, W, which * 128, c0, n)
                                    while pend_tail:
                                        pend_tail.pop(0)()
                                    if which == 3:
                                        A(lambda e, pi=pi, c0=c0, n=n: e.activation(out=gT[:, c0:c0 + n], in_=PB[pi][:, 0:n], func=AF.Silu), [PB[pi]], [gT])
                                        continue
                                    cntb += 1
                                    acc = accs[cntb % 2]; sq = sqs[cntb % 2]; rs1 = rs1s[cntb % 2]; vTb = vTbs[cntb % 2]
                                    if bi < 4:
                                        z = zc[bi % 2]
                                        if bi == 0:
                                            G(lambda e, z=z: e.memset(z[:, 0:3], 0.0), [], [z])
                                        A(lambda e, z=z, pi=pi: e.copy(z[:, 3:515], PB[pi][:, 0:512]), [PB[pi]], [z])
                                        if bi < 3:
                                            zn = zc[(bi + 1) % 2]
                                            V(lambda e, z=z, zn=zn: e.tensor_copy(zn[:, 0:3], z[:, 512:515]), [z], [zn])
                                        else:
                                            G(lambda e, z=z, chq=chq: e.tensor_copy(cnew_p[:, chq, :], z[:, 512:515]), [z], [cnew_p])
                                        srcs = [z[:, i:i + 512] for i in range(4)]
                                        accv = acc[:, 0:512]
                                        zb_ = z
                                    else:
                                        V(lambda e, chq=chq: e.tensor_copy(zcs[:, :, 0:3], cst[:, chq, :].rearrange("p (s j) -> p s j", j=3)), [cst], [zcs])
                                        A(lambda e, pi=pi: e.copy(zcs[:, :, 3:11], PB[pi][:, 0:128].rearrange("p (s j) -> p s j", j=8)), [PB[pi]], [zcs])
                                        G(lambda e, chq=chq: e.tensor_copy(cnew_s[:, chq, :].rearrange("p (s j) -> p s j", j=3), zcs[:, :, 8:11]), [zcs], [cnew_s])
                                        srcs = [zcs[:, :, i:i + 8] for i in range(4)]
                                        accv = acc[:, 0:128].rearrange("p (s j) -> p s j", j=8)
                                        zb_ = zcs
                                    V(lambda e, accv=accv, srcs=srcs, chq=chq: e.tensor_scalar(accv, srcs[0], cw[:, chq, 0:1], None, ALU.mult), [zb_, cw], [acc])
                                    for i in range(1, 4):
                                        V(lambda e, i=i, accv=accv, srcs=srcs, chq=chq: e.scalar_tensor_tensor(accv, srcs[i], cw[:, chq, i:i + 1], accv, ALU.mult, ALU.add), [zb_, cw, acc], [acc])
                                    if which == 2:
                                        A(lambda e, n=n, acc=acc, vTb=vTb: e.activation(out=vTb[:, 0:n], in_=acc[:, 0:n], func=AF.Silu), [acc], [vTb])

                                        def tail_v(c0=c0, n=n, vTb=vTb):
                                            pj = nextpb()
                                            nt_ = n // 128
                                            for tt in range(nt_):
                                                c.mm(lambda e, tt=tt: e.transpose(PBb[pj][:, tt * 128:(tt + 1) * 128], vTb[:, tt * 128:(tt + 1) * 128], identb[:]), [vTb, identb], [PB[pj]], inc=(tt == nt_ - 1))
                                            V(lambda e: e.tensor_copy(vtok[:, c0 // 128:c0 // 128 + nt_, :], PBb[pj][:, 0:n].rearrange("p (t f) -> p t f", f=128)), [PB[pj]], [vtok])
                                        pend_tail.append(tail_v)
                                        continue
                                    dst = qnT if which == 0 else knT
                                    A(lambda e, n=n, acc=acc, dst=dst, c0=c0: e.activation(out=dst[:, c0:c0 + n], in_=acc[:, 0:n], func=AF.Silu), [acc], [dst])
                                    A(lambda e, n=n, sq=sq, dst=dst, c0=c0: e.activation(out=sq[:, 0:n], in_=dst[:, c0:c0 + n], func=AF.Square), [dst], [sq])
                                    pj = nextpb()
                                    nt_ = n // 128
                                    for tt in range(nt_):
                                        c.mm(lambda e, tt=tt, sq=sq: e.matmul(PB[pj][:, tt:tt + 1], lhsT=sq[:, tt * 128:(tt + 1) * 128], rhs=onesb[:, 0:1], start=True, stop=True),
                                             [sq, onesb], [PB[pj]], inc=(tt == nt_ - 1))
                                    V(lambda e, pj=pj, c0=c0, nt_=nt_: e.tensor_copy(sstok[:, c0 // 128:c0 // 128 + nt_], PB[pj][:, 0:nt_]), [PB[pj]], [sstok])
                                if which in (0, 1):
                                    sc = 128.0 if which == 0 else 1.0
                                    dst = qnT if which == 0 else knT
                                    while pend_tail:
                                        pend_tail.pop(0)()
                                    V(lambda e, sc=sc: e.tensor_scalar(rstok[:], sstok[:], sc, EPS * sc, ALU.mult, ALU.add), [sstok], [rstok])
                                    A(lambda e: e.activation(out=rstok[:], in_=rstok[:], func=AF.Sqrt), [rstok], [rstok])
                                    V(lambda e: e.reciprocal(rstok[:], rstok[:]), [rstok], [rstok])
                                    for (c0, n) in TB:
                                        nt_ = n // 128
                                        t0_ = c0 // 128
                                        pk = nextpb()
                                        for tt in range(nt_):
                                            c.mm(lambda e, tt=tt, c0=c0, dst=dst: e.transpose(PBb[pk][:, tt * 128:(tt + 1) * 128], dst[:, c0 + tt * 128:c0 + (tt + 1) * 128], identb[:]),
                                                 [dst, identb], [PB[pk]], inc=(tt == nt_ - 1))
                                        tok = ktok[:, t0_:t0_ + nt_, :] if which == 1 else qtok[:, 0:nt_, :]
                                        tokb = ktok if which == 1 else qtok
                                        V(lambda e, pk=pk, n=n, nt_=nt_, t0_=t0_, tok=tok: e.tensor_tensor(
                                            tok, PBb[pk][:, 0:n].rearrange("p (t f) -> p t f", f=128),
                                            rstok[:, t0_:t0_ + nt_].unsqueeze(2).to_broadcast([128, nt_, 128]), ALU.mult), [PB[pk], rstok], [tokb])
                                        pk2 = nextpb()
                                        for tt in range(nt_):
                                            src_t = ktok[:, t0_ + tt, :] if which == 1 else qtok[:, tt, :]
                                            c.mm(lambda e, tt=tt, src_t=src_t: e.transpose(PBb[pk2][:, tt * 128:(tt + 1) * 128], src_t, identb[:]),
                                                 [tokb, identb], [PB[pk2]], inc=(tt == nt_ - 1))
                                        A(lambda e, pk2=pk2, c0=c0, n=n, dst=dst: e.copy(dst[:, c0:c0 + n], PBb[pk2][:, 0:n]), [PB[pk2]], [dst])
                            while pend_tail:
                                pend_tail.pop(0)()
                            c.barrier()
                        if hd + 1 < 4:
                            load_head_w(hd + 1)
                        if hd == 0:
                            stage(3)
                            tap("qnT", qnT, qnT[:, :]); tap("knT", knT, knT[:, :]); tap("vtok", vtok, vtok[:, :, :]); tap("gT", gT, gT[:, :])
                        with contextlib.ExitStack() as TL:
                            sbt = lambda name, shape, dt=F32: c.sb(name, shape, dt, TL)
                            TS = []
                            NSET = 3
                            for p_ in range(NSET):
                                d_ = {}
                                for nm in ("gTri", "nd", "nd2", "eA2", "ub"):
                                    d_[nm] = sbt(nm + str(p_), [128, 128])
                                for nm in ("TTb", "Vb", "Kb", "ktl", "qh", "wdT", "qkTm", "u", "osq"):
                                    d_[nm] = sbt(nm + str(p_), [128, 128], BF16)
                                d_["XX"] = [sbt("XX%d%d" % (p_, i), [128, 256], F32) for i in range(2)]
                                d_["PT"] = [sbt("PT%d%d" % (p_, i), [128, 128], F32) for i in range(2)]
                                TS.append(d_)
                            G(lambda e: e.memset(S[:], 0.0), [], [S])
                            G(lambda e: e.memset(Sb[:], 0.0), [], [Sb])
                            V(lambda e: e.tensor_copy(Sbs[:], S0s[:]), [S0s], [Sbs])
                            pending = list(range(NT)); free_sets = list(range(NSET)); active = []
                            while pending or active:
                                while pending and free_sets:
                                    t_ = pending.pop(0); s_ = free_sets.pop(0)
                                    T_ = TS[s_]
                                    if t_ == 16:
                                        T_ = dict(T_, uT=T_["nd"], osum=T_["nd2"], ktm=[T_["Vb"], T_["Kb"]])
                                    active.append((gdn_tile(hd, t_, T_, s_), s_))
                                for ent in list(active):
                                    try:
                                        next(ent[0])
                                    except StopIteration:
                                        active.remove(ent); free_sets.append(ent[1])
                            c.dma("sp", o_delta_p[hd], S[:], reads=[S], is_output=True)
                            c.dma("sp", o_delta_s[:, hd].rearrange("s p d -> p s d"), Sout[:], reads=[Sout], is_output=True)
                            c.barrier()
                        stage(4 + hd)
                    c.barrier()
                    HD.close()
                    cno_p = c.sb("cno_p", [3, 1536], F32, GD); cno_s = c.sb("cno_s", [48, 1536], F32, GD)
                    for ch in range(12):
                        pi = nextpb()
                        c.mm(lambda e, ch=ch, pi=pi: e.transpose(PB[pi][0:3, 0:128], cnew_p[:, ch, :], ident[:]), [cnew_p, ident], [PB[pi]])
                        c.mm(lambda e, ch=ch, pi=pi: e.transpose(PB[pi][0:48, 128:256], cnew_s[:, ch, :], ident[:]), [cnew_s, ident], [PB[pi]])
                        V(lambda e, ch=ch, pi=pi: e.tensor_copy(cno_p[:, ch * 128:(ch + 1) * 128], PB[pi][0:3, 0:128]), [PB[pi]], [cno_p])
                        V(lambda e, ch=ch, pi=pi: e.tensor_copy(cno_s[:, ch * 128:(ch + 1) * 128], PB[pi][0:48, 128:256]), [PB[pi]], [cno_s])
                    c.dma("sp", o_conv_p, cno_p[:], reads=[cno_p], is_output=True)
                    c.dma("sp", o_conv_s, cno_s[:], reads=[cno_s], is_output=True)
                    c.barrier()
                tap("omT", omT, omT[:, 0, :])

                omTb = c.sb("omTb", [128, 4, TT], BF16, L0)
                with contextlib.ExitStack() as SG:
                    sb = lambda name, shape, dt=F32: c.sb(name, shape, dt, SG)
                    uT = sb("uT", [128, 4, TT], BF16)
                    Wzu = sb("Wzu", [128, 8, 512], BF16); Wzv = sb("Wzv", [128, 8, 512], BF16)
                    wspn = sb("wspn", [128, 4, 128]); WspT = sb("WspT", [128, 4, 128], BF16); WspTf = sb("WspTf", [128, 4, 128])
                    WspS = sb("WspS", [128, 4, 128], BF16); x8 = sb("x8", [8, 128])
                    bspb = sb("bspb", [128, 4, 128]); bsps = sb("bsps", [128, 4, 128])
                    lngb = sb("lngb", [128, 512]); lnbb = sb("lnbb", [128, 512])
                    vg = sb("vg", [128, 512]); vsq = sb("vsq", [128, 512], BF16); vf = sb("vf", [128, 512]); vb = sb("vb", [128, 512], BF16)
                    st1 = sb("st1", [128, 8]); mx = sb("mx", [128, 4, 128])
                    load_w(Wzu, Wzu[:], wview(w_in_ab, 2056, 512))
                    load_w(Wzv, Wzv[:], wview(w_in_ab, 2568, 512))
                    c.dma("sp", wspn[:], w_sp.rearrange("g t s -> t g s"), writes=[wspn])
                    c.dma("sp", bspb[:].rearrange("p g t -> p (g t)"), b_sp.rearrange("g t -> (g t)").partition_broadcast(128), writes=[bspb])
                    c.dma("sp", lngb[:], ln_g.partition_broadcast(128), writes=[lngb])
                    c.dma("sp", lnbb[:], ln_b.partition_broadcast(128), writes=[lnbb])
                    for g in range(4):
                        pi = nextpb()
                        c.mm(lambda e, g=g: e.transpose(PB[pi][:, 0:128], wspn[:, g, :], ident[:]), [wspn, ident], [PB[pi]])
                        V(lambda e, g=g: e.tensor_tensor(WspTf[:, g, :], PB[pi][:, 0:128], masks[:, M_INCLU, :], ALU.mult), [PB[pi], masks], [WspTf])
                        V(lambda e, g=g: e.tensor_copy(WspT[:, g, :], WspTf[:, g, :]), [WspTf], [WspT])
                        V(lambda e, g=g: e.tensor_copy(x8[0:8, :].rearrange("p (s j) -> p s j", j=8), WspTf[0:8, g, 0:8].unsqueeze(1).to_broadcast([8, 16, 8])), [WspTf], [x8])
                        pj = nextpb()
                        c.mm(lambda e: e.matmul(PB[pj][:, 0:128], lhsT=masks[0:8, M_SPS, :], rhs=x8[0:8, :], start=True, stop=True), [masks, x8], [PB[pj]])
                        V(lambda e, g=g: e.tensor_tensor(WspS[:, g, :], PB[pj][:, 0:128], masks[:, M_INCLU_S, :], ALU.mult), [PB[pj], masks], [WspS])
                    V(lambda e: e.tensor_copy(bsps[:].rearrange("p g (s j) -> p g s j", j=8), bspb[:, :, 0:8].unsqueeze(2).to_broadcast([128, 4, 16, 8])), [bspb], [bsps])
                    for g in range(4):
                        for (c0, n) in TB:
                            pi = nextpb()
                            proj_fm(pi, Wzu, g * 128, c0, n)
                            A(lambda e, g=g, c0=c0, n=n, pi=pi: e.activation(out=uT[:, g, c0:c0 + n], in_=PB[pi][:, 0:n], func=AF.Gelu_apprx_tanh), [PB[pi]], [uT])
                    vgs = [vg, sb("vg2", [128, 512])]; vbs = [vb, vb]; st1s = [st1, sb("st1b", [128, 8])]
                    pzv = {}

                    def zv_proj(t):
                        pzv[t] = nextpb()
                        proj_tm(pzv[t], Wzv, 0, 512, t)
                    zv_proj(0)
                    for t in range(NT):
                        ts = slice(t * 128, (t + 1) * 128)
                        vg_, vb_, s1 = vgs[t % 2], vbs[t % 2], st1s[t % 2]
                        pi = pzv.pop(t)
                        if t + 1 < NT:
                            zv_proj(t + 1)
                        A(lambda e, pi=pi: e.activation(out=vg_[:], in_=PB[pi][:, 0:512], func=AF.Gelu_apprx_tanh, accum_out=s1[:, 0:1]), [PB[pi]], [vg_, s1])
                        A(lambda e: e.activation(out=vsq[:], in_=vg_[:], func=AF.Square, accum_out=s1[:, 1:2]), [vg_, s1], [vsq, s1])
                        V(lambda e: e.tensor_scalar(s1[:, 2:3], s1[:, 0:1], 1.0 / 512, None, ALU.mult), [s1], [s1])
                        V(lambda e: e.tensor_tensor(s1[:, 3:4], s1[:, 2:3], s1[:, 2:3], ALU.mult), [s1], [s1])
                        V(lambda e: e.scalar_tensor_tensor(s1[:, 4:5], s1[:, 1:2], 1.0 / 512, s1[:, 3:4], ALU.mult, ALU.subtract), [s1], [s1])
                        A(lambda e: e.activation(out=s1[:, 5:6], in_=s1[:, 4:5], func=AF.Sqrt, bias=EPS), [s1], [s1])
                        V(lambda e: e.reciprocal(s1[:, 5:6], s1[:, 5:6]), [s1], [s1])
                        V(lambda e: e.scalar_tensor_tensor(s1[:, 6:7], s1[:, 2:3], -1.0, s1[:, 5:6], ALU.mult, ALU.mult), [s1], [s1])
                        A(lambda e: e.activation(out=vg_[:], in_=vg_[:], func=AF.Identity, scale=s1[:, 5:6], bias=s1[:, 6:7]), [vg_, s1], [vg_])
                        V(lambda e: e.tensor_tensor(vg_[:], vg_[:], lngb[:], ALU.mult), [vg_, lngb], [vg_])
                        if t >= 15:
                            V(lambda e: e.tensor_tensor(vf[:], vg_[:], lnbb[:], ALU.add), [vg_, lnbb], [vf])
                            c.dma("sp", o_sguv_p if t == 15 else o_sguv_s, vf[:], reads=[vf], is_output=True)
                        V(lambda e: e.tensor_tensor(vb_[:], vg_[:], lnbb[:], ALU.add), [vg_, lnbb], [vb_])
                        Wm = WspS if t == 16 else WspT
                        bb = bsps if t == 16 else bspb
                        pj = nextpb()
                        for g in range(4):
                            c.mm(lambda e, g=g: e.matmul(PB[pj][:, g * 128:(g + 1) * 128], lhsT=vb_[:, g * 128:(g + 1) * 128], rhs=Wm[:, g, :], start=True, stop=True),
                                 [vb_, Wm], [PB[pj]], inc=(g == 3))
                        V(lambda e, pj=pj: e.tensor_tensor(mx[:], PB[pj][:, 0:512].rearrange("p (g t) -> p g t", g=4), bb[:], ALU.add), [PB[pj], bb], [mx])
                        G(lambda e, ts=ts: e.tensor_tensor(omTb[:, :, ts], mx[:], uT[:, :, ts], ALU.mult), [mx, uT], [omTb])
                    tap("omB", omTb, omTb[:, :, :].rearrange("p g t -> p (g t)"))
                    c.barrier()
                with contextlib.ExitStack() as WO:
                    Wo = c.sb("Wo", [128, 8, 1024], BF16, WO)
                    load_w(Wo, Wo[:], wview(w_out_ab, 0, 1024))
                    for t in range(NT):
                        for cb in range(2):
                            pi = nextpb()
                            for k in range(8):
                                src = omT if k < 4 else omTb
                                c.mm(lambda e, k=k, src=src: e.matmul(PB[pi][:, 0:512], lhsT=src[:, k % 4, t * 128:(t + 1) * 128], rhs=Wo[:, k, cb * 512:(cb + 1) * 512],
                                                                   start=(k == 0), stop=(k == 7)), [src, Wo], [PB[pi]], inc=(k == 7))
                            V(lambda e, t=t, cb=cb, pi=pi: e.tensor_tensor(h[:, t, cb * 512:(cb + 1) * 512], h[:, t, cb * 512:(cb + 1) * 512], PB[pi][:, 0:512], ALU.add), [h, PB[pi]], [h])
                    c.barrier()
            tap("hA", h, h[:, :, :])

            def ffn(li, gi):
                with contextlib.ExitStack() as FF:
                    Wu = [c.sb("Wu%d" % i, [128, 8, 512], BF16, FF) for i in range(2)]
                    Wd = [c.sb("Wd%d" % i, [128, 4, 1024], BF16, FF) for i in range(2)]
                    aT = [c.sb("aT%d" % i, [128, 4, 512], BF16, FF) for i in range(3)]
                    rl = [c.sb("rl%d" % i, [128, 512], BF16, FF) for i in range(2)]

                    def load_pass(p):
                        load_w(Wu[p % 2], Wu[p % 2][:], wview(w_up[li], p * 512, 512))
                        load_w(Wd[p % 2], Wd[p % 2][:], wview(w_down[li], 0, 1024, k0=p * 4, nk=4))

                    seq = [(p, c0, n) for p in range(8) for (c0, n) in TB]

                    def up(i):
                        p, c0, n = seq[i]
                        a = aT[i % 3]
                        for j in range(4):
                            pi = nextpb()
                            proj_fm(pi, Wu[p % 2], j * 128, c0, n)
                            r_ = rl[j % 2]
                            A(lambda e, pi=pi, n=n, r_=r_: e.activation(out=r_[:, 0:n], in_=PB[pi][:, 0:n], func=AF.Relu), [PB[pi]], [r_])
                            G(lambda e, j=j, n=n, r_=r_, a=a: e.tensor_tensor(a[:, j, 0:n], r_[:, 0:n], r_[:, 0:n], ALU.mult), [r_], [a])

                    def down(i):
                        p, c0, n = seq[i]
                        a = aT[i % 3]; wd = Wd[p % 2]
                        for tt in range(n // 128):
                            t = c0 // 128 + tt
                            for cb in range(2):
                                pi = nextpb()
                                for j in range(4):
                                    c.mm(lambda e, j=j, tt=tt, cb=cb: e.matmul(PB[pi][:, 0:512], lhsT=a[:, j, tt * 128:(tt + 1) * 128], rhs=wd[:, j, cb * 512:(cb + 1) * 512],
                                                                           start=(j == 0), stop=(j == 3)), [a, wd], [PB[pi]], inc=(j == 3))
                                V(lambda e, t=t, cb=cb, pi=pi: e.tensor_tensor(h[:, t, cb * 512:(cb + 1) * 512], h[:, t, cb * 512:(cb + 1) * 512], PB[pi][:, 0:512], ALU.add), [h, PB[pi]], [h])

                    load_pass(0)
                    load_pass(1)
                    rmsnorm_T(gi)
                    up(0)
                    for i in range(len(seq)):
                        if i + 1 < len(seq):
                            up(i + 1)
                        down(i)
                        if seq[i][1] == 2048 and seq[i][0] + 2 < 8:
                            load_pass(seq[i][0] + 2)
                    c.barrier()

            def ple(li, gi):
                with contextlib.ExitStack() as PE_:
                    Wg = c.sb("Wg", [128, 8, 1024], BF16, PE_); Wp = c.sb("Wp", [128, 2, 1024], BF16, PE_)
                    ptok = c.sb("ptok", [128, NT, 256], BF16, PE_); pT = c.sb("pT", [128, 2, TT], BF16, PE_)
                    gt = [c.sb("gt%d" % i, [128, 512], F32, PE_) for i in range(2)]
                    load_w(Wg, Wg[:], wview(w_pe_gate[li], 0, 1024))
                    load_w(Wp, Wp[:], wview(w_pe_proj[li], 0, 1024, nk=2))
                    c.dma("pool", ptok[:, 0:16, :], pp[li].rearrange("(n p) d -> p n d", p=128), writes=[ptok])
                    c.dma("pool", ptok[:, 16, :], psm[li], writes=[ptok])
                    for t in range(NT):
                        pi = nextpb()
                        for k in range(2):
                            c.mm(lambda e, k=k, t=t: e.transpose(PBb[pi][:, k * 128:(k + 1) * 128], ptok[:, t, k * 128:(k + 1) * 128], identb[:]), [ptok, identb], [PB[pi]], inc=(k == 1))
                        V(lambda e, t=t, pi=pi: e.tensor_copy(pT[:, :, t * 128:(t + 1) * 128], PBb[pi][:, 0:256].rearrange("p (k f) -> p k f", k=2)), [PB[pi]], [pT])
                    rmsnorm_T(gi)
                    for t in range(NT):
                        for cb in range(2):
                            g_ = gt[(2 * t + cb) % 2]
                            pi = nextpb()
                            proj_tm(pi, Wg, cb * 512, 512, t)
                            A(lambda e, pi=pi, g_=g_: e.activation(out=g_[:], in_=PB[pi][:, 0:512], func=AF.Sigmoid), [PB[pi]], [g_])
                            pj = nextpb()
                            proj_tm(pj, Wp, cb * 512, 512, t, src=pT, nk=2)
                            V(lambda e, pj=pj, g_=g_: e.tensor_tensor(g_[:], g_[:], PB[pj][:, 0:512], ALU.mult), [g_, PB[pj]], [g_])
                            G(lambda e, t=t, cb=cb, g_=g_: e.tensor_tensor(h[:, t, cb * 512:(cb + 1) * 512], h[:, t, cb * 512:(cb + 1) * 512], g_[:], ALU.add), [h, g_], [h])
                    c.barrier()

            ffn(0, 1)
            tap("hF0", h, h[:, :, :])
            ple(0, 2)
            tap("hP0", h, h[:, :, :])
            stage(30)

            rmsnorm_T(3)
            with contextlib.ExitStack() as L1:
                omC = c.sb("omC", [128, 8, TT], BF16, L1)
                with contextlib.ExitStack() as PL:
                    sb = lambda name, shape, dt=F32: c.sb(name, shape, dt, PL)
                    Wc = sb("Wc", [128, 8, 512], BF16)
                    load_w(Wc, Wc[:], wview(w_in_cd, 0, 512))
                    wpl = sb("wpl", [128, 4, 128], BF16)
                    c.dma("pool", wpl[:], w_pool.rearrange("g i o -> i g o"), writes=[wpl])
                    psc = sb("psc", [128, 4])
                    c.dma("sp", psc[:], pool_scale.rearrange("(g p) -> p g", p=128), writes=[psc], allow_slow_non_contiguous=True)
                    pst = sb("pst", [128, 4, 240])
                    with contextlib.ExitStack() as TMP:
                        stp = [c.sb("stp%d" % i, [120, 512], F32, TMP) for i in range(2)]
                        for i in range(2):
                            c.dma("sp", stp[i][:], st_pool.rearrange("s j d -> (s j) d")[i * 120:(i + 1) * 120, :], writes=[stp[i]])
                        for gi in range(4):
                            pi = nextpb()
                            for i in range(2):
                                c.mm(lambda e, i=i, gi=gi: e.transpose(PB[pi][:, i * 120:(i + 1) * 120], stp[i][0:120, gi * 128:(gi + 1) * 128], ident[0:120, 0:120]), [stp[i], ident], [PB[pi]])
                            V(lambda e, gi=gi, pi=pi: e.tensor_copy(pst[:, gi, :], PB[pi][:, 0:240]), [PB[pi]], [pst])
                        c.barrier()
                    stage(31)
                    xe = sb("xe", [128, 2063]); xa = [sb("xa%d" % i, [128, 2063]) for i in range(2)]
                    xes = sb("xes", [128, 16, 23]); xas = [sb("xas%d" % i, [128, 16, 23]) for i in range(2)]
                    mT = sb("mT", [128, TT], BF16); tmpf = sb("tmpf", [128, 16]); t240 = sb("t240", [128, 240])
                    pno_p = sb("pno_p", [15, 512]); pno_s = sb("pno_s", [120, 2, 512])
                    G(lambda e: e.memset(xe[:, 0:15], 0.0), [], [xe])
                    for b_ in xa + xas:
                        G(lambda e, b_=b_: e.memset(b_[:], 0.0), [], [b_])
                    for gi in range(4):
                        win = 2 ** (gi + 1)
                        for (c0, n) in TB:
                            pi = nextpb()
                            proj_fm(pi, Wc, gi * 128, c0, n)
                            if c0 < 2048:
                                A(lambda e, pi=pi, c0=c0: e.copy(xe[:, 15 + c0:15 + c0 + 512], PB[pi][:, 0:512]), [PB[pi]], [xe])
                            else:
                                A(lambda e, pi=pi: e.copy(xes[:, :, 15:23], PB[pi][:, 0:128].rearrange("p (s j) -> p s j", j=8)), [PB[pi]], [xes])
                        G(lambda e, gi=gi: e.tensor_copy(xes[:, :, 0:15], pst[:, gi, :].rearrange("p (s j) -> p s j", j=15)), [pst], [xes])
                        src, srcs = xe, xes
                        for k in range(gi + 1):
                            sh = 2 ** k
                            dst, dsts = xa[k % 2], xas[k % 2]
                            V(lambda e, src=src, dst=dst, sh=sh: e.tensor_tensor(dst[:, sh:2063], src[:, sh:2063], src[:, 0:2063 - sh], ALU.add), [src], [dst])
                            V(lambda e, srcs=srcs, dsts=dsts, sh=sh: e.tensor_tensor(dsts[:, :, sh:23], srcs[:, :, sh:23], srcs[:, :, 0:23 - sh], ALU.add), [srcs], [dsts])
                            src, srcs = dst, dsts
                        V(lambda e, src=src, win=win: e.scalar_tensor_tensor(mT[:, 0:2048], src[:, 15:2063], 1.0 / win, xe[:, 15:2063], ALU.mult, ALU.subtract), [src, xe], [mT])
                        V(lambda e, src=src, win=win: e.tensor_tensor(tmpf[:, 0:win - 1], src[:, 15:15 + win - 1], misc[:, 0:win - 1], ALU.mult), [src, misc], [tmpf])
                        V(lambda e, win=win: e.tensor_tensor(mT[:, 0:win - 1], tmpf[:, 0:win - 1], xe[:, 15:15 + win - 1], ALU.subtract), [tmpf, xe], [mT])
                        V(lambda e, srcs=srcs, win=win: e.scalar_tensor_tensor(mT[:, 2048:2176].rearrange("p (s j) -> p s j", j=8), srcs[:, :, 15:23], 1.0 / win, xes[:, :, 15:23], ALU.mult, ALU.subtract), [srcs, xes], [mT])
                        pi = nextpb()
                        c.mm(lambda e: e.transpose(PB[pi][0:15, 0:128], xe[:, 2048:2063], ident[:]), [xe, ident], [PB[pi]])
                        V(lambda e, gi=gi, pi=pi: e.tensor_copy(pno_p[0:15, gi * 128:(gi + 1) * 128], PB[pi][0:15, 0:128]), [PB[pi]], [pno_p])
                        G(lambda e: e.tensor_copy(t240[:].rearrange("p (s j) -> p s j", j=15), xes[:, :, 8:23]), [xes], [t240])
                        for i in range(2):
                            pj = nextpb()
                            c.mm(lambda e, i=i, pj=pj: e.transpose(PB[pj][0:120, 0:128], t240[:, i * 120:(i + 1) * 120], ident[:]), [t240, ident], [PB[pj]])
                            V(lambda e, i=i, gi=gi, pj=pj: e.tensor_copy(pno_s[0:120, i, gi * 128:(gi + 1) * 128], PB[pj][0:120, 0:128]), [PB[pj]], [pno_s])
                        for (c0, n) in TB:
                            pi = nextpb()
                            c.mm(lambda e, gi=gi, c0=c0, n=n, pi=pi: e.matmul(PB[pi][:, 0:n], lhsT=wpl[:, gi, :], rhs=mT[:, c0:c0 + n], start=True, stop=True), [wpl, mT], [PB[pi]])
                            V(lambda e, gi=gi, c0=c0, n=n, pi=pi: e.tensor_scalar(omC[:, gi, c0:c0 + n], PB[pi][:, 0:n], psc[:, gi:gi + 1], None, ALU.mult), [PB[pi], psc], [omC])
                        if gi == 0: stage(32)
                    c.dma("sp", o_pool_p, pno_p[:], reads=[pno_p], is_output=True)
                    for i in range(2):
                        c.dma("sp", o_pool_s.rearrange("s j d -> (s j) d")[i * 120:(i + 1) * 120, :], pno_s[:, i, :], reads=[pno_s], is_output=True)
                    c.barrier()

                with contextlib.ExitStack() as S5:
                    sb = lambda name, shape, dt=F32: c.sb(name, shape, dt, S5)
                    PI2 = 6.283185307179586
                    xdT = sb("xdT", [128, 4, TT], BF16)
                    with contextlib.ExitStack() as TMP:
                        Wd5 = c.sb("Wd5", [128, 8, 512], BF16, TMP)
                        load_w(Wd5, Wd5[:], wview(w_in_cd, 512, 512))
                        for cc in range(4):
                            for (c0, n) in TB:
                                pi = nextpb()
                                proj_fm(pi, Wd5, cc * 128, c0, n)
                                A(lambda e, cc=cc, c0=c0, n=n, pi=pi: e.copy(xdT[:, cc, c0:c0 + n], PB[pi][:, 0:n]), [PB[pi]], [xdT])
                        c.barrier()
                    flat = xnT.t.bitcast(F32).rearrange("p k f -> p (k f)")
                    EXr = Buf(flat[:, 0:2048].rearrange("p (m f) -> p m f", m=16), "EXr"); EXi = Buf(flat[:, 2048:4096].rearrange("p (m f) -> p m f", m=16), "EXi")
                    Er = sb("Er", [128, 16, 128]); Ei = sb("Ei", [128, 16, 128])
                    cv = lambda k: xnT.t[:, k, 0:2048].rearrange("p (m f) -> p m f", m=16)
                    CexpR = Buf(cv(4), "CexpR"); CexpIn = Buf(cv(5), "CexpIn"); BexpR = Buf(cv(6), "BexpR"); BexpI = Buf(cv(7), "BexpI")
                    SET = S5.enter_context(contextlib.ExitStack())
                    sbs = lambda name, shape, dt=F32: c.sb(name, shape, dt, SET)
                    diagD = sb("diagD", [128, 4, 128], BF16); Wgl = sb("Wgl", [128, 4, 512], BF16)
                    prm = sb("prm", [128, 24, 16])
                    prmi = sb("prmi", [128, 16], I32)
                    dsk = sb("dsk", [128, 4]); bgl = sb("bgl", [128, 4])
                    s0 = sb("s0", [128, 2, 16, 16])
                    sst = sb("sst", [128, 2, 16]); tiny = sb("tiny", [128, 4]); ee2 = sb("ee2", [128, 16, 2])
                    ldtb = sbs("ldtb", [128, 32]); l16 = sbs("l16", [16, 2, 128]); d4 = sbs("d4", [4, 2, 128])
                    Ball = sbs("Ball", [128, 2, 16, 16]); Bs = sbs("Bs", [128, 2, 16, 16]); Btmp = sbs("Btmp", [128, 2, 16, 16])
                    Cn = sbs("Cn", [128, 2, 64]); CX = sbs("CX", [128, 2, 128])
                    s0n = Buf(flat[0:16, 0:2048], "s0n")
                    P_ = lambda i: prm[:, i, :]
                    LRE, LIM, LDT, DTV, RHO, TH, Q, QF, R_, SN, CS, AB, LBR, LBI, AA, DEN, CR, CI, T1, T2 = range(20)
                    c.dma("sp", l16[:, 0, :], lam_re.rearrange("(m p) -> m p", p=128), writes=[l16])
                    c.dma("sp", l16[:, 1, :], lam_im.rearrange("(m p) -> m p", p=128), writes=[l16])
                    c.dma("sp", ldtb[:], log_dt.partition_broadcast(128), writes=[ldtb])
                    c.dma("sp", d4[:, 0, :], d_skip.rearrange("(c p) -> c p", p=128), writes=[d4])
                    c.dma("sp", d4[:, 1, :], b_glu.rearrange("(c p) -> c p", p=128), writes=[d4])
                    c.dma("sp", Ball[:, 0, :, :], b_re.rearrange("(m two) n c -> (two n) m c", two=2), writes=[Ball])
                    c.dma("sp", Ball[:, 1, :, :], b_im.rearrange("(m two) n c -> (two n) m c", two=2), writes=[Ball])
                    load_w(Wgl, Wgl[:], wview(w_glu, 0, 512, nk=4))
                    pi = nextpb()
                    c.mm(lambda e: e.transpose(PB[pi][:, 0:16], l16[0:16, 0, :], ident[0:16, 0:16]), [l16, ident], [PB[pi]])
                    c.mm(lambda e: e.transpose(PB[pi][:, 16:32], l16[0:16, 1, :], ident[0:16, 0:16]), [l16, ident], [PB[pi]])
                    c.mm(lambda e: e.transpose(PB[pi][:, 32:36], d4[0:4, 0, :], ident[0:4, 0:4]), [d4, ident], [PB[pi]])
                    c.mm(lambda e: e.transpose(PB[pi][:, 36:40], d4[0:4, 1, :], ident[0:4, 0:4]), [d4, ident], [PB[pi]])
                    V(lambda e: e.tensor_copy(prm[:, 0:2, :], PB[pi][:, 0:32].rearrange("p (a m) -> p a m", a=2)), [PB[pi]], [prm])
                    V(lambda e: e.tensor_copy(dsk[:], PB[pi][:, 32:36]), [PB[pi]], [dsk])
                    V(lambda e: e.tensor_copy(bgl[:], PB[pi][:, 36:40]), [PB[pi]], [bgl])
                    V(lambda e: e.tensor_copy(prm[0:64, LDT, :], ldtb[0:64, 0:32:2]), [ldtb], [prm])
                    V(lambda e: e.tensor_copy(prm[64:128, LDT, :], ldtb[64:128, 1:32:2]), [ldtb], [prm])
                    pp_ = [prm]
                    A(lambda e: e.activation(out=P_(DTV), in_=P_(LDT), func=AF.Exp), pp_, pp_)
                    V(lambda e: e.tensor_tensor(P_(RHO), P_(LRE), P_(DTV), ALU.mult), pp_, pp_)
                    A(lambda e: e.activation(out=P_(RHO), in_=P_(RHO), func=AF.Exp), pp_, pp_)
                    V(lambda e: e.tensor_tensor(P_(TH), P_(LIM), P_(DTV), ALU.mult), pp_, pp_)
                    V(lambda e: e.tensor_scalar(P_(R_), P_(TH), 0.125, None, ALU.mult), pp_, pp_)
                    V(lambda e: e.tensor_scalar(P_(R_), P_(R_), -3.14159, 3.14159, ALU.max, ALU.min), pp_, pp_)
                    A(lambda e: e.activation(out=P_(SN), in_=P_(R_), func=AF.Sin), pp_, pp_)
                    V(lambda e: e.tensor_scalar(P_(T1), P_(R_), -1.0, None, ALU.mult), pp_, pp_)
                    V(lambda e: e.tensor_tensor(P_(AB), P_(R_), P_(T1), ALU.max), pp_, pp_)
                    V(lambda e: e.tensor_scalar(P_(AB), P_(AB), -1.0, 1.5707963, ALU.mult, ALU.add), pp_, pp_)
                    A(lambda e: e.activation(out=P_(CS), in_=P_(AB), func=AF.Sin), pp_, pp_)
                    for _ in range(3):
                        V(lambda e: e.tensor_tensor(P_(T1), P_(CS), P_(CS), ALU.mult), pp_, pp_)
                        V(lambda e: e.tensor_tensor(P_(T2), P_(SN), P_(SN), ALU.mult), pp_, pp_)
                        V(lambda e: e.scalar_tensor_tensor(P_(SN), P_(CS), 2.0, P_(SN), ALU.mult, ALU.mult), pp_, pp_)
                        V(lambda e: e.tensor_tensor(P_(CS), P_(T1), P_(T2), ALU.subtract), pp_, pp_)
                    V(lambda e: e.tensor_tensor(P_(LBR), P_(RHO), P_(CS), ALU.mult), pp_, pp_)
                    V(lambda e: e.tensor_tensor(P_(LBI), P_(RHO), P_(SN), ALU.mult), pp_, pp_)
                    V(lambda e: e.tensor_scalar(P_(AA), P_(LBR), -1.0, None, ALU.add), pp_, pp_)
                    V(lambda e: e.tensor_tensor(P_(T1), P_(LRE), P_(LRE), ALU.mult), pp_, pp_)
                    V(lambda e: e.tensor_tensor(P_(T2), P_(LIM), P_(LIM), ALU.mult), pp_, pp_)
                    V(lambda e: e.tensor_tensor(P_(DEN), P_(T1), P_(T2), ALU.add), pp_, pp_)
                    V(lambda e: e.reciprocal(P_(DEN), P_(DEN)), pp_, pp_)
                    V(lambda e: e.tensor_tensor(P_(T1), P_(AA), P_(LRE), ALU.mult), pp_, pp_)
                    V(lambda e: e.tensor_tensor(P_(T2), P_(LBI), P_(LIM), ALU.mult), pp_, pp_)
                    V(lambda e: e.tensor_tensor(P_(CR), P_(T1), P_(T2), ALU.add), pp_, pp_)
                    V(lambda e: e.tensor_tensor(P_(CR), P_(CR), P_(DEN), ALU.mult), pp_, pp_)
                    V(lambda e: e.tensor_tensor(P_(T1), P_(LBI), P_(LRE), ALU.mult), pp_, pp_)
                    V(lambda e: e.tensor_tensor(P_(T2), P_(AA), P_(LIM), ALU.mult), pp_, pp_)
                    V(lambda e: e.tensor_tensor(P_(CI), P_(T1), P_(T2), ALU.subtract), pp_, pp_)
                    V(lambda e: e.tensor_tensor(P_(CI), P_(CI), P_(DEN), ALU.mult), pp_, pp_)
                    V(lambda e: e.tensor_copy(Er[:, :, 0], P_(CS)), pp_, [Er])
                    V(lambda e: e.tensor_copy(Ei[:, :, 0], P_(SN)), pp_, [Ei])
                    ta = Buf(flat[:, 0:1024].rearrange("p (m f) -> p m f", m=16), "ta")
                    tb = Buf(flat[:, 1024:2048].rearrange("p (m f) -> p m f", m=16), "tb")
                    L = 1
                    while L < 128:
                        bc = lambda T_: T_[:, :, L - 1:L].to_broadcast([128, 16, L])
                        V(lambda e, L=L: e.tensor_tensor(ta[:, :, 0:L], Er[:, :, 0:L], Er[:, :, L - 1:L].to_broadcast([128, 16, L]), ALU.mult), [Er], [ta])
                        V(lambda e, L=L: e.tensor_tensor(tb[:, :, 0:L], Ei[:, :, 0:L], Ei[:, :, L - 1:L].to_broadcast([128, 16, L]), ALU.mult), [Ei], [tb])
                        V(lambda e, L=L: e.tensor_tensor(Er[:, :, L:2 * L], ta[:, :, 0:L], tb[:, :, 0:L], ALU.subtract), [ta, tb], [Er])
                        V(lambda e, L=L: e.tensor_tensor(ta[:, :, 0:L], Er[:, :, 0:L], Ei[:, :, L - 1:L].to_broadcast([128, 16, L]), ALU.mult), [Er, Ei], [ta])
                        V(lambda e, L=L: e.tensor_tensor(tb[:, :, 0:L], Ei[:, :, 0:L], Er[:, :, L - 1:L].to_broadcast([128, 16, L]), ALU.mult), [Er, Ei], [tb])
                        V(lambda e, L=L: e.tensor_tensor(Ei[:, :, L:2 * L], ta[:, :, 0:L], tb[:, :, 0:L], ALU.add), [ta, tb], [Ei])
                        L *= 2
                    c.barrier()
                    crb = prm[:, CR, :].unsqueeze(2).to_broadcast([128, 16, 16]); cib = prm[:, CI, :].unsqueeze(2).to_broadcast([128, 16, 16])
                    V(lambda e: e.tensor_tensor(Bs[:, 0], Ball[:, 0], crb, ALU.mult), [Ball, prm], [Bs])
                    V(lambda e: e.tensor_tensor(Btmp[:, 0], Ball[:, 1], cib, ALU.mult), [Ball, prm], [Btmp])
                    V(lambda e: e.tensor_tensor(Bs[:, 0], Bs[:, 0], Btmp[:, 0], ALU.subtract), [Bs, Btmp], [Bs])
                    V(lambda e: e.tensor_tensor(Bs[:, 1], Ball[:, 1], crb, ALU.mult), [Ball, prm], [Bs])
                    V(lambda e: e.tensor_tensor(Btmp[:, 1], Ball[:, 0], cib, ALU.mult), [Ball, prm], [Btmp])
                    V(lambda e: e.tensor_tensor(Bs[:, 1], Bs[:, 1], Btmp[:, 1], ALU.add), [Bs, Btmp], [Bs])
                    V(lambda e: e.memset(EXr[:], 0.0), [], [EXr])
                    V(lambda e: e.memset(EXi[:], 0.0), [], [EXi])
                    for a_, EX in ((0, EXr), (1, EXi)):
                        for j in range(4):
                            V(lambda e, a_=a_, EX=EX, j=j: e.tensor_copy(EX[0:64, j:16:4, 32 * j:32 * j + 16], Bs[0:64, a_, j:16:4, :]), [Bs], [EX])
                            V(lambda e, a_=a_, EX=EX, j=j: e.tensor_copy(EX[64:128, j:16:4, 32 * j + 16:32 * j + 32], Bs[64:128, a_, j:16:4, :]), [Bs], [EX])
                    for a_, EX, BX in ((0, EXr, BexpR), (1, EXi, BexpI)):
                        for m4 in range(4):
                            pi = nextpb()
                            for j in range(4):
                                c.mm(lambda e, j=j, m4=m4, EX=EX: e.transpose(PB[pi][:, j * 128:(j + 1) * 128], EX[:, m4 * 4 + j, :], ident[:]), [EX, ident], [PB[pi]], inc=(j == 3))
                            V(lambda e, m4=m4, BX=BX, pi=pi: e.tensor_copy(BX[:, m4 * 4:m4 * 4 + 4, :], PB[pi][:, 0:512].rearrange("p (j f) -> p j f", j=4)), [PB[pi]], [BX])
                    c.barrier()
                    V(lambda e: e.memset(CexpR[:], 0.0), [], [CexpR])
                    V(lambda e: e.memset(CexpIn[:], 0.0), [], [CexpIn])
                    for cc in range(4):
                        c.dma("sp", Cn[:, 0, :], c_re.rearrange("g c n -> (g c) n")[cc * 128:(cc + 1) * 128, :], writes=[Cn])
                        c.dma("sp", Cn[:, 1, :], c_im.rearrange("g c n -> (g c) n")[cc * 128:(cc + 1) * 128, :], writes=[Cn])
                        for a_ in range(2):
                            V(lambda e, a_=a_: e.tensor_scalar(CX[:, a_, 0:64], Cn[:, a_, :], misc[:, 16:17], None, ALU.mult), [Cn, misc], [CX])
                            V(lambda e, a_=a_: e.tensor_scalar(CX[:, a_, 64:128], Cn[:, a_, :], misc[:, 17:18], None, ALU.mult), [Cn, misc], [CX])
                        pi = nextpb()
                        c.mm(lambda e: e.transpose(PB[pi][:, 0:128], CX[:, 0, :], ident[:]), [CX, ident], [PB[pi]], inc=False)
                        c.mm(lambda e: e.transpose(PB[pi][:, 128:256], CX[:, 1, :], ident[:]), [CX, ident], [PB[pi]])
                        for j in range(4):
                            V(lambda e, cc=cc, j=j, pi=pi: e.tensor_copy(CexpR[:, cc * 4 + j, 32 * j:32 * j + 32], PB[pi][:, 32 * j:32 * j + 32]), [PB[pi]], [CexpR])
                            V(lambda e, cc=cc, j=j, pi=pi: e.tensor_scalar(CexpIn[:, cc * 4 + j, 32 * j:32 * j + 32], PB[pi][:, 128 + 32 * j:128 + 32 * j + 32], -1.0, None, ALU.mult), [PB[pi]], [CexpIn])
                        V(lambda e, cc=cc: e.tensor_scalar(diagD[:, cc, :], ident[:], dsk[:, cc:cc + 1], None, ALU.mult), [ident, dsk], [diagD])
                    for a_ in range(2):
                        c.dma("sp", s0n[:, :], st_s5re if a_ == 0 else st_s5im, writes=[s0n])
                        for m4 in range(4):
                            pi = nextpb()
                            for j in range(4):
                                m = m4 * 4 + j
                                c.mm(lambda e, a_=a_, m=m, j=j: e.transpose(PB[pi][:, j * 16:(j + 1) * 16], s0n[0:16, m * 128:(m + 1) * 128], ident[0:16, 0:16]), [s0n, ident], [PB[pi]], inc=(j == 3))
                            V(lambda e, a_=a_, m4=m4, pi=pi: e.tensor_copy(s0[:, a_, m4 * 4:m4 * 4 + 4, :], PB[pi][:, 0:64].rearrange("p (j s) -> p j s", j=4)), [PB[pi]], [s0])
                    V(lambda e: e.memset(sst[:], 0.0), [], [sst])
                    V(lambda e: e.tensor_scalar(ee2[:, :, 0], Ei[:, :, 127], -1.0, None, ALU.mult), [Ei], [ee2])
                    V(lambda e: e.tensor_copy(ee2[:, :, 1], Ei[:, :, 127]), [Ei], [ee2])
                    c.barrier()
                    SET.close()
                    sfin = sb("sfin", [128, 2, 16, 16])
                    sbr = [sb("sbr%d" % i, [128, 512], BF16) for i in range(2)]; sbi = [sb("sbi%d" % i, [128, 512], BF16) for i in range(2)]
                    ygT = sb("ygT", [128, 4, 512], BF16); gate = sb("gate", [128, 512], BF16)
                    sop = sb("sop", [16, 2, 128])
                    WT = [Buf(flat[:, i * 512:(i + 1) * 512], "wt%d" % i) for i in range(8)]
                    WX = [Buf(xsb[0].t.bitcast(F32)[:, 0:512], "wx0"), Buf(xsb[1].t.bitcast(F32)[:, 0:512], "wx1")]
                    bank = [0]

                    def nb():
                        bank[0] = (bank[0] + 1) % 6
                        return bank[0]
                    it = 0
                    for bi, (c0, n) in enumerate(TB):
                        is_s = (c0 == 2048)
                        nt_ = n // 128
                        if is_s:
                            v3 = lambda ap: ap[:, 0:128].rearrange("p (s j) -> p s j", j=8)
                            eb = lambda T_, m: T_[:, m, 0:8].unsqueeze(1).to_broadcast([128, 16, 8])
                        else:
                            v3 = lambda ap: ap[:, 0:512].rearrange("p (t f) -> p t f", f=128)
                            eb = lambda T_, m: T_[:, m, :].unsqueeze(1).to_broadcast([128, 4, 128])
                        def emit_b(m_):
                            cc_ = m_ // 4
                            pr_ = nb()
                            c.mm(lambda e: e.matmul(PB[pr_][:, 0:n], lhsT=BexpR[:, m_, :], rhs=xdT[:, cc_, c0:c0 + n], start=True, stop=True), [BexpR, xdT], [PB[pr_]])
                            pq_ = nb()
                            c.mm(lambda e: e.matmul(PB[pq_][:, 0:n], lhsT=BexpI[:, m_, :], rhs=xdT[:, cc_, c0:c0 + n], start=True, stop=True), [BexpI, xdT], [PB[pq_]])
                            return pr_, pq_
                        bq = {0: emit_b(0)}
                        for cc in range(4):
                            pY = 6 + (cc % 2)
                            for j in range(4):
                                m = cc * 4 + j
                                it += 1
                                if m + 1 < 16:
                                    bq[m + 1] = emit_b(m + 1)
                                t1, t2, t3, q1, q2, q3 = WT[0], WT[1], WT[2], WX[0], WX[1], WT[3]
                                rbase = 3072 if it % 2 == 0 else 2048
                                rre, rim = (WT[6], WT[7]) if it % 2 == 0 else (WT[4], WT[5])
                                pr, pq = bq.pop(m)
                                V(lambda e, m=m: e.tensor_tensor(v3(t1), v3(PB[pr]), eb(Er, m), ALU.mult), [PB[pr], Er], [t1])
                                V(lambda e, m=m: e.tensor_tensor(v3(t2), v3(PB[pq]), eb(Ei, m), ALU.mult), [PB[pq], Ei], [t2])
                                V(lambda e: e.tensor_tensor(t1[:, 0:n], t1[:, 0:n], t2[:, 0:n], ALU.add), [t1, t2], [t1])
                                V(lambda e, m=m: e.tensor_tensor(v3(t2), v3(PB[pq]), eb(Er, m), ALU.mult), [PB[pq], Er], [t2])
                                V(lambda e, m=m: e.tensor_tensor(v3(t3), v3(PB[pr]), eb(Ei, m), ALU.mult), [PB[pr], Ei], [t3])
                                V(lambda e: e.tensor_tensor(t2[:, 0:n], t2[:, 0:n], t3[:, 0:n], ALU.subtract), [t2, t3], [t2])
                                rho_b = prm[:, RHO, m:m + 1]
                                if not is_s:
                                    for tt in range(nt_):
                                        sl = slice(tt * 128, (tt + 1) * 128)
                                        V(lambda e, sl=sl, m=m: e.tensor_tensor_scan(rre[:, sl], rho_b.to_broadcast([128, 128]), t1[:, sl], sst[:, 0, m:m + 1], ALU.mult, ALU.add), [prm, t1, sst], [rre])
                                        V(lambda e, sl=sl, m=m: e.tensor_tensor_scan(rim[:, sl], rho_b.to_broadcast([128, 128]), t2[:, sl], sst[:, 1, m:m + 1], ALU.mult, ALU.add), [prm, t2, sst], [rim])
                                        last = rbase + tt * 128 + 127
                                        a_fwd = flat[:, last:last + 513:512]
                                        a_rev = flat[:, last + 512:last - 1:-512]
                                        V(lambda e, a_rev=a_rev, m=m: e.tensor_tensor(tiny[:, 0:2], a_rev, ee2[:, m, :], ALU.mult), [rre, rim, ee2], [tiny])
                                        V(lambda e, a_fwd=a_fwd, m=m: e.scalar_tensor_tensor(sst[:, :, m], a_fwd, Er[:, m, 127:128], tiny[:, 0:2], ALU.mult, ALU.add), [rre, rim, Er, tiny], [sst])
                                else:
                                    for s_ in range(16):
                                        sl = slice(s_ * 8, s_ * 8 + 8)
                                        V(lambda e, sl=sl, m=m, s_=s_: e.tensor_tensor_scan(rre[:, sl], rho_b.to_broadcast([128, 8]), t1[:, sl], s0[:, 0, m, s_:s_ + 1], ALU.mult, ALU.add), [prm, t1, s0], [rre])
                                        V(lambda e, sl=sl, m=m, s_=s_: e.tensor_tensor_scan(rim[:, sl], rho_b.to_broadcast([128, 8]), t2[:, sl], s0[:, 1, m, s_:s_ + 1], ALU.mult, ALU.add), [prm, t2, s0], [rim])
                                G(lambda e, m=m: e.tensor_tensor(v3(q1), v3(rre), eb(Er, m), ALU.mult), [rre, Er], [q1])
                                G(lambda e, m=m: e.tensor_tensor(v3(q2), v3(rim), eb(Ei, m), ALU.mult), [rim, Ei], [q2])
                                br_, bi_ = sbr[it % 2], sbi[it % 2]
                                if is_s:
                                    G(lambda e: e.tensor_tensor(q3[:, 0:n], q1[:, 0:n], q2[:, 0:n], ALU.subtract), [q1, q2], [q3])
                                    G(lambda e, m=m: e.tensor_copy(sfin[:, 0, m, :], q3[:, 7:128:8]), [q3], [sfin])
                                G(lambda e, br_=br_: e.tensor_tensor(br_[:, 0:n], q1[:, 0:n], q2[:, 0:n], ALU.subtract), [q1, q2], [br_])
                                G(lambda e, m=m: e.tensor_tensor(v3(q2), v3(rim), eb(Er, m), ALU.mult), [rim, Er], [q2])
                                G(lambda e, m=m: e.tensor_tensor(v3(q3), v3(rre), eb(Ei, m), ALU.mult), [rre, Ei], [q3])
                                if is_s:
                                    G(lambda e: e.tensor_tensor(q1[:, 0:n], q2[:, 0:n], q3[:, 0:n], ALU.add), [q2, q3], [q1])
                                    G(lambda e, m=m: e.tensor_copy(sfin[:, 1, m, :], q1[:, 7:128:8]), [q1], [sfin])
                                G(lambda e, bi_=bi_: e.tensor_tensor(bi_[:, 0:n], q2[:, 0:n], q3[:, 0:n], ALU.add), [q2, q3], [bi_])
                                c.mm(lambda e, m=m, j=j, br_=br_: e.matmul(PB[pY][:, 0:n], lhsT=CexpR[:, m, :], rhs=br_[:, 0:n], start=(j == 0), stop=False), [CexpR, br_], [PB[pY]], inc=False)
                                c.mm(lambda e, m=m, bi_=bi_: e.matmul(PB[pY][:, 0:n], lhsT=CexpIn[:, m, :], rhs=bi_[:, 0:n], start=False, stop=False), [CexpIn, bi_], [PB[pY]], inc=True)
                            c.mm(lambda e, cc=cc: e.matmul(PB[pY][:, 0:n], lhsT=diagD[:, cc, :], rhs=xdT[:, cc, c0:c0 + n], start=False, stop=True), [diagD, xdT], [PB[pY]])
                            A(lambda e, cc=cc, pY=pY: e.activation(out=ygT[:, cc, 0:n], in_=PB[pY][:, 0:n], func=AF.Gelu_apprx_tanh), [PB[pY]], [ygT])
                        for oc in range(4):
                            pg = nb()
                            for cc in range(4):
                                c.mm(lambda e, cc=cc, oc=oc: e.matmul(PB[pg][:, 0:n], lhsT=Wgl[:, cc, oc * 128:(oc + 1) * 128], rhs=ygT[:, cc, 0:n], start=(cc == 0), stop=(cc == 3)),
                                     [Wgl, ygT], [PB[pg]], inc=(cc == 3))
                            A(lambda e, oc=oc, pg=pg: e.activation(out=gate[:, 0:n], in_=PB[pg][:, 0:n], func=AF.Sigmoid, bias=bgl[:, oc:oc + 1]), [PB[pg], bgl], [gate])
                            V(lambda e, oc=oc: e.tensor_tensor(omC[:, 4 + oc, c0:c0 + n], ygT[:, oc, 0:n], gate[:, 0:n], ALU.mult), [ygT, gate], [omC])
                    c.barrier()
                    so = Buf(flat[0:16, 0:4096].rearrange("p (a f) -> p a f", a=2), "so")
                    pi = nextpb()
                    c.mm(lambda e: e.transpose(PB[pi][0:16, 0:128], sst[:, 0, :], ident[:]), [sst, ident], [PB[pi]], inc=False)
                    c.mm(lambda e: e.transpose(PB[pi][0:16, 128:256], sst[:, 1, :], ident[:]), [sst, ident], [PB[pi]])
                    V(lambda e: e.tensor_copy(sop[:], PB[pi][0:16, 0:256].rearrange("p (a f) -> p a f", a=2)), [PB[pi]], [sop])
                    c.dma("sp", o_s5re_p.rearrange("(m p) -> m p", p=128), sop[:, 0, :], reads=[sop], is_output=True)
                    c.dma("sp", o_s5im_p.rearrange("(m p) -> m p", p=128), sop[:, 1, :], reads=[sop], is_output=True)
                    for a_ in range(2):
                        for m4 in range(4):
                            pi = nextpb()
                            for j in range(4):
                                c.mm(lambda e, a_=a_, m4=m4, j=j: e.transpose(PB[pi][0:16, j * 128:(j + 1) * 128], sfin[:, a_, m4 * 4 + j, :], ident[:]), [sfin, ident], [PB[pi]], inc=(j == 3))
                            V(lambda e, a_=a_, m4=m4, pi=pi: e.tensor_copy(so[:, a_, m4 * 512:(m4 + 1) * 512], PB[pi][0:16, 0:512]), [PB[pi]], [so])
                    c.dma("sp", o_s5re_s, so[:, 0, :], reads=[so], is_output=True)
                    c.dma("sp", o_s5im_s, so[:, 1, :], reads=[so], is_output=True)
                    c.barrier()
                stage(33)
                tap("omC", omC, omC[:, :, :].rearrange("p g t -> p (g t)"))
                with contextlib.ExitStack() as WO:
                    Wo = c.sb("Wo1", [128, 8, 1024], BF16, WO)
                    load_w(Wo, Wo[:], wview(w_out_cd, 0, 1024))
                    for t in range(NT):
                        for cb in range(2):
                            pi = nextpb()
                            for k in range(8):
                                c.mm(lambda e, k=k: e.matmul(PB[pi][:, 0:512], lhsT=omC[:, k, t * 128:(t + 1) * 128], rhs=Wo[:, k, cb * 512:(cb + 1) * 512],
                                                             start=(k == 0), stop=(k == 7)), [omC, Wo], [PB[pi]], inc=(k == 7))
                            V(lambda e, t=t, cb=cb, pi=pi: e.tensor_tensor(h[:, t, cb * 512:(cb + 1) * 512], h[:, t, cb * 512:(cb + 1) * 512], PB[pi][:, 0:512], ALU.add), [h, PB[pi]], [h])
                    c.barrier()
            tap("hC", h, h[:, :, :])
            stage(34)
            ffn(1, 4)
            stage(35)
            ple(1, 5)
            stage(36)
            with contextlib.ExitStack() as FN:
                gfb = c.sb("gfb", [128, D], F32, FN)
                yo = [c.sb("yo%d" % i, [128, D], F32, FN) for i in range(2)]
                c.dma("sp", gfb[:], norm_final.partition_broadcast(128), writes=[gfb])
                for t in range(NT):
                    A(lambda e, t=t: e.activation(out=junk[:], in_=h[:, t, :], func=AF.Square, accum_out=ss[:, t:t + 1]), [h], [junk, ss])
                V(lambda e: e.tensor_scalar(rstd[:], ss[:], 1.0 / D, EPS, ALU.mult, ALU.add), [ss], [rstd])
                A(lambda e: e.activation(out=rstd[:], in_=rstd[:], func=AF.Sqrt), [rstd], [rstd])
                V(lambda e: e.reciprocal(rstd[:], rstd[:]), [rstd], [rstd])
                for t in range(NT):
                    y_ = yo[t % 2]
                    V(lambda e, t=t, y_=y_: e.scalar_tensor_tensor(y_[:], h[:, t, :], rstd[:, t:t + 1], gfb[:], ALU.mult, ALU.mult), [h, rstd, gfb], [y_])
                    dst = y_p[t * 128:(t + 1) * 128, :] if t < 16 else y_s
                    c.dma("sp", dst, y_[:], reads=[y_], is_output=True)

      except _Stop:
        pass
      c.dead = False
      c.finish()
    return nc


def _consts():
    i = np.arange(128)
    blk = (i[:, None] // 8) == (i[None, :] // 8)
    m = np.zeros((10, 128, 128), np.float32)
    le = i[:, None] <= i[None, :]
    gt = i[:, None] > i[None, :]
    m[0] = le; m[1] = le & blk
    m[2] = gt; m[3] = gt & blk
    m[4] = le; m[5] = le & blk
    m[6] = gt; m[7] = gt & blk
    m[8, :8, :] = (i[None, :] % 8 == np.arange(8)[:, None])
    sel = (i[:, None] // 8 == np.arange(16)[None, :]).astype(np.float32)
    misc = np.zeros((128, 32), np.float32)
    misc[:, 0:15] = 1.0 / (np.arange(15)[None, :] + 1.0)
    misc[:, 16] = ((i // 16) % 2 == 0)
    misc[:, 17] = ((i // 16) % 2 == 1)
    return np.eye(128, dtype=np.float32), m, sel, misc


def make_in_maps(inp):
    f = lambda a: np.ascontiguousarray(np.asarray(a, dtype=np.float32))
    ident, masks, sel, misc = _consts()
    shared = {
        "norm_mix": f(inp["norm_mix"]), "norm_ffn": f(inp["norm_ffn"]), "norm_pe": f(inp["norm_pe"]), "norm_final": f(inp["norm_final"]),
        "w_in_ab": f(inp["w_in_ab"][0]), "conv_qkv": f(inp["conv_qkv"][0]), "a_log": f(inp["a_log"][0]), "dt_bias": f(inp["dt_bias"][0]),
        "norm_o": f(inp["norm_o"][0]), "ln_g": f(inp["ln_v_gain"][0]), "ln_b": f(inp["ln_v_bias"][0]), "w_sp": f(inp["w_spatial"][0]),
        "b_sp": f(inp["b_spatial"][0]), "w_out_ab": f(inp["w_out_ab"][0]), "w_in_cd": f(inp["w_in_cd"][0]), "w_pool": f(inp["w_pool"][0]),
        "pool_scale": f(inp["pool_scale"][0]), "lam_re": f(inp["lam_re"][0]).reshape(2048), "lam_im": f(inp["lam_im"][0]).reshape(2048),
        "log_dt": f(inp["log_dt"][0]), "b_re": f(inp["b_re"][0]), "b_im": f(inp["b_im"][0]), "c_re": f(inp["c_re"][0]), "c_im": f(inp["c_im"][0]),
        "d_skip": f(inp["d_skip"][0]), "w_glu": f(inp["w_glu"][0]), "b_glu": f(inp["b_glu"][0]), "w_out_cd": f(inp["w_out_cd"][0]),
        "w_up": f(inp["w_ffn_up"]), "w_down": f(inp["w_ffn_down"]), "w_pe_proj": f(inp["w_pe_proj"]), "w_pe_gate": f(inp["w_pe_gate"]),
        "c_ident": ident, "c_masks": masks, "c_selcol": sel, "c_misc": misc,
    }
    maps = []
    for ci in range(NCORES):
        sl = slice(16 * ci, 16 * ci + 16)
        m = dict(shared)
        m["xp"] = f(inp["x_prompt"][ci]); m["xs"] = f(inp["x_sample"][sl]).reshape(128, D)
        m["pp"] = f(inp["p_prompt"][:, ci]); m["psm"] = f(inp["p_sample"][:, sl]).reshape(2, 128, 256)
        m["st_conv"] = f(inp["state_conv"][0, sl]).reshape(48, 1536); m["st_delta"] = f(inp["state_delta"][0, sl])
        m["st_pool"] = f(inp["state_pool"][0, sl]); m["st_s5re"] = f(inp["state_s5_re"][0, sl]).reshape(16, 2048)
        m["st_s5im"] = f(inp["state_s5_im"][0, sl]).reshape(16, 2048)
        maps.append(m)
    return maps


_NC_CACHE = {}


def kernel(**inputs):
    if "nc" not in _NC_CACHE:
        _NC_CACHE["nc"] = build_nc()
    nc = _NC_CACHE["nc"]
    maps = make_in_maps(inputs)
    res = run_bass_kernel_spmd(nc, maps, core_ids=list(range(NCORES)))
    R = res.results
    cat = lambda k, shp: np.concatenate([np.asarray(r[k], np.float32).reshape(shp) for r in R], axis=0)
    y_prompt = cat("y_p", (1, 2048, D)); y_sample = cat("y_s", (16, 8, D))
    conv_p = cat("o_conv_p", (1, 3, 1536))[None]; delta_p = cat("o_delta_p", (1, 4, 128, 128))[None]
    sguv_p = cat("o_sguv_p", (1, 128, 512))[None]; pool_p = cat("o_pool_p", (1, 15, 512))[None]
    s5re_p = cat("o_s5re_p", (1, 32, 64))[None]; s5im_p = cat("o_s5im_p", (1, 32, 64))[None]
    conv_s = cat("o_conv_s", (16, 3, 1536))[None]; delta_s = cat("o_delta_s", (16, 4, 128, 128))[None]
    sguv_s = cat("o_sguv_s", (16, 8, 512))[None]; pool_s = cat("o_pool_s", (16, 15, 512))[None]
    s5re_s = cat("o_s5re_s", (16, 32, 64))[None]; s5im_s = cat("o_s5im_s", (16, 32, 64))[None]
    return (y_prompt, y_sample, conv_p, delta_p, sguv_p, pool_p, s5re_p, s5im_p,
            conv_s, delta_s, sguv_s, pool_s, s5re_s, s5im_s)
```
